# Optimizing a Trainium2 kernel written in Bass

```python
import math
import jax, jax.numpy as jnp
from jax import lax
import numpy as np

D_MODEL = 1024
BATCH = 8
SEQ = 2048
DEPTH = 1

GRID_W = 64
CTX_LEN = 256
MIX_W = D_MODEL
F_GROUPS = 4
F_DIM = D_MODEL // 8
FOURIER_W = F_GROUPS * F_DIM
GDN_HEADS = 4
GDN_DK = D_MODEL // 8
GDN_DV = D_MODEL // 8
QK_W = GDN_HEADS * GDN_DK
V_W = GDN_HEADS * GDN_DV
IN_W = FOURIER_W + 2 * QK_W + 2 * V_W + 4 * GDN_HEADS
SHORT_CONV = 3
CHUNK = 64
D_FF = ((8 * D_MODEL // 3 + 127) // 128) * 128
EPS = 1e-6

kernel_name = "hybrid_fourier_gdn_convglu_dit"


def rmsnorm(t, g):
    tf = t.astype(jnp.float32)
    y = tf * lax.rsqrt(jnp.mean(tf * tf, axis=-1, keepdims=True) + EPS)
    return y.astype(t.dtype) * g


def l2norm(t):
    tf = t.astype(jnp.float32)
    return (tf * lax.rsqrt(jnp.sum(tf * tf, axis=-1, keepdims=True) + EPS)).astype(t.dtype)


def modulate(h, shift, scale):
    return h * (1 + scale) + shift


def dwconv1d(t, w):
    k_w = w.shape[0]
    pad = k_w // 2
    length = t.shape[1]
    tp = jnp.pad(t, ((0, 0), (pad, pad), (0, 0)))
    out = tp[:, 0:length] * w[0]
    for j in range(1, k_w):
        out = out + tp[:, j:j + length] * w[j]
    return out


def gated_delta_chunked(q, k, v, beta, g, s0):
    f32 = jnp.float32
    bsz, length, heads, dk = q.shape
    dv = v.shape[-1]
    n_chunks = length // CHUNK

    def chunks(t):
        t = t.astype(f32).reshape(bsz, n_chunks, CHUNK, heads, t.shape[-1])
        return jnp.transpose(t, (1, 0, 3, 2, 4))

    def chunks_s(t):
        t = t.astype(f32).reshape(bsz, n_chunks, CHUNK, heads)
        return jnp.transpose(t, (1, 0, 3, 2))

    qc = chunks(q) * (dk ** -0.5)
    kc = chunks(k)
    vc = chunks(v)
    bc = chunks_s(beta)
    gc = jnp.cumsum(chunks_s(g), axis=-1)
    lower = jnp.tril(jnp.ones((CHUNK, CHUNK), dtype=bool))
    strict = jnp.tril(jnp.ones((CHUNK, CHUNK), dtype=bool), -1)
    diff = gc[..., :, None] - gc[..., None, :]
    decay = jnp.where(lower, jnp.exp(jnp.where(lower, diff, 0.0)), 0.0)

    kb = kc * bc[..., None]
    vb = vc * bc[..., None]
    a_mat = jnp.where(strict, jnp.einsum('nbhid,nbhjd->nbhij', kb, kc) * decay, 0.0)
    eye = jnp.eye(CHUNK, dtype=f32)
    t_inv = lax.linalg.triangular_solve(eye + a_mat, jnp.broadcast_to(eye, a_mat.shape),
                                        left_side=True, lower=True, unit_diagonal=True)
    u_val = jnp.einsum('nbhij,nbhjd->nbhid', t_inv, vb)
    w_key = jnp.einsum('nbhij,nbhjd->nbhid', t_inv, kb * jnp.exp(gc)[..., None])
    attn_intra = jnp.where(lower, jnp.einsum('nbhid,nbhjd->nbhij', qc, kc) * decay, 0.0)
    g_last = gc[..., -1]
    k_dec = kc * jnp.exp(g_last[..., None] - gc)[..., None]
    q_dec = qc * jnp.exp(gc)[..., None]

    def step(s, xs):
        u_i, w_i, a_i, qd_i, kd_i, gl_i = xs
        v_new = u_i - jnp.einsum('bhcd,bhde->bhce', w_i, s)
        o_i = jnp.einsum('bhcd,bhde->bhce', qd_i, s) + jnp.einsum('bhij,bhje->bhie', a_i, v_new)
        s = s * jnp.exp(gl_i)[..., None, None] + jnp.einsum('bhcd,bhce->bhde', kd_i, v_new)
        return s, o_i

    s_final, o = lax.scan(step, s0.astype(f32), (u_val, w_key, attn_intra, q_dec, k_dec, g_last))
    o = jnp.transpose(o, (1, 0, 3, 2, 4)).reshape(bsz, length, heads, dv)
    return o.astype(v.dtype), s_final


def bidir_gdn(q, k, v, beta, g, s0_f, s0_b):
    flip = lambda t: jnp.flip(t, axis=1)
    o_f, s_f = gated_delta_chunked(q, k, v, beta[:, :, 0], g[:, :, 0], s0_f)
    o_b, s_b = gated_delta_chunked(flip(q), flip(k), flip(v), flip(beta[:, :, 1]), flip(g[:, :, 1]), s0_b)
    return o_f + flip(o_b), s_f, s_b


def mixer_inputs(h, w_in_l, conv_l, a_log_l, dt_bias_l):
    bsz, length, _ = h.shape
    p = h @ w_in_l
    o1 = FOURIER_W
    o2 = o1 + 2 * QK_W + V_W
    o3 = o2 + V_W
    u = p[..., :o1]
    qkv = jax.nn.silu(dwconv1d(p[..., o1:o2], conv_l))
    z = p[..., o2:o3]
    ab = p[..., o3:].astype(jnp.float32).reshape(bsz, length, 4, GDN_HEADS)
    q = l2norm(qkv[..., :QK_W].reshape(bsz, length, GDN_HEADS, GDN_DK))
    k = l2norm(qkv[..., QK_W:2 * QK_W].reshape(bsz, length, GDN_HEADS, GDN_DK))
    v = qkv[..., 2 * QK_W:].reshape(bsz, length, GDN_HEADS, GDN_DV)
    beta = jax.nn.sigmoid(ab[:, :, 0:2])
    g = -jnp.exp(a_log_l.astype(jnp.float32)) * jax.nn.softplus(ab[:, :, 2:4] + dt_bias_l.astype(jnp.float32))
    return u, z, q, k, v, beta, g


def fourier_mix(u, w_f):
    bsz, length, _ = u.shape
    ug = u.astype(jnp.float32).reshape(bsz, length, F_GROUPS, F_DIM)
    y = jnp.fft.fft2(ug, axes=(1, 3), norm='ortho').real.astype(u.dtype)
    return jnp.einsum('blgc,gcd->blgd', y, w_f).reshape(bsz, length, FOURIER_W)


def mixer_out(u, o, z, w_f, g_gdn_l, w_out_l):
    bsz, length, _ = z.shape
    y_f = fourier_mix(u, w_f)
    y_d = rmsnorm(o, g_gdn_l) * jax.nn.silu(z).reshape(bsz, length, GDN_HEADS, GDN_DV)
    return jnp.concatenate([y_f, y_d.reshape(bsz, length, V_W)], axis=-1) @ w_out_l


def conv_ffn(h, w_up_l, w_dwc_l, w_down_l, on_grid):
    bsz, length, _ = h.shape
    p = h @ w_up_l
    val, gate = p[..., :D_FF], p[..., D_FF:]
    if on_grid:
        rows = length // GRID_W
        gate = lax.conv_general_dilated(
            gate.reshape(bsz, rows, GRID_W, D_FF), w_dwc_l[:, :, None, :],
            window_strides=(1, 1), padding='SAME',
            dimension_numbers=('NHWC', 'HWIO', 'NHWC'),
            feature_group_count=D_FF).reshape(bsz, length, D_FF)
    else:
        gate = dwconv1d(gate, w_dwc_l[1])
    return (jax.nn.silu(gate) * val) @ w_down_l


def setup_inputs(seed: int = 0) -> dict:
    key = jax.random.key(seed)
    ks = jax.random.split(key, 20)
    f32 = jnp.float32

    def nrm(k, shape, s):
        return jax.random.normal(k, shape, f32) * s

    L = DEPTH
    x = nrm(ks[0], (BATCH, SEQ, D_MODEL), 1.0)
    c = nrm(ks[1], (BATCH, D_MODEL), 1.0)
    ctx = nrm(ks[2], (BATCH, CTX_LEN, D_MODEL), 1.0)
    c_ctx = nrm(ks[3], (D_MODEL,), 1.0)
    w_ada = nrm(ks[4], (L, D_MODEL, 6 * D_MODEL), 0.5 * D_MODEL ** -0.5)
    b_ada = nrm(ks[5], (L, 6 * D_MODEL), 0.02)
    g_pre_mix = 1.0 + nrm(ks[6], (L, D_MODEL), 0.1)
    g_post_mix = 1.0 + nrm(ks[7], (L, D_MODEL), 0.1)
    g_pre_ffn = 1.0 + nrm(ks[8], (L, D_MODEL), 0.1)
    g_post_ffn = 1.0 + nrm(ks[9], (L, D_MODEL), 0.1)
    w_in = nrm(ks[10], (L, D_MODEL, IN_W), D_MODEL ** -0.5)
    w_qkv_conv = nrm(ks[11], (L, SHORT_CONV, 2 * QK_W + V_W), SHORT_CONV ** -0.5)
    a_log = jnp.log(jax.random.uniform(ks[12], (L, 2, GDN_HEADS), f32, minval=1.0, maxval=16.0))
    dt = jnp.exp(jax.random.uniform(ks[13], (L, 2, GDN_HEADS), f32,
                                    minval=math.log(1e-3), maxval=math.log(1e-1)))
    dt_bias = dt + jnp.log(-jnp.expm1(-dt))
    g_gdn = 1.0 + nrm(ks[14], (L, GDN_DV), 0.1)
    w_fourier = nrm(ks[15], (L, F_GROUPS, F_DIM, F_DIM), F_DIM ** -0.5)
    w_out = nrm(ks[16], (L, MIX_W, D_MODEL), MIX_W ** -0.5)
    w_up = nrm(ks[17], (L, D_MODEL, 2 * D_FF), D_MODEL ** -0.5)
    w_dwc = nrm(ks[18], (L, 3, 3, D_FF), 1.0 / 3.0)
    w_down = nrm(ks[19], (L, D_FF, D_MODEL), D_FF ** -0.5)
    return {"x": x, "c": c, "ctx": ctx, "c_ctx": c_ctx, "w_ada": w_ada, "b_ada": b_ada,
            "g_pre_mix": g_pre_mix, "g_post_mix": g_post_mix, "g_pre_ffn": g_pre_ffn,
            "g_post_ffn": g_post_ffn, "w_in": w_in, "w_qkv_conv": w_qkv_conv, "a_log": a_log,
            "dt_bias": dt_bias, "g_gdn": g_gdn, "w_fourier": w_fourier, "w_out": w_out,
            "w_up": w_up, "w_dwc": w_dwc, "w_down": w_down}


def reference(x, c, ctx, c_ctx, w_ada, b_ada, g_pre_mix, g_post_mix, g_pre_ffn, g_post_ffn,
              w_in, w_qkv_conv, a_log, dt_bias, g_gdn, w_fourier, w_out, w_up, w_dwc, w_down):
    xl, xc = x, ctx
    bsz = x.shape[0]
    silu_c = jax.nn.silu(c)
    silu_cc = jax.nn.silu(c_ctx)
    for i in range(DEPTH):
        update_ctx = i < DEPTH - 1
        mod_l = (silu_c @ w_ada[i] + b_ada[i])[:, None, :]
        mod_c = silu_cc @ w_ada[i] + b_ada[i]
        sh1l, sc1l, gt1l, sh2l, sc2l, gt2l = jnp.split(mod_l, 6, axis=-1)
        sh1c, sc1c, gt1c, sh2c, sc2c, gt2c = jnp.split(mod_c, 6, axis=-1)

        hl = modulate(rmsnorm(xl, g_pre_mix[i]), sh1l, sc1l)
        hc = modulate(rmsnorm(xc, g_pre_mix[i]), sh1c, sc1c)
        ul, zl, ql, kl, vl, bl, gl = mixer_inputs(hl, w_in[i], w_qkv_conv[i], a_log[i], dt_bias[i])
        uc, zc, qc, kc, vc, bc, gc = mixer_inputs(hc, w_in[i], w_qkv_conv[i], a_log[i], dt_bias[i])
        s_zero = jnp.zeros((bsz, GDN_HEADS, GDN_DK, GDN_DV), jnp.float32)
        oc, s_f, s_b = bidir_gdn(qc, kc, vc, bc, gc, s_zero, s_zero)
        ol, _, _ = bidir_gdn(ql, kl, vl, bl, gl, s_f, s_b)
        yl = mixer_out(ul, ol, zl, w_fourier[i], g_gdn[i], w_out[i])
        xl = xl + gt1l * rmsnorm(yl, g_post_mix[i])
        if update_ctx:
            yc = mixer_out(uc, oc, zc, w_fourier[i], g_gdn[i], w_out[i])
            xc = xc + gt1c * rmsnorm(yc, g_post_mix[i])

        hl = modulate(rmsnorm(xl, g_pre_ffn[i]), sh2l, sc2l)
        xl = xl + gt2l * rmsnorm(conv_ffn(hl, w_up[i], w_dwc[i], w_down[i], True), g_post_ffn[i])
        if update_ctx:
            hc = modulate(rmsnorm(xc, g_pre_ffn[i]), sh2c, sc2c)
            xc = xc + gt2c * rmsnorm(conv_ffn(hc, w_up[i], w_dwc[i], w_down[i], False), g_post_ffn[i])
    return xl
```

```python
import os
import numpy as np
import ml_dtypes
import concourse.bass as bass
import concourse.mybir as mybir
from concourse.bass_utils import run_bass_kernel_spmd

F32 = mybir.dt.float32
BF16 = mybir.dt.bfloat16
AF = mybir.ActivationFunctionType
ALU = mybir.AluOpType

P = 128
D_MODEL = 1024
SEQ = 2048
CTX = 256
NTOK = SEQ + CTX
NT = NTOK // P
NLT = SEQ // P
IN_W = 2576
D_FF = 2816
NFC = D_FF // P
EPS = 1e-6
EPOCH = 12000
NDMA = 24


class Tile:
    def __init__(self, ap, name="", psum=False):
        self.ap = ap
        self.name = name
        self.w = None
        self.r = {}
        self.psum = psum

    def __getitem__(self, idx):
        return V(self, self.ap[idx])

    def v(self, ap):
        return V(self, ap)


class V:
    def __init__(self, tile, a):
        self.tile = tile
        self.a = a

    def __getitem__(self, idx):
        return V(self.tile, self.a[idx])

    def bc(self, shape):
        return V(self.tile, self.a.broadcast_to(list(shape)))

    def bitcast(self, dt):
        return V(self.tile, self.a.bitcast(dt))


class _Eng:
    def __init__(self, K, name, h):
        self.K = K
        self.name = name
        self.h = h
        self.sem = None
        self.cnt = 0
        self.seen = {}
        self.nsem = 0
        self.ninst = 0

    def newsem(self):
        self.sem = self.K.nc.alloc_semaphore(f"s_{self.name}_{self.nsem}")
        self.nsem += 1
        self.cnt = 0


class Kern:
    def __init__(self, nc):
        self.nc = nc
        self.E = {}
        for name, h in (("pe", nc.tensor), ("dve", nc.vector), ("act", nc.scalar),
                        ("pool", nc.gpsimd), ("sp", nc.sync)):
            e = _Eng(self, name, h)
            if name != "sp":
                e.newsem()
            self.E[name] = e
        self.dpool = [[nc.alloc_semaphore(f"s_dma_{i}"), 0] for i in range(NDMA)]
        self.drr = 0
        self.out_tokens = []

    def _deps(self, e, W, R):
        need = []
        for v in R:
            t = v.tile
            if t.w is not None:
                need.append((t.w, "raw"))
            if t.psum:
                for tok in t.r.values():
                    if tok[2] is not e:
                        need.append((tok, "rar"))
        for v in W:
            t = v.tile
            if t.w is not None:
                need.append((t.w, "waw"))
            for tok in t.r.values():
                need.append((tok, "war"))
        for (sem, val, src), kind in need:
            if src is e:
                if e.name == "pe":
                    continue
            if e.seen.get(id(sem), 0) >= val:
                continue
            e.h.wait_ge(sem, val)
            e.seen[id(sem)] = val

    def _post(self, tok, W, R):
        sem = tok[0]
        for v in R:
            v.tile.r[id(sem)] = tok
        for v in W:
            v.tile.w = tok
            v.tile.r = {}

    def emit(self, eng, fn, W=(), R=()):
        e = self.E[eng]
        self._deps(e, W, R)
        if e.cnt >= EPOCH:
            e.newsem()
        inst = fn(e.h)
        e.cnt += 1
        e.ninst += 1
        inst.then_inc(e.sem, 1)
        tok = (e.sem, e.cnt, e)
        self._post(tok, W, R)
        return tok

    def dma(self, out, in_, W=(), R=(), is_output=False, queue="sp"):
        e = self.E[queue]
        self._deps(e, W, R)
        slot = self.dpool[self.drr]
        self.drr = (self.drr + 1) % NDMA
        sem, n = slot
        if n > 0 and e.seen.get(id(sem), 0) < 16 * n:
            e.h.wait_ge(sem, 16 * n)
            e.seen[id(sem)] = 16 * n
        e.h.dma_start(out=out, in_=in_).then_inc(sem, 16)
        slot[1] = n + 1
        tok = (sem, 16 * (n + 1), None)
        self._post(tok, W, R)
        if is_output:
            self.out_tokens.append(tok)
        return tok

    def barrier(self):
        names = ["pe", "dve", "act", "pool"]
        for a in names:
            ea = self.E[a]
            for b in names + ["sp"]:
                if a == b:
                    continue
                eb = self.E[b]
                if b == "sp":
                    continue
                if eb.cnt == 0:
                    continue
                if ea.seen.get(id(eb.sem), 0) >= eb.cnt:
                    continue
                ea.h.wait_ge(eb.sem, eb.cnt)
                ea.seen[id(eb.sem)] = eb.cnt
            for sem, n in self.dpool:
                if n > 0 and ea.seen.get(id(sem), 0) < 16 * n:
                    ea.h.wait_ge(sem, 16 * n)
                    ea.seen[id(sem)] = 16 * n
        sp = self.E["sp"]
        for b in names:
            eb = self.E[b]
            if eb.cnt and sp.seen.get(id(eb.sem), 0) < eb.cnt:
                sp.h.wait_ge(eb.sem, eb.cnt)
                sp.seen[id(eb.sem)] = eb.cnt
        for sem, n in self.dpool:
            if n > 0 and sp.seen.get(id(sem), 0) < 16 * n:
                sp.h.wait_ge(sem, 16 * n)
                sp.seen[id(sem)] = 16 * n

    def finish(self):
        sp = self.E["sp"]
        for sem, val, _ in self.out_tokens:
            if sp.seen.get(id(sem), 0) < val:
                sp.h.wait_ge(sem, val)
                sp.seen[id(sem)] = val


class Arena:
    def __init__(self, nc, K, start, size, name):
        self.nc = nc
        self.K = K
        self.start = start
        self.end = start + size
        self.cur = start
        self.name = name
        self.n = 0

    def alloc(self, shape, dt, name=None):
        esz = 4 if dt == F32 else 2
        nbytes = esz
        for s in shape[1:]:
            nbytes *= s
        off = (self.cur + 63) // 64 * 64
        if off + nbytes > self.end:
            raise RuntimeError(f"arena {self.name} overflow allocating {name} {shape}: "
                               f"{off - self.start}+{nbytes} > {self.end - self.start}")
        self.cur = off + nbytes
        self.n += 1
        return self.nc.alloc_sbuf_tensor_at(f"{self.name}_{name or 't'}_{self.n}", list(shape), dt, offset=off)

    def tile(self, shape, dt, name=None):
        return Tile(self.alloc(shape, dt, name), name or "t")

    def reset(self, barrier=True):
        if barrier:
            self.K.barrier()
        self.cur = self.start


KB = 1024


def build_program(debug=None, stop_after=None):
    debug = debug or []
    nc = bass.Bass("TRN2", target_bir_lowering=False)
    K = Kern(nc)
    base = (nc.sbuf_base + 63) // 64 * 64
    avail = nc.sbuf_top - base
    R0 = Arena(nc, K, base, 16 * KB, "R0")
    R1 = Arena(nc, K, base + 16 * KB, 64 * KB, "R1")
    R2 = Arena(nc, K, base + 80 * KB, 16 * KB, "R2")
    R3 = Arena(nc, K, base + 96 * KB, 16 * KB, "R3")
    R4 = Arena(nc, K, base + 112 * KB, 16 * KB, "R4")
    R5 = Arena(nc, K, base + 128 * KB, 54 * KB, "R5")
    R6 = Arena(nc, K, base + 182 * KB, avail - 182 * KB, "R6")
    dbg_out = {}

    def dram(name, shape, dt=F32, kind="ExternalInput"):
        return nc.dram_tensor(name, list(shape), dt, kind=kind).ap()

    x_d = dram("x", [SEQ, D_MODEL])
    ctx_d = dram("ctx", [CTX, D_MODEL])
    wada_d = dram("w_ada", [D_MODEL, 6 * D_MODEL])
    win_d = dram("w_in", [21, P, 8, P])
    wout_d = dram("w_out", [D_MODEL, D_MODEL])
    wup_d = dram("w_up", [NFC, P, 8, 2, P])
    wdown_d = dram("w_down", [D_FF, D_MODEL])
    vecs_d = dram("vecs", [96, P])
    convw_d = dram("convw", [36, P])
    dwcw_d = dram("dwcw", [198, P])
    ggdn_d = dram("g_gdn", [1, P])
    alog_d = dram("a_log", [8])
    dtb_d = dram("dt_bias", [8])
    wf_d = dram("w_fourier", [4, P, P])
    gconst_d = dram("gconst", [P, NG, P])
    cfsf_d = dram("cfsf", [P, 256])
    tab_d = dram("dft_tab", [4, 16, P, 2, 512], BF16)
    out_d = dram("out", [SEQ, D_MODEL], kind="ExternalOutput")

    PS = [Tile(nc.alloc_psum_tensor(f"ps{i}", [P, 512], F32), f"ps{i}", psum=True) for i in range(8)]

    def dump(name, view, shape, dt=F32):
        if name not in debug:
            return
        d = dram("dbg_" + name, shape, dt, kind="ExternalOutput")
        dbg_out[name] = (shape, dt)
        K.dma(d, view.a, R=[view], is_output=True)

    def finish():
        K.finish()
        return nc, dbg_out

    def mm(out, lhsT, rhs, start=True, stop=True):
        K.emit("pe", lambda e: e.matmul(out.a, lhsT=lhsT.a, rhs=rhs.a, start=start, stop=stop),
               W=[out], R=[lhsT, rhs])

    def tr(out, in_, ident):
        K.emit("pe", lambda e: e.transpose(out.a, in_.a, ident.a), W=[out], R=[in_, ident])

    def act(out, in_, func, scale=1.0, bias=None, accum=None):
        R = [in_]
        kw = {}
        if isinstance(scale, V):
            R.append(scale)
            kw["scale"] = scale.a
        else:
            kw["scale"] = float(scale)
        if bias is not None:
            R.append(bias)
            kw["bias"] = bias.a
        W = [out]
        if accum is not None:
            W.append(accum)
            kw["accum_out"] = accum.a
        K.emit("act", lambda e: e.activation(out.a, in_.a, func, **kw), W=W, R=R)

    def ts(eng, out, in0, s1, op0, s2=None, op1=None):
        R = [in0]
        a1 = s1.a if isinstance(s1, V) else float(s1)
        if isinstance(s1, V):
            R.append(s1)
        a2 = None
        if s2 is not None:
            a2 = s2.a if isinstance(s2, V) else float(s2)
            if isinstance(s2, V):
                R.append(s2)
        if op1 is None:
            K.emit(eng, lambda e: e.tensor_scalar(out.a, in0.a, a1, None, op0), W=[out], R=R)
        else:
            K.emit(eng, lambda e: e.tensor_scalar(out.a, in0.a, a1, a2, op0, op1), W=[out], R=R)

    def tt(eng, out, in0, in1, op):
        K.emit(eng, lambda e: e.tensor_tensor(out.a, in0.a, in1.a, op), W=[out], R=[in0, in1])

    def stt(out, in0, s, in1, op0, op1):
        R = [in0, in1]
        a = s.a if isinstance(s, V) else float(s)
        if isinstance(s, V):
            R.append(s)
        K.emit("dve", lambda e: e.scalar_tensor_tensor(out.a, in0.a, a, in1.a, op0, op1), W=[out], R=R)

    def cp(eng, out, in_):
        if eng == "act":
            act(out, in_, AF.Identity)
        else:
            K.emit(eng, lambda e: e.tensor_copy(out.a, in_.a), W=[out], R=[in_])

    def memset(eng, out, val):
        K.emit(eng, lambda e: e.memset(out.a, val), W=[out])

    def hview(view, pat, **kw):
        return V(view.tile, view.a.rearrange(pat, **kw))

    def bc_mid(view, n):
        sh = list(view.a.shape)
        return V(view.tile, view.a.unsqueeze(1).broadcast_to([sh[0], n, sh[1]]))

    def bc_last(view, n):
        sh = list(view.a.shape)
        return V(view.tile, view.a.unsqueeze(len(sh)).broadcast_to(sh + [n]))

    def ps4(ps):
        return hview(ps[:, :], "p (h d) -> p h d", d=P)

    GC = R0.tile([P, NG, P], F32, "gconst")
    K.dma(GC.ap[:], gconst_d, W=[GC[:]])

    def gcv(i):
        return GC[:, i, :]

    IDN = gcv(G_IDN)
    ONES = gcv(G_ONES)
    NEG1 = gcv(G_NEG1)
    idn_bf = R0.tile([P, P], BF16, "idn_bf")
    ones_bf = R0.tile([P, P], BF16, "ones_bf")
    cp("dve", idn_bf[:], IDN)
    cp("dve", ones_bf[:], ONES)
    cst = R0.tile([P, 8], F32, "cst")
    memset("dve", cst[:, 0:1], EPS)
    memset("dve", cst[:, 1:2], -0.5)
    memset("dve", cst[:, 2:3], 1.0)
    c_eps = cst[:, 0:1]
    c_mhalf = cst[:, 1:2]
    c_one = cst[:, 2:3]
    if stop_after == -3:
        dump("idn", idn_bf[:], [P, P], BF16)
        return finish()

    vec_fm = R0.tile([P, 96], F32, "vec_fm")
    cw_fm = R0.tile([P, 36], F32, "cw_fm")
    dw_fm = R0.tile([P, 198], F32, "dw_fm")
    gg_fm = R0.tile([P, 1], F32, "gg_fm")
    st_v = R1.tile([96, P], F32, "st_v")
    st_c = R1.tile([36, P], F32, "st_c")
    st_d1 = R1.tile([P, P], F32, "st_d1")
    st_d2 = R1.tile([70, P], F32, "st_d2")
    st_g = R1.tile([1, P], F32, "st_g")
    K.dma(st_v.ap[:], vecs_d, W=[st_v[:]])
    K.dma(st_c.ap[:], convw_d, W=[st_c[:]])
    K.dma(st_d1.ap[:], dwcw_d[0:128, :], W=[st_d1[:]])
    K.dma(st_d2.ap[:], dwcw_d[128:198, :], W=[st_d2[:]])
    K.dma(st_g.ap[:], ggdn_d, W=[st_g[:]])
    ps = PS[0]
    tr(ps[:, 0:96], st_v[:], GC[0:96, G_IDN, 0:96])
    cp("dve", vec_fm[:], ps[:, 0:96])
    ps = PS[1]
    tr(ps[:, 0:36], st_c[:], GC[0:36, G_IDN, 0:36])
    tr(ps[:, 36:37], st_g[:], GC[0:1, G_IDN, 0:1])
    cp("dve", cw_fm[:], ps[:, 0:36])
    cp("dve", gg_fm[:], ps[:, 36:37])
    ps = PS[2]
    tr(ps[:, 0:128], st_d1[:], IDN)
    tr(ps[:, 128:198], st_d2[:], GC[0:70, G_IDN, 0:70])
    cp("dve", dw_fm[:], ps[:, 0:198])
    if stop_after == -2:
        dump("dw_fm", dw_fm[:], [P, 198])
        return finish()

    ab_c = R0.tile([P, 16], F32, "ab_c")
    K.dma(ab_c.ap[:, 0:8], dtb_d.partition_broadcast(P), W=[ab_c[:]])
    K.dma(ab_c.ap[:, 8:16], alog_d.partition_broadcast(P), W=[ab_c[:]])
    act(ab_c[:, 8:16], ab_c[:, 8:16], AF.Exp)
    ts("dve", ab_c[:, 8:16], ab_c[:, 8:16], -1.0, ALU.mult)

    silu_c = R0.tile([P, 8, 2], F32, "silu_c")
    act(silu_c[:, :, 0], vec_fm[:, 80:88], AF.Silu)
    act(silu_c[:, :, 1], vec_fm[:, 88:96], AF.Silu)
    if stop_after == -1:
        dump("silu_c", silu_c[:], [P, 8, 2])
        dump("ab_c", ab_c[:], [P, 16])
        return finish()

    mod = R0.tile([P, 48, 2], F32, "mod")
    wada_v = wada_d.rearrange("(kc p) n -> p kc n", p=P)
    wa = [R1.tile([P, 8, 512], F32, f"wa{i}") for i in range(2)]
    psm = PS[3]
    for gidx in range(12):
        w = wa[gidx % 2]
        K.dma(w.ap[:], wada_v[:, :, gidx * 512:(gidx + 1) * 512], W=[w[:]])
        for cc in range(4):
            j = gidx * 4 + cc
            for kc in range(8):
                mm(psm[:, 2 * j:2 * j + 2], w[:, kc, cc * 128:(cc + 1) * 128], silu_c[:, kc, :],
                   start=(kc == 0), stop=(kc == 7))
    tt("dve", mod[:], hview(psm[:, 0:96], "p (j v) -> p j v", v=2), bc_last(vec_fm[:, 0:48], 2), ALU.add)
    dump("mod", mod[:], [P, 48, 2])
    if stop_after == -0.5:
        return finish()

    sA1 = R0.tile([P, 8, 2], F32, "sA1")
    sA2 = R0.tile([P, 8], F32, "sA2")
    gtg = R0.tile([P, 16], F32, "gtg")
    stt(sA1[:], mod[:, 8:16, :], 1.0, bc_last(vec_fm[:, 48:56], 2), ALU.add, ALU.mult)
    stt(sA2[:], mod[:, 32:40, 0], 1.0, vec_fm[:, 64:72], ALU.add, ALU.mult)
    tt("dve", gtg[:, 0:8], mod[:, 16:24, 0], vec_fm[:, 56:64], ALU.mult)
    tt("dve", gtg[:, 8:16], mod[:, 40:48, 0], vec_fm[:, 72:80], ALU.mult)

    def make_GT(j, GTt, dgs, psa, psb_):
        for half in range(2):
            psb = (psa, psb_)[half]
            for c4 in range(4):
                c = half * 4 + c4
                dgt = dgs[c % 2]
                ts("dve", dgt[:], IDN, gtg[:, j * 8 + c:j * 8 + c + 1], ALU.mult)
                mm(psb[:, c4 * 128:(c4 + 1) * 128], ONES, dgt[:])
            cp("act", GTt[:, half * 512:(half + 1) * 512], psb[:])

    R1.reset()
    if stop_after == 0:
        return finish()

    TOKBLK = [(0, 256)] + [(256 + 512 * b, 512) for b in range(4)]
    hT_raw = R1.alloc([P, 8, NTOK], BF16, "hT")
    hTb = [Tile(hT_raw[:, :, o:o + n], f"hT{b}") for b, (o, n) in enumerate(TOKBLK)]

    def hT_view(tok0, n):
        for b, (o, nn) in enumerate(TOKBLK):
            if o <= tok0 and tok0 + n <= o + nn:
                return hTb[b], tok0 - o
        raise AssertionError

    ssq = R0.tile([P, NT], F32, "ssq")
    rstd = R0.tile([P, NT], F32, "rstd")
    xts = [R1.tile([P, D_MODEL], F32, f"xt{i}") for i in range(3)]
    xns = [R1.tile([P, D_MODEL], BF16, f"xn{i}") for i in range(2)]
    junk = R1.tile([P, D_MODEL], BF16, "junk")

    def norm_tile(xt_v, xn, ssq_col, rstd_col, junk_t):
        act(junk_t[:], xt_v, AF.Square, accum=ssq_col)
        ts("dve", rstd_col, ssq_col, 1.0 / D_MODEL, ALU.mult, EPS, ALU.add)
        tt("pool", rstd_col, rstd_col, c_mhalf, ALU.pow)
        ts("dve", xn[:], xt_v, rstd_col, ALU.mult)

    def transpose_mod(xn, pst, dst_fn, sc_fn, bi_fn, use_act=True):
        pst_bf = V(pst, pst.ap[:].bitcast(BF16))
        for kc in range(8):
            tr(pst_bf[:, kc * 128:(kc + 1) * 128], xn[:, kc * 128:(kc + 1) * 128], idn_bf[:])
        for kc in range(8):
            if use_act:
                act(dst_fn(kc), pst_bf[:, kc * 128:(kc + 1) * 128], AF.Identity, scale=sc_fn(kc), bias=bi_fn(kc))
            else:
                ts("dve", dst_fn(kc), pst_bf[:, kc * 128:(kc + 1) * 128], sc_fn(kc), ALU.mult, bi_fn(kc), ALU.add)

    def p1_stageA(ti):
        xt = xts[ti % 3]
        src = ctx_d[ti * P:(ti + 1) * P, :] if ti < 2 else x_d[(ti - 2) * P:(ti - 1) * P, :]
        K.dma(xt.ap[:], src, W=[xt[:]])
        norm_tile(xt[:], xns[ti % 2], ssq[:, ti:ti + 1], rstd[:, ti:ti + 1], junk)

    def p1_stageB(ti):
        vsel = 1 if ti < 2 else 0
        hb, off = hT_view(ti * P, P)
        transpose_mod(xns[ti % 2], PS[ti % 2],
                      lambda kc: hb[:, kc, off:off + P],
                      lambda kc: sA1[:, kc, vsel:vsel + 1],
                      lambda kc: mod[:, kc, vsel:vsel + 1], use_act=(ti % 2 == 0))

    p1_stageA(0)
    for ti in range(NT):
        if ti + 1 < NT:
            p1_stageA(ti + 1)
        p1_stageB(ti)
    dump("hT", hTb[1][:], [P, 8, 512], BF16)
    if stop_after == 1:
        return finish()

    uT_raw = R3.alloc([P, 4, SEQ], BF16, "uT")
    szT_raw = R4.alloc([P, 4, SEQ], BF16, "szT")
    qT_raw = R2.alloc([P, 4, SEQ], BF16, "qT")
    kT_raw = R5.alloc([P, 4, NTOK], BF16, "kT")
    ktok_raw = R5.alloc([P, NT, 4, P], BF16, "ktok")
    vtok_raw = R5.alloc([P, NT, 4, P], BF16, "vtok")
    uT = [Tile(uT_raw[:, g, :], f"uT{g}") for g in range(4)]
    szT = Tile(szT_raw, "szT")
    qT = [Tile(qT_raw[:, g, :], f"qT{g}") for g in range(4)]
    kT = [Tile(kT_raw[:, g, :], f"kT{g}") for g in range(4)]
    ktok = Tile(ktok_raw, "ktok")
    vtok = Tile(vtok_raw, "vtok")
    abtok = R0.tile([P, NT, 16], F32, "abtok")

    K.barrier()
    R1.cur = R1.start + 8 * NTOK * 2
    wbf = [R6.tile([P, 8, P], BF16, f"wbf{i}") for i in range(3)]
    PADW = NTOK + 4
    pads = [R6.tile([P, PADW], BF16, f"pad{i}") for i in range(2)]
    dgc = [R1.tile([P, 3, P], BF16, f"dgc{i}") for i in range(2)]
    sil = R1.tile([P, NTOK], F32, "sil")
    sqs = [R1.tile([P, 512], BF16, f"sq{i}") for i in range(2)]
    lnt = R1.tile([P, NTOK], F32, "lnt")
    vT_tmp = V(sil, sil.ap[:].bitcast(BF16)[:, 0:NTOK])
    for pd in pads:
        memset("pool", pd[:], 0.0)

    def padcol(tok):
        return 1 + tok if tok < CTX else 3 + tok

    NCH = 21

    def load_w(j):
        w = wbf[j % 3]
        K.dma(w.ap[:], win_d[j], W=[w[:]], queue="pool")

    load_w(0)
    load_w(1)
    state = {"mmrr": 0, "evrr": 0}

    def kind_of(j):
        return "uqkvz"[j // 4] if j < 20 else "ab"

    def main_part(j):
        if j + 2 < NCH:
            load_w(j + 2)
        wb = wbf[j % 3]
        kind = kind_of(j)
        g = j % 4
        if kind == "ab":
            psab = PS[7]
            for t in range(NT):
                hb, off = hT_view(t * P, P)
                for kc in range(8):
                    mm(psab[:, t * 16:(t + 1) * 16], hb[:, kc, off:off + P], wb[:, kc, 0:16],
                       start=(kc == 0), stop=(kc == 7))
            cp("dve", abtok[:], hview(psab[:, 0:NT * 16], "p (t c) -> p t c", c=16))
            return
        blocks = list(range(5)) if kind in "kv" else list(range(1, 5))
        pad = pads[j % 2]
        if kind in "qkv":
            dg = dgc[j % 2]
            cj = j - 4
            for tap in range(3):
                ts("dve", dg[:, tap, :], IDN, cw_fm[:, tap * 12 + cj:tap * 12 + cj + 1], ALU.mult)
        yield
        for b_ in blocks:
            o, n = TOKBLK[b_]
            psa = PS[state["mmrr"] % 5]
            state["mmrr"] += 1
            for kc in range(8):
                mm(psa[:, 0:n], wb[:, kc, :], hTb[b_][:, kc, :], start=(kc == 0), stop=(kc == 7))
            if kind == "u":
                cp("dve", uT[g][:, o - CTX:o - CTX + n], psa[:, 0:n])
            elif kind == "z":
                act(szT[:, g, o - CTX:o - CTX + n], psa[:, 0:n], AF.Silu)
            else:
                pc = padcol(o)
                cp("act" if state["evrr"] % 2 == 0 else "dve", pad[:, pc:pc + n], psa[:, 0:n])
                state["evrr"] += 1
            yield

    def post_a(j):
        kind = kind_of(j)
        if kind not in "qkv":
            return
        blocks = list(range(5)) if kind in "kv" else list(range(1, 5))
        pad = pads[j % 2]
        dg = dgc[j % 2]
        for b_ in blocks:
            o, n = TOKBLK[b_]
            psc = PS[5 + (b_ % 2)]
            pc = padcol(o)
            for tap in range(3):
                mm(psc[:, 0:n], dg[:, tap, :], pad[:, pc + tap - 1:pc + tap - 1 + n],
                   start=(tap == 0), stop=(tap == 2))
            if kind == "v":
                act(vT_tmp[:, o:o + n], psc[:, 0:n], AF.Silu)
            else:
                act(sil[:, o:o + n], psc[:, 0:n], AF.Silu)
            yield
        if kind == "v":
            return
        for b_ in blocks:
            o, n = TOKBLK[b_]
            sq = sqs[b_ % 2]
            tt("dve", sq[:, 0:n], sil[:, o:o + n], sil[:, o:o + n], ALU.mult)
            psn = PS[7]
            mm(psn[:, 0:n], ones_bf[:], sq[:, 0:n])
            act(lnt[:, o:o + n], psn[:, 0:n], AF.Ln, bias=c_eps)
            yield

    def post_b(j):
        kind = kind_of(j)
        if kind not in "qkv":
            return
        g = j % 4
        blocks = list(range(5)) if kind in "kv" else list(range(1, 5))
        if kind == "v":
            src = vT_tmp
            dst = vtok
        else:
            scale = float(P) ** -0.5 if kind == "q" else 1.0
            for b_ in blocks:
                o, n = TOKBLK[b_]
                act(lnt[:, o:o + n], lnt[:, o:o + n], AF.Exp, scale=-0.5)
                if kind == "q":
                    stt(qT[g][:, o - CTX:o - CTX + n], sil[:, o:o + n], scale, lnt[:, o:o + n],
                        ALU.mult, ALU.mult)
                else:
                    stt(kT[g][:, o:o + n], sil[:, o:o + n], scale, lnt[:, o:o + n],
                        ALU.mult, ALU.mult)
                yield
            if kind == "q":
                return
            src = kT[g]
            dst = ktok
        for t0 in (0, 8, 16):
            nt = min(8, NT - t0)
            pst = PS[5 + ((t0 // 8) % 2)]
            pst_bf = V(pst, pst.ap[:].bitcast(BF16))
            for i in range(nt):
                t = t0 + i
                tr(pst_bf[:, i * P:(i + 1) * P], src[:, t * P:(t + 1) * P], idn_bf[:])
            cp("dve", dst[:, t0:t0 + nt, g, :], hview(pst_bf[:, 0:nt * P], "p (t d) -> p t d", d=P))
            yield

    def gen_or_empty(fn, *a):
        r_ = fn(*a)
        return r_ if r_ is not None else iter(())

    def chain2(*gens):
        for g_ in gens:
            yield from g_

    def rr_weighted(threads):
        live = [[g_, w_] for g_, w_ in threads]
        while live:
            for ent in list(live):
                for _ in range(ent[1]):
                    try:
                        next(ent[0])
                    except StopIteration:
                        live.remove(ent)
                        break

    for _ in gen_or_empty(main_part, 0):
        pass
    for j in range(NCH):
        threads = [(chain2(gen_or_empty(post_a, j), gen_or_empty(post_b, j)), 100000)]
        if j + 1 < NCH:
            threads.insert(0, (gen_or_empty(main_part, j + 1), 100000))
        rr_weighted(threads)
    dump("uT", V(uT[0], uT_raw[:, :, 0:512]), [P, 4, 512], BF16)
    dump("szT", szT[:, :, 0:512], [P, 4, 512], BF16)
    dump("qT", V(qT[0], qT_raw[:, :, 0:512]), [P, 4, 512], BF16)
    dump("kT", V(kT[0], kT_raw[:, :, 0:768]), [P, 4, 768], BF16)
    dump("ktok", ktok[:, 0:4, :, :], [P, 4, 4, P], BF16)
    dump("vtok", vtok[:, 0:4, :, :], [P, 4, 4, P], BF16)
    dump("abtok", abtok[:], [P, NT, 16])
    R1.reset()
    R6.reset(barrier=False)
    if stop_after == 2:
        return finish()

    def sc8(name):
        return R6.tile([P, NT, 8], F32, name)

    beta = sc8("beta")
    gg = sc8("g")
    Gs = sc8("Gs")
    Gam = sc8("Gam")
    epsl = sc8("epsl")
    nGb = sc8("nGb")
    xs_t = sc8("xs")
    ax_t = sc8("ax")
    egl = R6.tile([P, NT, 2, 8], F32, "egl")
    act(beta[:], abtok[:, :, 0:8], AF.Sigmoid)
    tt("dve", xs_t[:], abtok[:, :, 8:16], bc_mid(ab_c[:, 0:8], NT), ALU.add)
    stt(ax_t[:], xs_t[:], -1.0, xs_t[:], ALU.mult, ALU.max)
    act(ax_t[:], ax_t[:], AF.Exp, scale=-1.0)
    act(ax_t[:], ax_t[:], AF.Ln, bias=c_one)
    stt(xs_t[:], xs_t[:], 0.0, ax_t[:], ALU.max, ALU.add)
    tt("dve", gg[:], xs_t[:], bc_mid(ab_c[:, 8:16], NT), ALU.mult)
    psG, psGL, psE = PS[0], PS[1], PS[2]
    for t in range(NT):
        for d in range(2):
            mm(psG[:, t * 8 + d * 4:t * 8 + d * 4 + 4], gcv(G_MF if d == 0 else G_MB), gg[:, t, d * 4:d * 4 + 4])
        mm(psGL[:, t * 8:(t + 1) * 8], gcv(G_CB), gg[:, t, :])
        for c in range(2):
            mm(psE[:, (t * 2 + c) * 8:(t * 2 + c + 1) * 8], gcv(G_IND0 if c == 0 else G_IND1), gg[:, t, :])
    cp("act", Gs[:], hview(psG[:, 0:NT * 8], "p (t c) -> p t c", c=8))
    act(Gam[:], Gs[:], AF.Exp)
    tt("dve", epsl[:], hview(psGL[:, 0:NT * 8], "p (t c) -> p t c", c=8), Gs[:], ALU.subtract)
    act(epsl[:], epsl[:], AF.Exp)
    ts("dve", nGb[:], Gam[:], -1.0, ALU.mult)
    act(egl[:], hview(psE[:, 0:NT * 16], "p (t c e) -> p t c e", c=2, e=8), AF.Exp)
    dump("g", gg[:], [P, NT, 8])
    dump("beta", beta[:], [P, NT, 8])
    dump("Gs", Gs[:], [P, NT, 8])
    dump("epsl", epsl[:], [P, NT, 8])
    dump("egl", egl[:], [P, NT, 2, 8])
    if stop_after == 2.5:
        return finish()

    F32R = mybir.dt.float32r
    cr = R6.tile([P, 2, P], F32, "ones_neg1_r")
    cp("dve", V(cr, cr.ap[:, 0, :].bitcast(F32R)), ONES)
    cp("dve", V(cr, cr.ap[:, 1, :].bitcast(F32R)), NEG1)
    ONES_r = cr[:, 0, :]
    NEG1_r = cr[:, 1, :]

    def rr(view):
        return V(view.tile, view.a.bitcast(F32R))

    def f4(name, reg=R1):
        return reg.tile([P, 4, P], F32, name)

    Ttmp = [dict(tA=f4(f"tA{d}"), Qa=f4(f"Qa{d}"), QTa=f4(f"QTa{d}"), Qb=f4(f"Qb{d}"), QTb=f4(f"QTb{d}"))
            for d in range(2)]
    TTs = [[f4(f"TT{d}{i}") for i in range(2)] for d in range(2)]
    attnTs = [[R6.tile([P, 4, P], BF16, f"attnT{d}{i}") for i in range(2)] for d in range(2)]
    qdecs = [[R6.tile([P, 4, P], BF16, f"qdec{d}{i}") for i in range(2)] for d in range(2)]
    rhs2s = [f4(f"rhs2_{d}") for d in range(2)]
    OF_raw = R1.alloc([P, NLT, 4, P], F32, "OF")
    OF = [Tile(OF_raw[:, l], f"OF{l}") for l in range(NLT)]
    vns = [R0.tile([P, 4, P], BF16, f"vn{d}") for d in range(2)]
    vdcs = [[(R0 if (d, c) != (1, 1) else R6).tile([P, 4, P], BF16, f"vd{d}{c}") for c in range(2)] for d in range(2)]
    S = [f4(f"S{d}", R6) for d in range(2)]
    Sb = [R6.tile([P, 4, P], BF16, f"Sb{d}") for d in range(2)]
    for d in range(2):
        memset("pool", S[d][:], 0.0)
        memset("pool", Sb[d][:], 0.0)
        memset("pool", vns[d][:], 0.0)
        memset("pool", Ttmp[d]["tA"][:], 0.0)
        cp("dve", rr(rhs2s[d][:]), Ttmp[d]["tA"][:])
        for c in range(2):
            memset("pool", vdcs[d][c][:], 0.0)
    TPSd = [[PS[0], PS[1], PS[2]], [PS[3], PS[4], PS[5]]]
    SPS = [PS[6], PS[7]]

    def t_phase(t, d, slot):
        lat = t >= 2
        l = t - 2
        Mdir = gcv(G_MF if d == 0 else G_MB)
        NMA = gcv(G_NM_SL if d == 0 else G_NM_SU)
        PMT = gcv(G_PM_UI if d == 0 else G_PM_LI)
        gcol = gg[:, t, d * 4:d * 4 + 4]
        bcol = beta[:, t, d * 4:d * 4 + 4]
        tk = slice(t * P, (t + 1) * P)
        TT = TTs[d][slot]
        aT = attnTs[d][slot]
        qd = qdecs[d][slot]
        tm = Ttmp[d]
        tA, Qa, QTa, Qb_, QTb_ = tm["tA"], tm["Qa"], tm["QTa"], tm["Qb"], tm["QTb"]
        dA, dT_ = QTb_, Qb_
        p0, p1, p2 = TPSd[d]
        tt("dve", rr(Qa[:]), bc_mid(Mdir, 4), bc_last(gcol, P), ALU.mult)
        for h in range(4):
            mm(p0[:, h * P:(h + 1) * P], rr(Qa[:, h, :]), rr(ONES_r), start=True, stop=False)
            mm(p0[:, h * P:(h + 1) * P], rr(NEG1_r), rr(Qa[:, h, :]), start=False, stop=True)
        for h in range(4):
            mm(p1[:, h * P:(h + 1) * P], kT[h][:, tk], kT[h][:, tk])
        yield
        stt(rr(dA[:]), ps4(p0), 0.0, bc_mid(NMA, 4), ALU.min, ALU.add)
        act(rr(dA[:]), dA[:], AF.Exp)
        if lat:
            stt(rr(dT_[:]), ps4(p0), 0.0, bc_mid(PMT, 4), ALU.max, ALU.add)
            act(rr(dT_[:]), dT_[:], AF.Exp, scale=-1.0)
        yield
        tt("dve", tA[:], ps4(p1), dA[:], ALU.mult)
        tt("dve", rr(Qa[:]), tA[:], bc_last(bcol, P), ALU.mult)
        for h in range(4):
            tr(p0[:, h * P:(h + 1) * P], Qa[:, h, :], IDN)
        if lat:
            for h in range(4):
                mm(p1[:, h * P:(h + 1) * P], kT[h][:, tk], qT[h][:, l * P:(l + 1) * P])
        yield
        cp("act", rr(QTa[:]), ps4(p0))
        stt(rr(TT[:]), QTa[:], -1.0, bc_mid(IDN, 4), ALU.mult, ALU.add)
        if lat:
            tt("dve", aT[:], ps4(p1), dT_[:], ALU.mult)
            tt("dve", rr(QTb_[:]), bc_mid(IDN, 4), bc_last(Gam[:, t, d * 4:d * 4 + 4], P), ALU.mult)
        yield
        for h in range(4):
            mm(p0[:, h * P:(h + 1) * P], rr(QTa[:, h, :]), rr(Qa[:, h, :]))
        for h in range(4):
            mm(p1[:, h * P:(h + 1) * P], rr(Qa[:, h, :]), rr(QTa[:, h, :]))
        if lat:
            for h in range(4):
                mm(p2[:, h * P:(h + 1) * P], rr(ONES_r), rr(QTb_[:, h, :]))
        yield
        cp("act", rr(Qb_[:]), ps4(p0))
        cp("act", rr(QTb_[:]), ps4(p1))
        if lat:
            tt("dve", qd[:], ps4(p2), V(qT[0], qT_raw[:, :, l * P:(l + 1) * P]), ALU.mult)
        yield
        Q, QT = Qb_, QTb_
        for k in range(1, 6):
            Qn, QTn = (Qa, QTa) if k % 2 == 1 else (Qb_, QTb_)
            if k < 5:
                for h in range(4):
                    mm(p0[:, h * P:(h + 1) * P], rr(QT[:, h, :]), rr(Q[:, h, :]))
            if k < 4:
                for h in range(4):
                    mm(p1[:, h * P:(h + 1) * P], rr(Q[:, h, :]), rr(QT[:, h, :]))
            for h in range(4):
                mm(p2[:, h * P:(h + 1) * P], rr(Q[:, h, :]), rr(TT[:, h, :]))
            yield
            if k < 5:
                cp("act", rr(Qn[:]), ps4(p0))
            if k < 4:
                cp("act", rr(QTn[:]), ps4(p1))
            tt("dve", rr(TT[:]), ps4(p2), TT[:], ALU.add)
            yield
            Q, QT = Qn, QTn
        tt("dve", rr(TT[:]), TT[:], bc_last(bcol, P), ALU.mult)
        yield

    def scan(t, d, slot):
        lat = t >= 2
        l = t - 2
        psX = SPS[d]
        psS = psX
        TT = TTs[d][slot]
        aT = attnTs[d][slot]
        qd = qdecs[d][slot]
        rhs2, vn, vdc = rhs2s[d], vns[d], vdcs[d]
        tk = slice(t * P, (t + 1) * P)
        for c in ((0, 1) if d == 0 else (1, 0)):
            r0 = c * 64
            rows = slice(r0, r0 + 64)
            for h in range(4):
                mm(psX[:, h * P:(h + 1) * P], kT[h][:, tk], Sb[d][:, h, :])
            yield
            for h in range(4):
                stt(rr(rhs2[rows, h, :]), psX[rows, h * P:(h + 1) * P], nGb[rows, t, d * 4 + h:d * 4 + h + 1],
                    vtok[rows, t, h, :], ALU.mult, ALU.add)
            yield
            for h in range(4):
                mm(psX[:, h * P:(h + 1) * P], rr(TT[:, h, :]), rr(rhs2[:, h, :]))
            yield
            psX4 = hview(psX[rows, :], "p (h d) -> p h d", d=P)
            cp("act", vn[rows], psX4)
            for h in range(4):
                act(vdc[c][rows, h, :], psX[rows, h * P:(h + 1) * P], AF.Identity,
                    scale=epsl[rows, t, d * 4 + h:d * 4 + h + 1])
            yield
            if lat:
                for h in range(4):
                    mm(psX[:, h * P:(h + 1) * P], qd[:, h, :], Sb[d][:, h, :], start=True, stop=False)
                    mm(psX[:, h * P:(h + 1) * P], aT[:, h, :], vn[:, h, :], start=False, stop=True)
                yield
                if (d == 0) != (l < NLT // 2):
                    tt("dve", OF[l][rows], psX4, OF[l][rows], ALU.add)
                else:
                    cp("act", OF[l][rows], psX4)
            for h in range(4):
                mm(psS[:, h * P:(h + 1) * P], ktok[:, t, h, :], vdc[c][:, h, :])
            yield
            for h in range(4):
                stt(S[d][:, h, :], S[d][:, h, :], egl[:, t, c, d * 4 + h:d * 4 + h + 1],
                    psS[:, h * P:(h + 1) * P], ALU.mult, ALU.add)
            cp("act", Sb[d][:], S[d][:])
            yield

    orders = [list(range(NT)), [1, 0] + list(range(NT - 1, 1, -1))]
    nsteps = NT
    if "GDN_TILES" in os.environ:
        nsteps = int(os.environ["GDN_TILES"])
        orders[1] = [1, 0] + [None] * NT

    def chain(*gens):
        for g_ in gens:
            yield from g_

    def run_weighted(threads):
        live = [[g_, w_] for g_, w_ in threads]
        while live:
            for ent in list(live):
                for _ in range(ent[1]):
                    try:
                        next(ent[0])
                    except StopIteration:
                        live.remove(ent)
                        break

    def tgen(i, d):
        if i < nsteps and orders[d][i] is not None:
            return t_phase(orders[d][i], d, i % 2)
        return iter(())

    run_weighted([(tgen(0, 0), 1), (tgen(0, 1), 1)])
    TW = int(os.environ.get("GDN_TW", "1"))
    SW = int(os.environ.get("GDN_SW", "1"))
    for i in range(nsteps):
        threads = []
        for d in range(2):
            if orders[d][i] is not None:
                threads.append((scan(orders[d][i], d, i % 2), SW))
            threads.append((tgen(i + 1, d), TW))
        run_weighted(threads)
        if i == 1:
            dump("S_ctx", S[0][:], [P, 4, P])
        if i == 2:
            dump("TT", TTs[0][0][:], [P, 4, P])
    dump("OF", V(OF[0], OF_raw[:, 0:4]), [P, 4, 4, P])
    if stop_after == 3:
        return finish()

    K.barrier()
    ydT_raw = qT_raw
    ydT = Tile(ydT_raw, "ydT")
    ssq4 = R6.tile([P, NLT, 4], F32, "ssq4")
    junk4 = R6.tile([P, P], BF16, "junk4")
    on_bf = [R6.tile([P, 4, P], BF16, f"on{i}") for i in range(2)]
    for l in range(NLT):
        for h in range(4):
            act(junk4[:], OF[l][:, h, :], AF.Square, accum=ssq4[:, l, h:h + 1])
    ts("dve", ssq4[:], ssq4[:], 1.0 / P, ALU.mult, EPS, ALU.add)
    tt("pool", ssq4[:], ssq4[:], V(cst, cst.ap[:, 1:2].unsqueeze(2).broadcast_to([P, NLT, 4])), ALU.pow)
    for l in range(NLT):
        on = on_bf[l % 2]
        for h in range(4):
            ts("dve", on[:, h, :], OF[l][:, h, :], ssq4[:, l, h:h + 1], ALU.mult)
        pst = PS[l % 2]
        pst_bf = V(pst, pst.ap[:].bitcast(BF16))
        for h in range(4):
            tr(pst_bf[:, h * P:(h + 1) * P], on[:, h, :], idn_bf[:])
        stt(ydT[:, :, l * P:(l + 1) * P], hview(pst_bf[:, 0:4 * P], "p (h d) -> p h d", d=P), gg_fm[:, 0:1],
            szT[:, :, l * P:(l + 1) * P], ALU.mult, ALU.mult)
    dump("ydT", ydT[:, :, 0:512], [P, 4, 512], BF16)
    R1.reset()
    R6.reset(barrier=False)
    if stop_after == 4:
        return finish()

    R5.reset(barrier=False)
    Vsb = R5.tile([P, NLT, 4, 256], BF16, "Vsb")
    yfT_raw = R5.alloc([P, 4, SEQ], BF16, "yfT")
    yfT = [Tile(yfT_raw[:, g, :], f"yfT{g}") for g in range(4)]
    cfsf = R6.tile([P, 256], F32, "cfsf")
    wf_sb = R6.tile([P, 4, P], F32, "wf_sb")
    CW = R6.tile([P, 4, 256], BF16, "CW")
    K.dma(cfsf.ap[:], cfsf_d, W=[cfsf[:]])
    K.dma(wf_sb.ap[:], wf_d.rearrange("g c d -> c g d"), W=[wf_sb[:]])
    for g in range(4):
        psw = PS[g % 2]
        mm(psw[:, 0:P], cfsf[:, 0:P], wf_sb[:, g, :])
        mm(psw[:, P:2 * P], cfsf[:, P:2 * P], wf_sb[:, g, :])
        cp("dve", CW[:, g, :], psw[:, 0:256])
    for lt in range(NLT):
        for gp in range(2):
            psv = PS[2 + (lt * 2 + gp) % 2]
            for gi in range(2):
                g = gp * 2 + gi
                mm(psv[:, gi * 256:(gi + 1) * 256], uT[g][:, lt * P:(lt + 1) * P], CW[:, g, :])
            cp("act" if gp == 0 else "dve", Vsb[:, lt, gp * 2:gp * 2 + 2, :],
               hview(psv[:, :], "p (g c) -> p g c", c=256))
    tabs = [R1.tile([P, 4, 2, 512], BF16, f"tab{i}") for i in range(3)]
    ti = 0

    def load_tab(idx):
        mb, l4 = divmod(idx, 4)
        tb = tabs[idx % 3]
        K.dma(tb.ap[:], tab_d[mb, l4 * 4:(l4 + 1) * 4].rearrange("t p c m -> p t c m"), W=[tb[:]])

    load_tab(0)
    load_tab(1)
    for mb in range(4):
        pss = [PS[4 + g] for g in range(4)]
        for l4 in range(4):
            idx = mb * 4 + l4
            if idx + 2 < 16:
                load_tab(idx + 2)
            tb = tabs[idx % 3]
            for li in range(4):
                lt = l4 * 4 + li
                for g in range(4):
                    mm(pss[g][:, :], Vsb[:, lt, g, 0:P], tb[:, li, 0, :], start=(lt == 0), stop=False)
                    mm(pss[g][:, :], Vsb[:, lt, g, P:2 * P], tb[:, li, 1, :], start=False, stop=(lt == NLT - 1))
        for g in range(4):
            cp("act" if g % 2 == 0 else "dve", yfT[g][:, mb * 512:(mb + 1) * 512], pss[g][:, :])
    dump("yfT", V(yfT[0], yfT_raw[:, :, 0:512]), [P, 4, 512], BF16)
    R1.reset()
    R6.reset(barrier=False)
    if stop_after == 5:
        return finish()

    R3.reset(barrier=False)
    R4.reset(barrier=False)
    X1_raw = R1.alloc([P, NLT, D_MODEL], F32, "X1")
    X1 = [Tile(X1_raw[:, l, :], f"X1_{l}") for l in range(NLT)]
    wo_raw = R3.alloc([P, 8, D_MODEL], BF16, "wo_bf")
    wo_bf = [Tile(wo_raw[:, kc, :], f"wo{kc}") for kc in range(8)]
    GT1 = R4.tile([P, D_MODEL], F32, "GT1")
    xr = [R4.tile([P, D_MODEL], F32, f"xr{i}") for i in range(2)]
    tmp5 = R4.tile([P, D_MODEL], F32, "tmp5")
    junk5 = R6.tile([P, D_MODEL], BF16, "junk5")
    dgs5 = [R6.tile([P, P], F32, f"dgs5_{i}") for i in range(2)]
    ssq5 = R6.tile([P, NLT], F32, "ssq5")
    wout_v = wout_d.rearrange("(kc p) n -> p kc n", p=P)
    for kc in range(8):
        K.dma(wo_bf[kc].ap, wout_v[:, kc, :], W=[wo_bf[kc][:]], queue="pool")
    make_GT(0, GT1, dgs5, PS[4], PS[5])
    mixT = [yfT[g] for g in range(4)] + [V(ydT, ydT_raw[:, h, :]) for h in range(4)]

    ssq5b = R6.tile([P, NLT], F32, "ssq5b")
    for l in range(NLT):
        K.dma(xr[l % 2].ap[:], x_d[l * P:(l + 1) * P, :], W=[xr[l % 2][:]])
        psA, psB = PS[(l % 2) * 2], PS[(l % 2) * 2 + 1]
        for half, psx in enumerate((psA, psB)):
            for kc in range(8):
                mm(psx[:, :], mixT[kc][:, l * P:(l + 1) * P], wo_bf[kc][:, half * 512:(half + 1) * 512],
                   start=(kc == 0), stop=(kc == 7))
        act(junk5[:, 0:512], psA[:, :], AF.Square, accum=ssq5[:, l:l + 1])
        act(junk5[:, 512:1024], psB[:, :], AF.Square, accum=ssq5b[:, l:l + 1])
        tt("dve", ssq5[:, l:l + 1], ssq5[:, l:l + 1], ssq5b[:, l:l + 1], ALU.add)
        ts("dve", ssq5[:, l:l + 1], ssq5[:, l:l + 1], 1.0 / D_MODEL, ALU.mult, EPS, ALU.add)
        tt("pool", ssq5[:, l:l + 1], ssq5[:, l:l + 1], c_mhalf, ALU.pow)
        stt(tmp5[:, 0:512], psA[:, :], ssq5[:, l:l + 1], GT1[:, 0:512], ALU.mult, ALU.mult)
        stt(tmp5[:, 512:1024], psB[:, :], ssq5[:, l:l + 1], GT1[:, 512:1024], ALU.mult, ALU.mult)
        tt("pool", X1[l][:], tmp5[:], xr[l % 2][:], ALU.add)
    dump("X1", V(X1[0], X1_raw[:, 0:4, :]), [P, 4, D_MODEL])
    R2.reset()
    R3.reset(barrier=False)
    R4.reset(barrier=False)
    R5.reset(barrier=False)
    R6.reset(barrier=False)
    if stop_after == 6:
        return finish()

    FF = Arena(nc, K, base + 80 * KB, avail - 80 * KB, "FF")
    actT_raw = FF.alloc([P, NFC, 1024], BF16, "actT")
    actT = [Tile(actT_raw[:, c, :], f"actT{c}") for c in range(NFC)]
    GT2 = FF.tile([P, D_MODEL], F32, "GT2")
    ssq6 = FF.tile([P, 32], F32, "ssq6")
    ssq6b = FF.tile([P, 32], F32, "ssq6b")
    dgs6 = [FF.tile([P, P], F32, f"dgs6_{i}") for i in range(2)]
    make_GT(1, GT2, dgs6, PS[4], PS[5])
    NWA = 11
    wdA_raw = FF.alloc([P, NWA, D_MODEL], BF16, "wdA")
    wdA = [Tile(wdA_raw[:, kc, :], f"wdA{kc}") for kc in range(NWA)]
    ff_mark = FF.cur
    NLOC = 9 * P
    for hf in range(2):
        FF.cur = ff_mark
        h2T = FF.tile([P, 8, NLOC], BF16, "h2T")
        wbf6 = [FF.tile([P, 8, 2, P], BF16, f"wbf6_{i}") for i in range(3)]
        PR, PC = 18, 66
        gpad = [FF.tile([P, PR, PC], BF16, f"gpad{i}") for i in range(2)]
        dg9 = [FF.tile([P, 9, P], BF16, f"dg9_{i}") for i in range(2)]
        sg = [FF.tile([P, 512], F32, f"sg{i}") for i in range(2)]
        xn6 = [FF.tile([P, D_MODEL], BF16, f"xn6_{i}") for i in range(2)]
        junk6 = FF.tile([P, D_MODEL], BF16, "junk6")
        for gp_ in gpad:
            memset("pool", gp_[:], 0.0)
        tile0 = 0 if hf == 0 else 7
        def f_stageA(i):
            l = tile0 + i
            col = hf * 16 + i
            norm_tile(X1[l][:], xn6[i % 2], ssq6[:, col:col + 1], ssq6b[:, col:col + 1], junk6)

        def f_stageB(i):
            transpose_mod(xn6[i % 2], PS[i % 2],
                          lambda kc: h2T[:, kc, i * P:(i + 1) * P],
                          lambda kc: sA2[:, kc:kc + 1],
                          lambda kc: mod[:, 24 + kc, 0:1], use_act=(i % 2 == 0))

        f_stageA(0)
        for i in range(9):
            if i + 1 < 9:
                f_stageA(i + 1)
            f_stageB(i)
        main0 = 0 if hf == 0 else 128
        halo0 = 1024 if hf == 0 else 64
        halo_row = 17 if hf == 0 else 0

        def load_wup(cc):
            w = wbf6[cc % 3]
            K.dma(w.ap[:], wup_d[cc], W=[w[:]], queue="pool")

        load_wup(0)
        load_wup(1)
        if hf == 0:
            for kc in range(NWA):
                K.dma(wdA[kc].ap, wdown_d[kc * P:(kc + 1) * P, :], W=[wdA[kc][:]], queue="pool")
        for cc in range(NFC):
            if cc + 2 < NFC:
                load_wup(cc + 2)
            wb = wbf6[cc % 3]
            dg = dg9[cc % 2]
            gp_ = gpad[cc % 2]
            for tap in range(9):
                ts("dve", dg[:, tap, :], IDN, dw_fm[:, tap * NFC + cc:tap * NFC + cc + 1], ALU.mult)
            psgs = [PS[2], PS[3]]
            for piece in range(2):
                psg = psgs[piece]
                t0 = main0 + piece * 512
                for kc in range(8):
                    mm(psg[:, :], wb[:, kc, 1, :], h2T[:, kc, t0:t0 + 512], start=(kc == 0), stop=(kc == 7))
                cp("act", gp_[:, 1 + piece * 8:9 + piece * 8, 1:65], hview(psg[:, :], "p (r c) -> p r c", c=64))
            psh = PS[4]
            for kc in range(8):
                mm(psh[:, 0:64], wb[:, kc, 1, :], h2T[:, kc, halo0:halo0 + 64], start=(kc == 0), stop=(kc == 7))
            cp("act", gp_[:, halo_row, 1:65], psh[:, 0:64])
            psvs = [PS[0], PS[1]]
            for piece in range(2):
                t0 = main0 + piece * 512
                for kc in range(8):
                    mm(psvs[piece][:, :], wb[:, kc, 0, :], h2T[:, kc, t0:t0 + 512], start=(kc == 0), stop=(kc == 7))
            for piece in range(2):
                psc = PS[5 + piece]
                R0_ = 1 + piece * 8
                for tap in range(9):
                    kh, kw = divmod(tap, 3)
                    mm(hview(psc[:, :], "p (r c) -> p r c", c=64), dg[:, tap, :],
                       gp_[:, R0_ + kh - 1:R0_ + kh - 1 + 8, kw:kw + 64], start=(tap == 0), stop=(tap == 8))
                act(sg[piece][:], psc[:, :], AF.Silu)
                tt("dve", actT[cc][:, piece * 512:(piece + 1) * 512], psvs[piece][:, :], sg[piece][:], ALU.mult)
        if hf == 0:
            dump("actT", V(actT[0], actT_raw[:, 0:4, 0:512]), [P, 4, 512], BF16)
        K.barrier()
        FF.cur = ff_mark
        wdB_raw = FF.alloc([P, NFC - NWA, D_MODEL], BF16, "wdB")
        wdB = [Tile(wdB_raw[:, kc, :], f"wdB{kc}") for kc in range(NFC - NWA)]
        wd_bf = wdA + wdB
        ot = [FF.tile([P, D_MODEL], F32, f"ot{i}") for i in range(2)]
        tmp6 = FF.tile([P, D_MODEL], F32, "tmp6")
        junk7 = FF.tile([P, D_MODEL], BF16, "junk7")
        sq7 = FF.tile([P, 16], F32, "sq7")
        sq7b = FF.tile([P, 16], F32, "sq7b")
        for kc in range(NWA, NFC):
            K.dma(wd_bf[kc].ap, wdown_d[kc * P:(kc + 1) * P, :], W=[wd_bf[kc][:]], queue="pool")
        for grp in range(2):
            for kc in range(NFC):
                for j in range(4):
                    i = grp * 4 + j
                    for half in range(2):
                        mm(PS[2 * j + half][:, :], actT[kc][:, i * P:(i + 1) * P],
                           wd_bf[kc][:, half * 512:(half + 1) * 512], start=(kc == 0), stop=(kc == NFC - 1))
            for j in range(4):
                i = grp * 4 + j
                l = hf * 8 + i
                psA, psB = PS[2 * j], PS[2 * j + 1]
                c1, c2 = sq7[:, i:i + 1], sq7b[:, i:i + 1]
                act(junk7[:, 0:512], psA[:, :], AF.Square, accum=c1)
                act(junk7[:, 512:1024], psB[:, :], AF.Square, accum=c2)
                tt("dve", c1, c1, c2, ALU.add)
                ts("dve", c1, c1, 1.0 / D_MODEL, ALU.mult, EPS, ALU.add)
                tt("pool", c1, c1, c_mhalf, ALU.pow)
                stt(tmp6[:, 0:512], psA[:, :], c1, GT2[:, 0:512], ALU.mult, ALU.mult)
                stt(tmp6[:, 512:1024], psB[:, :], c1, GT2[:, 512:1024], ALU.mult, ALU.mult)
                o_t = ot[i % 2]
                tt("pool", o_t[:], tmp6[:], X1[l][:], ALU.add)
                K.dma(out_d[l * P:(l + 1) * P, :], o_t.ap[:], R=[o_t[:]], is_output=True)
        K.barrier()
    return finish()


G_IDN, G_ONES, G_NEG1, G_MF, G_MB, G_CB, G_IND0, G_IND1, G_NM_SL, G_NM_SU, G_PM_LI, G_PM_UI = range(12)
NG = 12
BIG = 30000.0


def _gconst():
    t = np.arange(P)
    a = t[:, None]
    b = t[None, :]
    same = (a // 64) == (b // 64)
    g = np.zeros((NG, P, P), np.float32)
    g[G_IDN] = np.eye(P)
    g[G_ONES] = 1.0
    g[G_NEG1] = -1.0
    g[G_MF] = (same & (a <= b))
    g[G_MB] = (same & (a >= b))
    g[G_CB] = same
    g[G_IND0] = (a < 64) * np.ones((1, P))
    g[G_IND1] = (a >= 64) * np.ones((1, P))
    g[G_NM_SL] = np.where(same & (a > b), 0.0, -BIG)
    g[G_NM_SU] = np.where(same & (a < b), 0.0, -BIG)
    g[G_PM_LI] = np.where(same & (a >= b), 0.0, BIG)
    g[G_PM_UI] = np.where(same & (a <= b), 0.0, BIG)
    return np.ascontiguousarray(g.transpose(1, 0, 2))


def _dft_consts():
    L, Fd = SEQ, 128
    sc = 1.0 / np.sqrt(L * Fd)
    n = np.arange(Fd)
    ang = 2.0 * np.pi * ((n[:, None] * n[None, :]) % Fd) / Fd
    cfsf = np.concatenate([np.cos(ang) * sc, -np.sin(ang) * sc], axis=1).astype(np.float32)
    l = np.arange(L, dtype=np.int64)
    prod = (l[:, None] * l[None, :]) % L
    angL = 2.0 * np.pi * prod / L
    c = np.cos(angL).astype(np.float32)
    s = np.sin(angL).astype(np.float32)
    tab = np.stack([c, s], axis=0)
    tab = tab.reshape(2, 16, 128, 4, 512)
    tab = tab.transpose(3, 1, 2, 0, 4)
    return cfsf, np.ascontiguousarray(tab).astype(ml_dtypes.bfloat16)


_CACHE = {}


def _host_consts():
    if "c" not in _CACHE:
        cfsf, tab = _dft_consts()
        _CACHE["c"] = (_gconst(), cfsf, tab)
    return _CACHE["c"]


def _relayout_win(w):
    wp = np.zeros((D_MODEL, 21 * P), np.float32)
    wp[:, :IN_W] = w
    return np.ascontiguousarray(wp.reshape(8, P, 21, P).transpose(2, 1, 0, 3))


def _relayout_wup(w):
    return np.ascontiguousarray(w.reshape(8, P, 2, NFC, P).transpose(3, 1, 0, 2, 4))


def make_in_maps(inputs, n_cores=8):
    gconst, cfsf, tab = _host_consts()
    f = lambda a: np.ascontiguousarray(np.asarray(a, dtype=np.float32))
    x = f(inputs["x"]); c = f(inputs["c"]); ctx = f(inputs["ctx"]); c_ctx = f(inputs["c_ctx"])
    shared = {
        "w_ada": f(inputs["w_ada"][0]), "w_in": _relayout_win(f(inputs["w_in"][0])), "w_out": f(inputs["w_out"][0]),
        "w_up": _relayout_wup(f(inputs["w_up"][0])), "w_down": f(inputs["w_down"][0]),
        "convw": f(inputs["w_qkv_conv"][0]).reshape(36, P),
        "dwcw": f(inputs["w_dwc"][0]).reshape(198, P),
        "g_gdn": f(inputs["g_gdn"][0]).reshape(1, P),
        "a_log": f(inputs["a_log"][0]).reshape(8), "dt_bias": f(inputs["dt_bias"][0]).reshape(8),
        "w_fourier": f(inputs["w_fourier"][0]),
        "gconst": gconst, "cfsf": cfsf, "dft_tab": tab,
    }
    maps = []
    for b in range(n_cores):
        vecs = np.concatenate([
            f(inputs["b_ada"][0]).reshape(48, P), f(inputs["g_pre_mix"][0]).reshape(8, P),
            f(inputs["g_post_mix"][0]).reshape(8, P), f(inputs["g_pre_ffn"][0]).reshape(8, P),
            f(inputs["g_post_ffn"][0]).reshape(8, P), c[b].reshape(8, P), c_ctx.reshape(8, P)], axis=0)
        m = dict(shared)
        m["x"] = x[b]
        m["ctx"] = ctx[b]
        m["vecs"] = np.ascontiguousarray(vecs)
        maps.append(m)
    return maps


def kernel(**inputs):
    nc, _ = build_program()
    maps = make_in_maps(inputs, 8)
    res = run_bass_kernel_spmd(nc, maps, core_ids=list(range(8)))
    return np.stack([np.asarray(r["out"], dtype=np.float32) for r in res.results], axis=0)
```

```python
import os
import numpy as np
import ml_dtypes
import concourse.bass as bass
import concourse.mybir as mybir
from concourse.bass_utils import run_bass_kernel_spmd

F32 = mybir.dt.float32
BF16 = mybir.dt.bfloat16
AF = mybir.ActivationFunctionType
ALU = mybir.AluOpType

P = 128
D_MODEL = 1024
SEQ = 2048
CTX = 256
NTOK = SEQ + CTX
NT = NTOK // P
NLT = SEQ // P
IN_W = 2576
D_FF = 2816
NFC = D_FF // P
EPS = 1e-6
EPOCH = 12000
NDMA = 24


class Tile:
    def __init__(self, ap, name="", psum=False):
        self.ap = ap
        self.name = name
        self.w = None
        self.r = {}
        self.psum = psum

    def __getitem__(self, idx):
        return V(self, self.ap[idx])

    def v(self, ap):
        return V(self, ap)


class V:
    def __init__(self, tile, a):
        self.tile = tile
        self.a = a

    def __getitem__(self, idx):
        return V(self.tile, self.a[idx])

    def bc(self, shape):
        return V(self.tile, self.a.broadcast_to(list(shape)))

    def bitcast(self, dt):
        return V(self.tile, self.a.bitcast(dt))


class _Eng:
    def __init__(self, K, name, h):
        self.K = K
        self.name = name
        self.h = h
        self.sem = None
        self.cnt = 0
        self.seen = {}
        self.nsem = 0
        self.ninst = 0

    def newsem(self):
        self.sem = self.K.nc.alloc_semaphore(f"s_{self.name}_{self.nsem}")
        self.nsem += 1
        self.cnt = 0


class Kern:
    def __init__(self, nc):
        self.nc = nc
        self.E = {}
        for name, h in (("pe", nc.tensor), ("dve", nc.vector), ("act", nc.scalar),
                        ("pool", nc.gpsimd), ("sp", nc.sync)):
            e = _Eng(self, name, h)
            if name != "sp":
                e.newsem()
            self.E[name] = e
        self.dpool = [[nc.alloc_semaphore(f"s_dma_{i}"), 0] for i in range(NDMA)]
        self.drr = 0
        self.out_tokens = []

    def _deps(self, e, W, R):
        need = []
        for v in R:
            t = v.tile
            if t.w is not None:
                need.append((t.w, "raw"))
            if t.psum:
                for tok in t.r.values():
                    if tok[2] is not e:
                        need.append((tok, "rar"))
        for v in W:
            t = v.tile
            if t.w is not None:
                need.append((t.w, "waw"))
            for tok in t.r.values():
                need.append((tok, "war"))
        for (sem, val, src), kind in need:
            if src is e:
                if e.name == "pe":
                    continue
            if e.seen.get(id(sem), 0) >= val:
                continue
            e.h.wait_ge(sem, val)
            e.seen[id(sem)] = val

    def _post(self, tok, W, R):
        sem = tok[0]
        for v in R:
            v.tile.r[id(sem)] = tok
        for v in W:
            v.tile.w = tok
            v.tile.r = {}

    def emit(self, eng, fn, W=(), R=()):
        e = self.E[eng]
        self._deps(e, W, R)
        if e.cnt >= EPOCH:
            e.newsem()
        inst = fn(e.h)
        e.cnt += 1
        e.ninst += 1
        inst.then_inc(e.sem, 1)
        tok = (e.sem, e.cnt, e)
        self._post(tok, W, R)
        return tok

    def dma(self, out, in_, W=(), R=(), is_output=False, queue="sp"):
        e = self.E[queue]
        self._deps(e, W, R)
        slot = self.dpool[self.drr]
        self.drr = (self.drr + 1) % NDMA
        sem, n = slot
        if n > 0 and e.seen.get(id(sem), 0) < 16 * n:
            e.h.wait_ge(sem, 16 * n)
            e.seen[id(sem)] = 16 * n
        e.h.dma_start(out=out, in_=in_).then_inc(sem, 16)
        slot[1] = n + 1
        tok = (sem, 16 * (n + 1), None)
        self._post(tok, W, R)
        if is_output:
            self.out_tokens.append(tok)
        return tok

    def barrier(self):
        names = ["pe", "dve", "act", "pool"]
        for a in names:
            ea = self.E[a]
            for b in names + ["sp"]:
                if a == b:
                    continue
                eb = self.E[b]
                if b == "sp":
                    continue
                if eb.cnt == 0:
                    continue
                if ea.seen.get(id(eb.sem), 0) >= eb.cnt:
                    continue
                ea.h.wait_ge(eb.sem, eb.cnt)
                ea.seen[id(eb.sem)] = eb.cnt
            for sem, n in self.dpool:
                if n > 0 and ea.seen.get(id(sem), 0) < 16 * n:
                    ea.h.wait_ge(sem, 16 * n)
                    ea.seen[id(sem)] = 16 * n
        sp = self.E["sp"]
        for b in names:
            eb = self.E[b]
            if eb.cnt and sp.seen.get(id(eb.sem), 0) < eb.cnt:
                sp.h.wait_ge(eb.sem, eb.cnt)
                sp.seen[id(eb.sem)] = eb.cnt
        for sem, n in self.dpool:
            if n > 0 and sp.seen.get(id(sem), 0) < 16 * n:
                sp.h.wait_ge(sem, 16 * n)
                sp.seen[id(sem)] = 16 * n

    def finish(self):
        sp = self.E["sp"]
        for sem, val, _ in self.out_tokens:
            if sp.seen.get(id(sem), 0) < val:
                sp.h.wait_ge(sem, val)
                sp.seen[id(sem)] = val


class Arena:
    def __init__(self, nc, K, start, size, name):
        self.nc = nc
        self.K = K
        self.start = start
        self.end = start + size
        self.cur = start
        self.name = name
        self.n = 0

    def alloc(self, shape, dt, name=None):
        esz = 4 if dt == F32 else 2
        nbytes = esz
        for s in shape[1:]:
            nbytes *= s
        off = (self.cur + 63) // 64 * 64
        if off + nbytes > self.end:
            raise RuntimeError(f"arena {self.name} overflow allocating {name} {shape}: "
                               f"{off - self.start}+{nbytes} > {self.end - self.start}")
        self.cur = off + nbytes
        self.n += 1
        return self.nc.alloc_sbuf_tensor_at(f"{self.name}_{name or 't'}_{self.n}", list(shape), dt, offset=off)

    def tile(self, shape, dt, name=None):
        return Tile(self.alloc(shape, dt, name), name or "t")

    def reset(self, barrier=True):
        if barrier:
            self.K.barrier()
        self.cur = self.start


KB = 1024


def build_program(debug=None, stop_after=None):
    debug = debug or []
    nc = bass.Bass("TRN2", target_bir_lowering=False)
    K = Kern(nc)
    base = (nc.sbuf_base + 63) // 64 * 64
    avail = nc.sbuf_top - base
    R0 = Arena(nc, K, base, 16 * KB, "R0")
    R1 = Arena(nc, K, base + 16 * KB, 64 * KB, "R1")
    R2 = Arena(nc, K, base + 80 * KB, 16 * KB, "R2")
    R3 = Arena(nc, K, base + 96 * KB, 16 * KB, "R3")
    R4 = Arena(nc, K, base + 112 * KB, 16 * KB, "R4")
    R5 = Arena(nc, K, base + 128 * KB, 54 * KB, "R5")
    R6 = Arena(nc, K, base + 182 * KB, avail - 182 * KB, "R6")
    dbg_out = {}

    def dram(name, shape, dt=F32, kind="ExternalInput"):
        return nc.dram_tensor(name, list(shape), dt, kind=kind).ap()

    x_d = dram("x", [SEQ, D_MODEL])
    ctx_d = dram("ctx", [CTX, D_MODEL])
    wada_d = dram("w_ada", [D_MODEL, 6 * D_MODEL])
    win_d = dram("w_in", [21, P, 8, P])
    wout_d = dram("w_out", [D_MODEL, D_MODEL])
    wup_d = dram("w_up", [NFC, P, 8, 2, P])
    wdown_d = dram("w_down", [D_FF, D_MODEL])
    vecs_d = dram("vecs", [96, P])
    convw_d = dram("convw", [36, P])
    dwcw_d = dram("dwcw", [198, P])
    ggdn_d = dram("g_gdn", [1, P])
    alog_d = dram("a_log", [8])
    dtb_d = dram("dt_bias", [8])
    wf_d = dram("w_fourier", [4, P, P])
    gconst_d = dram("gconst", [P, NG, P])
    cfsf_d = dram("cfsf", [P, 256])
    tab_d = dram("dft_tab", [4, 16, P, 2, 512], BF16)
    out_d = dram("out", [SEQ, D_MODEL], kind="ExternalOutput")

    PS = [Tile(nc.alloc_psum_tensor(f"ps{i}", [P, 512], F32), f"ps{i}", psum=True) for i in range(8)]

    def dump(name, view, shape, dt=F32):
        if name not in debug:
            return
        d = dram("dbg_" + name, shape, dt, kind="ExternalOutput")
        dbg_out[name] = (shape, dt)
        K.dma(d, view.a, R=[view], is_output=True)

    def finish():
        K.finish()
        return nc, dbg_out

    def mm(out, lhsT, rhs, start=True, stop=True):
        K.emit("pe", lambda e: e.matmul(out.a, lhsT=lhsT.a, rhs=rhs.a, start=start, stop=stop),
               W=[out], R=[lhsT, rhs])

    def tr(out, in_, ident):
        K.emit("pe", lambda e: e.transpose(out.a, in_.a, ident.a), W=[out], R=[in_, ident])

    def act(out, in_, func, scale=1.0, bias=None, accum=None):
        R = [in_]
        kw = {}
        if isinstance(scale, V):
            R.append(scale)
            kw["scale"] = scale.a
        else:
            kw["scale"] = float(scale)
        if bias is not None:
            R.append(bias)
            kw["bias"] = bias.a
        W = [out]
        if accum is not None:
            W.append(accum)
            kw["accum_out"] = accum.a
        K.emit("act", lambda e: e.activation(out.a, in_.a, func, **kw), W=W, R=R)

    def ts(eng, out, in0, s1, op0, s2=None, op1=None):
        R = [in0]
        a1 = s1.a if isinstance(s1, V) else float(s1)
        if isinstance(s1, V):
            R.append(s1)
        a2 = None
        if s2 is not None:
            a2 = s2.a if isinstance(s2, V) else float(s2)
            if isinstance(s2, V):
                R.append(s2)
        if op1 is None:
            K.emit(eng, lambda e: e.tensor_scalar(out.a, in0.a, a1, None, op0), W=[out], R=R)
        else:
            K.emit(eng, lambda e: e.tensor_scalar(out.a, in0.a, a1, a2, op0, op1), W=[out], R=R)

    def tt(eng, out, in0, in1, op):
        K.emit(eng, lambda e: e.tensor_tensor(out.a, in0.a, in1.a, op), W=[out], R=[in0, in1])

    def stt(out, in0, s, in1, op0, op1):
        R = [in0, in1]
        a = s.a if isinstance(s, V) else float(s)
        if isinstance(s, V):
            R.append(s)
        K.emit("dve", lambda e: e.scalar_tensor_tensor(out.a, in0.a, a, in1.a, op0, op1), W=[out], R=R)

    def cp(eng, out, in_):
        if eng == "act":
            act(out, in_, AF.Identity)
        else:
            K.emit(eng, lambda e: e.tensor_copy(out.a, in_.a), W=[out], R=[in_])

    def memset(eng, out, val):
        K.emit(eng, lambda e: e.memset(out.a, val), W=[out])

    def hview(view, pat, **kw):
        return V(view.tile, view.a.rearrange(pat, **kw))

    def bc_mid(view, n):
        sh = list(view.a.shape)
        return V(view.tile, view.a.unsqueeze(1).broadcast_to([sh[0], n, sh[1]]))

    def bc_last(view, n):
        sh = list(view.a.shape)
        return V(view.tile, view.a.unsqueeze(len(sh)).broadcast_to(sh + [n]))

    def ps4(ps):
        return hview(ps[:, :], "p (h d) -> p h d", d=P)

    GC = R0.tile([P, NG, P], F32, "gconst")
    K.dma(GC.ap[:], gconst_d, W=[GC[:]])

    def gcv(i):
        return GC[:, i, :]

    IDN = gcv(G_IDN)
    ONES = gcv(G_ONES)
    NEG1 = gcv(G_NEG1)
    idn_bf = R0.tile([P, P], BF16, "idn_bf")
    ones_bf = R0.tile([P, P], BF16, "ones_bf")
    cp("dve", idn_bf[:], IDN)
    cp("dve", ones_bf[:], ONES)
    cst = R0.tile([P, 8], F32, "cst")
    memset("dve", cst[:, 0:1], EPS)
    memset("dve", cst[:, 1:2], -0.5)
    memset("dve", cst[:, 2:3], 1.0)
    c_eps = cst[:, 0:1]
    c_mhalf = cst[:, 1:2]
    c_one = cst[:, 2:3]
    if stop_after == -3:
        dump("idn", idn_bf[:], [P, P], BF16)
        return finish()

    vec_fm = R0.tile([P, 96], F32, "vec_fm")
    cw_fm = R0.tile([P, 36], F32, "cw_fm")
    dw_fm = R0.tile([P, 198], F32, "dw_fm")
    gg_fm = R0.tile([P, 1], F32, "gg_fm")
    st_v = R1.tile([96, P], F32, "st_v")
    st_c = R1.tile([36, P], F32, "st_c")
    st_d1 = R1.tile([P, P], F32, "st_d1")
    st_d2 = R1.tile([70, P], F32, "st_d2")
    st_g = R1.tile([1, P], F32, "st_g")
    K.dma(st_v.ap[:], vecs_d, W=[st_v[:]])
    K.dma(st_c.ap[:], convw_d, W=[st_c[:]])
    K.dma(st_d1.ap[:], dwcw_d[0:128, :], W=[st_d1[:]])
    K.dma(st_d2.ap[:], dwcw_d[128:198, :], W=[st_d2[:]])
    K.dma(st_g.ap[:], ggdn_d, W=[st_g[:]])
    ps = PS[0]
    tr(ps[:, 0:96], st_v[:], GC[0:96, G_IDN, 0:96])
    cp("dve", vec_fm[:], ps[:, 0:96])
    ps = PS[1]
    tr(ps[:, 0:36], st_c[:], GC[0:36, G_IDN, 0:36])
    tr(ps[:, 36:37], st_g[:], GC[0:1, G_IDN, 0:1])
    cp("dve", cw_fm[:], ps[:, 0:36])
    cp("dve", gg_fm[:], ps[:, 36:37])
    ps = PS[2]
    tr(ps[:, 0:128], st_d1[:], IDN)
    tr(ps[:, 128:198], st_d2[:], GC[0:70, G_IDN, 0:70])
    cp("dve", dw_fm[:], ps[:, 0:198])
    if stop_after == -2:
        dump("dw_fm", dw_fm[:], [P, 198])
        return finish()

    ab_c = R0.tile([P, 16], F32, "ab_c")
    K.dma(ab_c.ap[:, 0:8], dtb_d.partition_broadcast(P), W=[ab_c[:]])
    K.dma(ab_c.ap[:, 8:16], alog_d.partition_broadcast(P), W=[ab_c[:]])
    act(ab_c[:, 8:16], ab_c[:, 8:16], AF.Exp)
    ts("dve", ab_c[:, 8:16], ab_c[:, 8:16], -1.0, ALU.mult)

    silu_c = R0.tile([P, 8, 2], F32, "silu_c")
    act(silu_c[:, :, 0], vec_fm[:, 80:88], AF.Silu)
    act(silu_c[:, :, 1], vec_fm[:, 88:96], AF.Silu)
    if stop_after == -1:
        dump("silu_c", silu_c[:], [P, 8, 2])
        dump("ab_c", ab_c[:], [P, 16])
        return finish()

    modA = R0.tile([P, 16, 2], F32, "modA")
    modB = R0.tile([P, 32, 2], F32, "modB")
    wada_v = wada_d.rearrange("(kc p) n -> p kc n", p=P)
    wa = [R5.tile([P, 8, 512], F32, f"wa{i}") for i in range(2)]
    sA1 = R0.tile([P, 8, 2], F32, "sA1")
    sA2 = R0.tile([P, 8], F32, "sA2")
    gtg = R0.tile([P, 16], F32, "gtg")

    def mod_groups(g0, g1, psm, jbase):
        for gidx in range(g0, g1):
            w = wa[gidx % 2]
            K.dma(w.ap[:], wada_v[:, :, gidx * 512:(gidx + 1) * 512], W=[w[:]])
            for cc in range(4):
                j = gidx * 4 + cc - jbase
                for kc in range(8):
                    mm(psm[:, 2 * j:2 * j + 2], w[:, kc, cc * 128:(cc + 1) * 128], silu_c[:, kc, :],
                       start=(kc == 0), stop=(kc == 7))
                if cc % 2 == 1:
                    yield

    for _ in mod_groups(0, 4, PS[3], 0):
        pass
    tt("dve", modA[:], hview(PS[3][:, 0:32], "p (j v) -> p j v", v=2), bc_last(vec_fm[:, 0:16], 2), ALU.add)
    stt(sA1[:], modA[:, 8:16, :], 1.0, bc_last(vec_fm[:, 48:56], 2), ALU.add, ALU.mult)

    def mod_rest():
        yield from mod_groups(4, 12, PS[2], 16)
        tt("dve", modB[:], hview(PS[2][:, 0:64], "p (j v) -> p j v", v=2), bc_last(vec_fm[:, 16:48], 2), ALU.add)
        stt(sA2[:], modB[:, 16:24, 0], 1.0, vec_fm[:, 64:72], ALU.add, ALU.mult)
        tt("dve", gtg[:, 0:8], modB[:, 0:8, 0], vec_fm[:, 56:64], ALU.mult)
        tt("dve", gtg[:, 8:16], modB[:, 24:32, 0], vec_fm[:, 72:80], ALU.mult)
        dump("mod", modB[:], [P, 32, 2])

    mod_rest_gen = mod_rest()
    if stop_after in (-0.5, 0):
        for _ in mod_rest_gen:
            pass
        return finish()

    def make_GT(j, GTt, dgs, psa, psb_):
        for half in range(2):
            psb = (psa, psb_)[half]
            for c4 in range(4):
                c = half * 4 + c4
                dgt = dgs[c % 2]
                ts("dve", dgt[:], IDN, gtg[:, j * 8 + c:j * 8 + c + 1], ALU.mult)
                mm(psb[:, c4 * 128:(c4 + 1) * 128], ONES, dgt[:])
            cp("act", GTt[:, half * 512:(half + 1) * 512], psb[:])

    R1.reset()
    if stop_after == 0:
        return finish()

    TOKBLK = [(0, 256)] + [(256 + 512 * b, 512) for b in range(4)]
    hT_raw = R1.alloc([P, 8, NTOK], BF16, "hT")
    hTb = [Tile(hT_raw[:, :, o:o + n], f"hT{b}") for b, (o, n) in enumerate(TOKBLK)]

    def hT_view(tok0, n):
        for b, (o, nn) in enumerate(TOKBLK):
            if o <= tok0 and tok0 + n <= o + nn:
                return hTb[b], tok0 - o
        raise AssertionError

    ssq = R0.tile([P, NT], F32, "ssq")
    rstd = R0.tile([P, NT], F32, "rstd")
    xts = [R1.tile([P, D_MODEL], F32, f"xt{i}") for i in range(3)]
    xns = [R1.tile([P, D_MODEL], BF16, f"xn{i}") for i in range(2)]
    junk = R1.tile([P, D_MODEL], BF16, "junk")

    def norm_tile(xt_v, xn, ssq_col, rstd_col, junk_t):
        act(junk_t[:], xt_v, AF.Square, accum=ssq_col)
        ts("dve", rstd_col, ssq_col, 1.0 / D_MODEL, ALU.mult, EPS, ALU.add)
        tt("pool", rstd_col, rstd_col, c_mhalf, ALU.pow)
        ts("dve", xn[:], xt_v, rstd_col, ALU.mult)

    def transpose_mod(xn, pst, dst_fn, sc_fn, bi_fn, use_act=True):
        pst_bf = V(pst, pst.ap[:].bitcast(BF16))
        for kc in range(8):
            tr(pst_bf[:, kc * 128:(kc + 1) * 128], xn[:, kc * 128:(kc + 1) * 128], idn_bf[:])
        for kc in range(8):
            if use_act:
                act(dst_fn(kc), pst_bf[:, kc * 128:(kc + 1) * 128], AF.Identity, scale=sc_fn(kc), bias=bi_fn(kc))
            else:
                ts("dve", dst_fn(kc), pst_bf[:, kc * 128:(kc + 1) * 128], sc_fn(kc), ALU.mult, bi_fn(kc), ALU.add)

    def p1_stageA(ti):
        xt = xts[ti % 3]
        src = ctx_d[ti * P:(ti + 1) * P, :] if ti < 2 else x_d[(ti - 2) * P:(ti - 1) * P, :]
        K.dma(xt.ap[:], src, W=[xt[:]])
        norm_tile(xt[:], xns[ti % 2], ssq[:, ti:ti + 1], rstd[:, ti:ti + 1], junk)

    def p1_stageB(ti):
        vsel = 1 if ti < 2 else 0
        hb, off = hT_view(ti * P, P)
        transpose_mod(xns[ti % 2], PS[ti % 2],
                      lambda kc: hb[:, kc, off:off + P],
                      lambda kc: sA1[:, kc, vsel:vsel + 1],
                      lambda kc: modA[:, kc, vsel:vsel + 1], use_act=(ti % 2 == 0))

    p1_stageA(0)
    for ti in range(NT):
        if ti + 1 < NT:
            p1_stageA(ti + 1)
        p1_stageB(ti)
        next(mod_rest_gen, None)
    for _ in mod_rest_gen:
        pass
    dump("hT", hTb[1][:], [P, 8, 512], BF16)
    if stop_after == 1:
        return finish()

    R5.reset(barrier=False)
    uT_raw = R3.alloc([P, 4, SEQ], BF16, "uT")
    szT_raw = R4.alloc([P, 4, SEQ], BF16, "szT")
    qT_raw = R2.alloc([P, 4, SEQ], BF16, "qT")
    kT_raw = R5.alloc([P, 4, NTOK], BF16, "kT")
    ktok_raw = R5.alloc([P, NT, 4, P], BF16, "ktok")
    vtok_raw = R5.alloc([P, NT, 4, P], BF16, "vtok")
    uT = [Tile(uT_raw[:, g, :], f"uT{g}") for g in range(4)]
    szT = Tile(szT_raw, "szT")
    qT = [Tile(qT_raw[:, g, :], f"qT{g}") for g in range(4)]
    kT = [Tile(kT_raw[:, g, :], f"kT{g}") for g in range(4)]
    ktok = Tile(ktok_raw, "ktok")
    vtok = Tile(vtok_raw, "vtok")
    abtok = R0.tile([P, NT, 16], F32, "abtok")

    K.barrier()
    R1.cur = R1.start + 8 * NTOK * 2
    wbf = [R6.tile([P, 8, P], BF16, f"wbf{i}") for i in range(3)]
    PADW = NTOK + 4
    pads = [R6.tile([P, PADW], BF16, f"pad{i}") for i in range(2)]
    dgc = [R1.tile([P, 3, P], BF16, f"dgc{i}") for i in range(2)]
    sil = R1.tile([P, NTOK], F32, "sil")
    sqs = [R1.tile([P, 512], BF16, f"sq{i}") for i in range(2)]
    lnt = R1.tile([P, NTOK], F32, "lnt")
    vT_tmp = V(sil, sil.ap[:].bitcast(BF16)[:, 0:NTOK])
    for pd in pads:
        memset("pool", pd[:], 0.0)

    def padcol(tok):
        return 1 + tok if tok < CTX else 3 + tok

    NCH = 21

    def load_w(j):
        w = wbf[j % 3]
        K.dma(w.ap[:], win_d[j], W=[w[:]], queue="pool")

    load_w(0)
    load_w(1)
    state = {"mmrr": 0, "evrr": 0}

    def kind_of(j):
        return "uqkvz"[j // 4] if j < 20 else "ab"

    def main_part(j):
        if j + 2 < NCH:
            load_w(j + 2)
        wb = wbf[j % 3]
        kind = kind_of(j)
        g = j % 4
        if kind == "ab":
            psab = PS[7]
            for t in range(NT):
                hb, off = hT_view(t * P, P)
                for kc in range(8):
                    mm(psab[:, t * 16:(t + 1) * 16], hb[:, kc, off:off + P], wb[:, kc, 0:16],
                       start=(kc == 0), stop=(kc == 7))
            cp("dve", abtok[:], hview(psab[:, 0:NT * 16], "p (t c) -> p t c", c=16))
            return
        blocks = list(range(5)) if kind in "kv" else list(range(1, 5))
        pad = pads[j % 2]
        if kind in "qkv":
            dg = dgc[j % 2]
            cj = j - 4
            for tap in range(3):
                ts("dve", dg[:, tap, :], IDN, cw_fm[:, tap * 12 + cj:tap * 12 + cj + 1], ALU.mult)
        yield
        for b_ in blocks:
            o, n = TOKBLK[b_]
            psa = PS[state["mmrr"] % 5]
            state["mmrr"] += 1
            for kc in range(8):
                mm(psa[:, 0:n], wb[:, kc, :], hTb[b_][:, kc, :], start=(kc == 0), stop=(kc == 7))
            if kind == "u":
                cp("dve", uT[g][:, o - CTX:o - CTX + n], psa[:, 0:n])
            elif kind == "z":
                act(szT[:, g, o - CTX:o - CTX + n], psa[:, 0:n], AF.Silu)
            else:
                pc = padcol(o)
                cp("act" if state["evrr"] % 2 == 0 else "dve", pad[:, pc:pc + n], psa[:, 0:n])
                state["evrr"] += 1
            yield

    def post_a(j):
        kind = kind_of(j)
        if kind not in "qkv":
            return
        blocks = list(range(5)) if kind in "kv" else list(range(1, 5))
        pad = pads[j % 2]
        dg = dgc[j % 2]
        for b_ in blocks:
            o, n = TOKBLK[b_]
            psc = PS[5 + (b_ % 2)]
            pc = padcol(o)
            for tap in range(3):
                mm(psc[:, 0:n], dg[:, tap, :], pad[:, pc + tap - 1:pc + tap - 1 + n],
                   start=(tap == 0), stop=(tap == 2))
            if kind == "v":
                act(vT_tmp[:, o:o + n], psc[:, 0:n], AF.Silu)
            else:
                act(sil[:, o:o + n], psc[:, 0:n], AF.Silu)
            yield
        if kind == "v":
            return
        for b_ in blocks:
            o, n = TOKBLK[b_]
            sq = sqs[b_ % 2]
            tt("dve", sq[:, 0:n], sil[:, o:o + n], sil[:, o:o + n], ALU.mult)
            psn = PS[7]
            mm(psn[:, 0:n], ones_bf[:], sq[:, 0:n])
            act(lnt[:, o:o + n], psn[:, 0:n], AF.Ln, bias=c_eps)
            yield

    def post_b(j):
        kind = kind_of(j)
        if kind not in "qkv":
            return
        g = j % 4
        blocks = list(range(5)) if kind in "kv" else list(range(1, 5))
        if kind == "v":
            src = vT_tmp
            dst = vtok
        else:
            scale = float(P) ** -0.5 if kind == "q" else 1.0
            for b_ in blocks:
                o, n = TOKBLK[b_]
                act(lnt[:, o:o + n], lnt[:, o:o + n], AF.Exp, scale=-0.5)
                if kind == "q":
                    stt(qT[g][:, o - CTX:o - CTX + n], sil[:, o:o + n], scale, lnt[:, o:o + n],
                        ALU.mult, ALU.mult)
                else:
                    stt(kT[g][:, o:o + n], sil[:, o:o + n], scale, lnt[:, o:o + n],
                        ALU.mult, ALU.mult)
                yield
            if kind == "q":
                return
            src = kT[g]
            dst = ktok
        for t0 in (0, 8, 16):
            nt = min(8, NT - t0)
            pst = PS[5 + ((t0 // 8) % 2)]
            pst_bf = V(pst, pst.ap[:].bitcast(BF16))
            for i in range(nt):
                t = t0 + i
                tr(pst_bf[:, i * P:(i + 1) * P], src[:, t * P:(t + 1) * P], idn_bf[:])
            cp("dve", dst[:, t0:t0 + nt, g, :], hview(pst_bf[:, 0:nt * P], "p (t d) -> p t d", d=P))
            yield

    def gen_or_empty(fn, *a):
        r_ = fn(*a)
        return r_ if r_ is not None else iter(())

    def chain2(*gens):
        for g_ in gens:
            yield from g_

    def rr_weighted(threads):
        live = [[g_, w_] for g_, w_ in threads]
        while live:
            for ent in list(live):
                for _ in range(ent[1]):
                    try:
                        next(ent[0])
                    except StopIteration:
                        live.remove(ent)
                        break

    for _ in gen_or_empty(main_part, 0):
        pass
    for j in range(NCH):
        threads = [(chain2(gen_or_empty(post_a, j), gen_or_empty(post_b, j)), 100000)]
        if j + 1 < NCH:
            threads.insert(0, (gen_or_empty(main_part, j + 1), 100000))
        rr_weighted(threads)
    dump("uT", V(uT[0], uT_raw[:, :, 0:512]), [P, 4, 512], BF16)
    dump("szT", szT[:, :, 0:512], [P, 4, 512], BF16)
    dump("qT", V(qT[0], qT_raw[:, :, 0:512]), [P, 4, 512], BF16)
    dump("kT", V(kT[0], kT_raw[:, :, 0:768]), [P, 4, 768], BF16)
    dump("ktok", ktok[:, 0:4, :, :], [P, 4, 4, P], BF16)
    dump("vtok", vtok[:, 0:4, :, :], [P, 4, 4, P], BF16)
    dump("abtok", abtok[:], [P, NT, 16])
    R1.reset()
    R6.reset(barrier=False)
    if stop_after == 2:
        return finish()

    def sc8(name):
        return R6.tile([P, NT, 8], F32, name)

    beta = sc8("beta")
    gg = sc8("g")
    Gs = sc8("Gs")
    Gam = sc8("Gam")
    epsl = sc8("epsl")
    nGb = sc8("nGb")
    xs_t = sc8("xs")
    ax_t = sc8("ax")
    egl = R6.tile([P, NT, 2, 8], F32, "egl")
    act(beta[:], abtok[:, :, 0:8], AF.Sigmoid)
    tt("dve", xs_t[:], abtok[:, :, 8:16], bc_mid(ab_c[:, 0:8], NT), ALU.add)
    stt(ax_t[:], xs_t[:], -1.0, xs_t[:], ALU.mult, ALU.max)
    act(ax_t[:], ax_t[:], AF.Exp, scale=-1.0)
    act(ax_t[:], ax_t[:], AF.Ln, bias=c_one)
    stt(xs_t[:], xs_t[:], 0.0, ax_t[:], ALU.max, ALU.add)
    tt("dve", gg[:], xs_t[:], bc_mid(ab_c[:, 8:16], NT), ALU.mult)
    psG, psGL, psE = PS[0], PS[1], PS[2]
    for t in range(NT):
        for d in range(2):
            mm(psG[:, t * 8 + d * 4:t * 8 + d * 4 + 4], gcv(G_MF if d == 0 else G_MB), gg[:, t, d * 4:d * 4 + 4])
        mm(psGL[:, t * 8:(t + 1) * 8], gcv(G_CB), gg[:, t, :])
        for c in range(2):
            mm(psE[:, (t * 2 + c) * 8:(t * 2 + c + 1) * 8], gcv(G_IND0 if c == 0 else G_IND1), gg[:, t, :])
    cp("act", Gs[:], hview(psG[:, 0:NT * 8], "p (t c) -> p t c", c=8))
    act(Gam[:], Gs[:], AF.Exp)
    tt("dve", epsl[:], hview(psGL[:, 0:NT * 8], "p (t c) -> p t c", c=8), Gs[:], ALU.subtract)
    act(epsl[:], epsl[:], AF.Exp)
    ts("dve", nGb[:], Gam[:], -1.0, ALU.mult)
    act(egl[:], hview(psE[:, 0:NT * 16], "p (t c e) -> p t c e", c=2, e=8), AF.Exp)
    dump("g", gg[:], [P, NT, 8])
    dump("beta", beta[:], [P, NT, 8])
    dump("Gs", Gs[:], [P, NT, 8])
    dump("epsl", epsl[:], [P, NT, 8])
    dump("egl", egl[:], [P, NT, 2, 8])
    if stop_after == 2.5:
        return finish()

    F32R = mybir.dt.float32r
    cr = R6.tile([P, 2, P], F32, "ones_neg1_r")
    cp("dve", V(cr, cr.ap[:, 0, :].bitcast(F32R)), ONES)
    cp("dve", V(cr, cr.ap[:, 1, :].bitcast(F32R)), NEG1)
    ONES_r = cr[:, 0, :]
    NEG1_r = cr[:, 1, :]

    def rr(view):
        return V(view.tile, view.a.bitcast(F32R))

    def f4(name, reg=R1):
        return reg.tile([P, 4, P], F32, name)

    Ttmp = [dict(tA=f4(f"tA{d}"), Qa=f4(f"Qa{d}"), QTa=f4(f"QTa{d}"), Qb=f4(f"Qb{d}"), QTb=f4(f"QTb{d}"))
            for d in range(2)]
    TTs = [[f4(f"TT{d}{i}") for i in range(2)] for d in range(2)]
    attnTs = [[R6.tile([P, 4, P], BF16, f"attnT{d}{i}") for i in range(2)] for d in range(2)]
    qdecs = [[R6.tile([P, 4, P], BF16, f"qdec{d}{i}") for i in range(2)] for d in range(2)]
    rhs2s = [f4(f"rhs2_{d}") for d in range(2)]
    OF_raw = R1.alloc([P, NLT, 4, P], F32, "OF")
    OF = [Tile(OF_raw[:, l], f"OF{l}") for l in range(NLT)]
    vns = [R0.tile([P, 4, P], BF16, f"vn{d}") for d in range(2)]
    vdcs = [[(R0 if (d, c) != (1, 1) else R6).tile([P, 4, P], BF16, f"vd{d}{c}") for c in range(2)] for d in range(2)]
    S = [f4(f"S{d}", R6) for d in range(2)]
    Sb = [R6.tile([P, 4, P], BF16, f"Sb{d}") for d in range(2)]
    for d in range(2):
        memset("pool", S[d][:], 0.0)
        memset("pool", Sb[d][:], 0.0)
        memset("pool", vns[d][:], 0.0)
        memset("pool", Ttmp[d]["tA"][:], 0.0)
        cp("dve", rr(rhs2s[d][:]), Ttmp[d]["tA"][:])
        for c in range(2):
            memset("pool", vdcs[d][c][:], 0.0)
    TPSd = [[PS[0], PS[1], PS[2]], [PS[3], PS[4], PS[5]]]
    SPS = [PS[6], PS[7]]

    def t_phase(t, d, slot):
        lat = t >= 2
        l = t - 2
        Mdir = gcv(G_MF if d == 0 else G_MB)
        NMA = gcv(G_NM_SL if d == 0 else G_NM_SU)
        PMT = gcv(G_PM_UI if d == 0 else G_PM_LI)
        gcol = gg[:, t, d * 4:d * 4 + 4]
        bcol = beta[:, t, d * 4:d * 4 + 4]
        tk = slice(t * P, (t + 1) * P)
        TT = TTs[d][slot]
        aT = attnTs[d][slot]
        qd = qdecs[d][slot]
        tm = Ttmp[d]
        tA, Qa, QTa, Qb_, QTb_ = tm["tA"], tm["Qa"], tm["QTa"], tm["Qb"], tm["QTb"]
        dA, dT_ = QTb_, Qb_
        p0, p1, p2 = TPSd[d]
        tt("dve", rr(Qa[:]), bc_mid(Mdir, 4), bc_last(gcol, P), ALU.mult)
        for h in range(4):
            mm(p0[:, h * P:(h + 1) * P], rr(Qa[:, h, :]), rr(ONES_r), start=True, stop=False)
            mm(p0[:, h * P:(h + 1) * P], rr(NEG1_r), rr(Qa[:, h, :]), start=False, stop=True)
        for h in range(4):
            mm(p1[:, h * P:(h + 1) * P], kT[h][:, tk], kT[h][:, tk])
        yield
        stt(rr(dA[:]), ps4(p0), 0.0, bc_mid(NMA, 4), ALU.min, ALU.add)
        act(rr(dA[:]), dA[:], AF.Exp)
        if lat:
            stt(rr(dT_[:]), ps4(p0), 0.0, bc_mid(PMT, 4), ALU.max, ALU.add)
            act(rr(dT_[:]), dT_[:], AF.Exp, scale=-1.0)
        yield
        tt("dve", tA[:], ps4(p1), dA[:], ALU.mult)
        tt("dve", rr(Qa[:]), tA[:], bc_last(bcol, P), ALU.mult)
        for h in range(4):
            tr(p0[:, h * P:(h + 1) * P], Qa[:, h, :], IDN)
        if lat:
            for h in range(4):
                mm(p1[:, h * P:(h + 1) * P], kT[h][:, tk], qT[h][:, l * P:(l + 1) * P])
        yield
        cp("act", rr(QTa[:]), ps4(p0))
        stt(rr(TT[:]), QTa[:], -1.0, bc_mid(IDN, 4), ALU.mult, ALU.add)
        if lat:
            tt("dve", aT[:], ps4(p1), dT_[:], ALU.mult)
            tt("dve", rr(QTb_[:]), bc_mid(IDN, 4), bc_last(Gam[:, t, d * 4:d * 4 + 4], P), ALU.mult)
        yield
        for h in range(4):
            mm(p0[:, h * P:(h + 1) * P], rr(QTa[:, h, :]), rr(Qa[:, h, :]))
        for h in range(4):
            mm(p1[:, h * P:(h + 1) * P], rr(Qa[:, h, :]), rr(QTa[:, h, :]))
        if lat:
            for h in range(4):
                mm(p2[:, h * P:(h + 1) * P], rr(ONES_r), rr(QTb_[:, h, :]))
        yield
        cp("act", rr(Qb_[:]), ps4(p0))
        cp("act", rr(QTb_[:]), ps4(p1))
        if lat:
            tt("dve", qd[:], ps4(p2), V(qT[0], qT_raw[:, :, l * P:(l + 1) * P]), ALU.mult)
        yield
        Q, QT = Qb_, QTb_
        for k in range(1, 6):
            Qn, QTn = (Qa, QTa) if k % 2 == 1 else (Qb_, QTb_)
            if k < 5:
                for h in range(4):
                    mm(p0[:, h * P:(h + 1) * P], rr(QT[:, h, :]), rr(Q[:, h, :]))
            if k < 4:
                for h in range(4):
                    mm(p1[:, h * P:(h + 1) * P], rr(Q[:, h, :]), rr(QT[:, h, :]))
            for h in range(4):
                mm(p2[:, h * P:(h + 1) * P], rr(Q[:, h, :]), rr(TT[:, h, :]))
            yield
            if k < 5:
                cp("act", rr(Qn[:]), ps4(p0))
            if k < 4:
                cp("act", rr(QTn[:]), ps4(p1))
            tt("dve", rr(TT[:]), ps4(p2), TT[:], ALU.add)
            yield
            Q, QT = Qn, QTn
        tt("dve", rr(TT[:]), TT[:], bc_last(bcol, P), ALU.mult)
        yield

    def scan(t, d, slot):
        lat = t >= 2
        l = t - 2
        psX = SPS[d]
        psS = psX
        TT = TTs[d][slot]
        aT = attnTs[d][slot]
        qd = qdecs[d][slot]
        rhs2, vn, vdc = rhs2s[d], vns[d], vdcs[d]
        tk = slice(t * P, (t + 1) * P)
        for c in ((0, 1) if d == 0 else (1, 0)):
            r0 = c * 64
            rows = slice(r0, r0 + 64)
            for h in range(4):
                mm(psX[:, h * P:(h + 1) * P], kT[h][:, tk], Sb[d][:, h, :])
            yield
            for h in range(4):
                stt(rr(rhs2[rows, h, :]), psX[rows, h * P:(h + 1) * P], nGb[rows, t, d * 4 + h:d * 4 + h + 1],
                    vtok[rows, t, h, :], ALU.mult, ALU.add)
            yield
            for h in range(4):
                mm(psX[:, h * P:(h + 1) * P], rr(TT[:, h, :]), rr(rhs2[:, h, :]))
            yield
            psX4 = hview(psX[rows, :], "p (h d) -> p h d", d=P)
            cp("act", vn[rows], psX4)
            for h in range(4):
                act(vdc[c][rows, h, :], psX[rows, h * P:(h + 1) * P], AF.Identity,
                    scale=epsl[rows, t, d * 4 + h:d * 4 + h + 1])
            yield
            if lat:
                for h in range(4):
                    mm(psX[:, h * P:(h + 1) * P], qd[:, h, :], Sb[d][:, h, :], start=True, stop=False)
                    mm(psX[:, h * P:(h + 1) * P], aT[:, h, :], vn[:, h, :], start=False, stop=True)
                yield
                if (d == 0) != (l < NLT // 2):
                    tt("dve", OF[l][rows], psX4, OF[l][rows], ALU.add)
                else:
                    cp("act", OF[l][rows], psX4)
            for h in range(4):
                mm(psS[:, h * P:(h + 1) * P], ktok[:, t, h, :], vdc[c][:, h, :])
            yield
            for h in range(4):
                stt(S[d][:, h, :], S[d][:, h, :], egl[:, t, c, d * 4 + h:d * 4 + h + 1],
                    psS[:, h * P:(h + 1) * P], ALU.mult, ALU.add)
            cp("act", Sb[d][:], S[d][:])
            yield

    orders = [list(range(NT)), [1, 0] + list(range(NT - 1, 1, -1))]
    nsteps = NT
    if "GDN_TILES" in os.environ:
        nsteps = int(os.environ["GDN_TILES"])
        orders[1] = [1, 0] + [None] * NT

    def chain(*gens):
        for g_ in gens:
            yield from g_

    def run_weighted(threads):
        live = [[g_, w_] for g_, w_ in threads]
        while live:
            for ent in list(live):
                for _ in range(ent[1]):
                    try:
                        next(ent[0])
                    except StopIteration:
                        live.remove(ent)
                        break

    def tgen(i, d):
        if i < nsteps and orders[d][i] is not None:
            return t_phase(orders[d][i], d, i % 2)
        return iter(())

    run_weighted([(tgen(0, 0), 1), (tgen(0, 1), 1)])
    TW = int(os.environ.get("GDN_TW", "1"))
    SW = int(os.environ.get("GDN_SW", "1"))
    for i in range(nsteps):
        threads = []
        for d in range(2):
            if orders[d][i] is not None:
                threads.append((scan(orders[d][i], d, i % 2), SW))
            threads.append((tgen(i + 1, d), TW))
        run_weighted(threads)
        if i == 1:
            dump("S_ctx", S[0][:], [P, 4, P])
        if i == 2:
            dump("TT", TTs[0][0][:], [P, 4, P])
    dump("OF", V(OF[0], OF_raw[:, 0:4]), [P, 4, 4, P])
    if stop_after == 3:
        return finish()

    K.barrier()
    R6.reset(barrier=False)
    R5.reset(barrier=False)
    R1.cur = R1.start
    ydT_raw = qT_raw
    ydT = Tile(ydT_raw, "ydT")
    ssq4 = R6.tile([P, NLT, 4], F32, "ssq4")
    junk4 = R6.tile([P, P], BF16, "junk4")
    on_bf = [R6.tile([P, 4, P], BF16, f"on{i}") for i in range(2)]
    Vsb = R5.tile([P, NLT, 4, 256], BF16, "Vsb")
    yfT_raw = R5.alloc([P, 4, SEQ], BF16, "yfT")
    yfT = [Tile(yfT_raw[:, g, :], f"yfT{g}") for g in range(4)]
    cfsf = R6.tile([P, 256], F32, "cfsf")
    wf_sb = R6.tile([P, 4, P], F32, "wf_sb")
    CW = R6.tile([P, 4, 256], BF16, "CW")
    tabs = [R1.tile([P, 4, 2, 512], BF16, f"tab{i}") for i in range(3)]

    def yd_gen():
        for l in range(NLT):
            for h in range(4):
                act(junk4[:], OF[l][:, h, :], AF.Square, accum=ssq4[:, l, h:h + 1])
            if l % 4 == 3:
                yield
        ts("dve", ssq4[:], ssq4[:], 1.0 / P, ALU.mult, EPS, ALU.add)
        tt("pool", ssq4[:], ssq4[:], V(cst, cst.ap[:, 1:2].unsqueeze(2).broadcast_to([P, NLT, 4])), ALU.pow)
        yield
        for l in range(NLT):
            on = on_bf[l % 2]
            for h in range(4):
                ts("dve", on[:, h, :], OF[l][:, h, :], ssq4[:, l, h:h + 1], ALU.mult)
            pst = PS[l % 2]
            pst_bf = V(pst, pst.ap[:].bitcast(BF16))
            for h in range(4):
                tr(pst_bf[:, h * P:(h + 1) * P], on[:, h, :], idn_bf[:])
            yield
            stt(ydT[:, :, l * P:(l + 1) * P], hview(pst_bf[:, 0:4 * P], "p (h d) -> p h d", d=P), gg_fm[:, 0:1],
                szT[:, :, l * P:(l + 1) * P], ALU.mult, ALU.mult)
            yield

    def load_tab(idx):
        mb, l4 = divmod(idx, 4)
        tb = tabs[idx % 3]
        K.dma(tb.ap[:], tab_d[mb, l4 * 4:(l4 + 1) * 4].rearrange("t p c m -> p t c m"), W=[tb[:]])

    def fourier_gen():
        K.dma(cfsf.ap[:], cfsf_d, W=[cfsf[:]])
        K.dma(wf_sb.ap[:], wf_d.rearrange("g c d -> c g d"), W=[wf_sb[:]])
        load_tab(0)
        load_tab(1)
        for g in range(4):
            psw = PS[2 + g % 2]
            mm(psw[:, 0:P], cfsf[:, 0:P], wf_sb[:, g, :])
            mm(psw[:, P:2 * P], cfsf[:, P:2 * P], wf_sb[:, g, :])
            cp("dve", CW[:, g, :], psw[:, 0:256])
        yield
        for lt in range(NLT):
            for gp in range(2):
                psv = PS[2 + (lt * 2 + gp) % 2]
                for gi in range(2):
                    g = gp * 2 + gi
                    mm(psv[:, gi * 256:(gi + 1) * 256], uT[g][:, lt * P:(lt + 1) * P], CW[:, g, :])
                cp("act" if gp == 0 else "dve", Vsb[:, lt, gp * 2:gp * 2 + 2, :],
                   hview(psv[:, :], "p (g c) -> p g c", c=256))
            if lt % 2 == 1:
                yield
        for mb in range(4):
            pss = [PS[4 + g] for g in range(4)]
            for l4 in range(4):
                idx = mb * 4 + l4
                if idx + 2 < 16:
                    load_tab(idx + 2)
                tb = tabs[idx % 3]
                for li in range(4):
                    lt = l4 * 4 + li
                    for g in range(4):
                        mm(pss[g][:, :], Vsb[:, lt, g, 0:P], tb[:, li, 0, :], start=(lt == 0), stop=False)
                        mm(pss[g][:, :], Vsb[:, lt, g, P:2 * P], tb[:, li, 1, :], start=False,
                           stop=(lt == NLT - 1))
                    yield
            for g in range(4):
                cp("act" if g % 2 == 0 else "dve", yfT[g][:, mb * 512:(mb + 1) * 512], pss[g][:, :])
            yield

    if stop_after == 4:
        for _ in yd_gen():
            pass
        dump("ydT", ydT[:, :, 0:512], [P, 4, 512], BF16)
        return finish()
    run_weighted([(yd_gen(), 1), (fourier_gen(), 2)])
    dump("ydT", ydT[:, :, 0:512], [P, 4, 512], BF16)
    dump("yfT", V(yfT[0], yfT_raw[:, :, 0:512]), [P, 4, 512], BF16)
    R1.reset()
    R6.reset(barrier=False)
    if stop_after == 5:
        return finish()

    R3.reset(barrier=False)
    R4.reset(barrier=False)
    X1_raw = R1.alloc([P, NLT, D_MODEL], F32, "X1")
    X1 = [Tile(X1_raw[:, l, :], f"X1_{l}") for l in range(NLT)]
    wo_raw = R3.alloc([P, 8, D_MODEL], BF16, "wo_bf")
    wo_bf = [Tile(wo_raw[:, kc, :], f"wo{kc}") for kc in range(8)]
    GT1 = R4.tile([P, D_MODEL], F32, "GT1")
    xr = [R4.tile([P, D_MODEL], F32, f"xr{i}") for i in range(2)]
    tmp5 = R4.tile([P, D_MODEL], F32, "tmp5")
    junk5 = R6.tile([P, D_MODEL], BF16, "junk5")
    dgs5 = [R6.tile([P, P], F32, f"dgs5_{i}") for i in range(2)]
    ssq5 = R6.tile([P, NLT], F32, "ssq5")
    wout_v = wout_d.rearrange("(kc p) n -> p kc n", p=P)
    for kc in range(8):
        K.dma(wo_bf[kc].ap, wout_v[:, kc, :], W=[wo_bf[kc][:]], queue="pool")
    make_GT(0, GT1, dgs5, PS[4], PS[5])
    mixT = [yfT[g] for g in range(4)] + [V(ydT, ydT_raw[:, h, :]) for h in range(4)]

    ssq5b = R6.tile([P, NLT], F32, "ssq5b")
    for l in range(NLT):
        K.dma(xr[l % 2].ap[:], x_d[l * P:(l + 1) * P, :], W=[xr[l % 2][:]])
        psA, psB = PS[(l % 4) * 2], PS[(l % 4) * 2 + 1]
        for half, psx in enumerate((psA, psB)):
            for kc in range(8):
                mm(psx[:, :], mixT[kc][:, l * P:(l + 1) * P], wo_bf[kc][:, half * 512:(half + 1) * 512],
                   start=(kc == 0), stop=(kc == 7))
        act(junk5[:, 0:512], psA[:, :], AF.Square, accum=ssq5[:, l:l + 1])
        act(junk5[:, 512:1024], psB[:, :], AF.Square, accum=ssq5b[:, l:l + 1])
        tt("dve", ssq5[:, l:l + 1], ssq5[:, l:l + 1], ssq5b[:, l:l + 1], ALU.add)
        ts("dve", ssq5[:, l:l + 1], ssq5[:, l:l + 1], 1.0 / D_MODEL, ALU.mult, EPS, ALU.add)
        tt("pool", ssq5[:, l:l + 1], ssq5[:, l:l + 1], c_mhalf, ALU.pow)
        stt(tmp5[:, 0:512], psA[:, :], ssq5[:, l:l + 1], GT1[:, 0:512], ALU.mult, ALU.mult)
        stt(tmp5[:, 512:1024], psB[:, :], ssq5[:, l:l + 1], GT1[:, 512:1024], ALU.mult, ALU.mult)
        tt("pool", X1[l][:], tmp5[:], xr[l % 2][:], ALU.add)
    dump("X1", V(X1[0], X1_raw[:, 0:4, :]), [P, 4, D_MODEL])
    R2.reset()
    R3.reset(barrier=False)
    R4.reset(barrier=False)
    R5.reset(barrier=False)
    R6.reset(barrier=False)
    if stop_after == 6:
        return finish()

    FF = Arena(nc, K, base + 80 * KB, avail - 80 * KB, "FF")
    actT_raw = FF.alloc([P, NFC, 1024], BF16, "actT")
    actT = [Tile(actT_raw[:, c, :], f"actT{c}") for c in range(NFC)]
    GT2 = FF.tile([P, D_MODEL], F32, "GT2")
    ssq6 = FF.tile([P, 32], F32, "ssq6")
    ssq6b = FF.tile([P, 32], F32, "ssq6b")
    dgs6 = [FF.tile([P, P], F32, f"dgs6_{i}") for i in range(2)]
    make_GT(1, GT2, dgs6, PS[4], PS[5])
    NWA = 11
    wdA_raw = FF.alloc([P, NWA, D_MODEL], BF16, "wdA")
    wdA = [Tile(wdA_raw[:, kc, :], f"wdA{kc}") for kc in range(NWA)]
    ff_mark = FF.cur
    NLOC = 9 * P
    for hf in range(2):
        FF.cur = ff_mark
        h2T = FF.tile([P, 8, NLOC], BF16, "h2T")
        wbf6 = [FF.tile([P, 8, 2, P], BF16, f"wbf6_{i}") for i in range(3)]
        PR, PC = 18, 66
        gpad = [FF.tile([P, PR, PC], BF16, f"gpad{i}") for i in range(2)]
        dg9 = [FF.tile([P, 9, P], BF16, f"dg9_{i}") for i in range(2)]
        sg = [FF.tile([P, 512], F32, f"sg{i}") for i in range(2)]
        xn6 = [FF.tile([P, D_MODEL], BF16, f"xn6_{i}") for i in range(2)]
        junk6 = FF.tile([P, D_MODEL], BF16, "junk6")
        for gp_ in gpad:
            memset("pool", gp_[:], 0.0)
        tile0 = 0 if hf == 0 else 7
        def f_stageA(i):
            l = tile0 + i
            col = hf * 16 + i
            norm_tile(X1[l][:], xn6[i % 2], ssq6[:, col:col + 1], ssq6b[:, col:col + 1], junk6)

        def f_stageB(i):
            transpose_mod(xn6[i % 2], PS[i % 2],
                          lambda kc: h2T[:, kc, i * P:(i + 1) * P],
                          lambda kc: sA2[:, kc:kc + 1],
                          lambda kc: modB[:, 8 + kc, 0:1], use_act=(i % 2 == 0))

        f_stageA(0)
        for i in range(9):
            if i + 1 < 9:
                f_stageA(i + 1)
            f_stageB(i)
        main0 = 0 if hf == 0 else 128
        halo0 = 1024 if hf == 0 else 64
        halo_row = 17 if hf == 0 else 0

        def load_wup(cc):
            w = wbf6[cc % 3]
            K.dma(w.ap[:], wup_d[cc], W=[w[:]], queue="pool")

        load_wup(0)
        load_wup(1)
        if hf == 0:
            for kc in range(NWA):
                K.dma(wdA[kc].ap, wdown_d[kc * P:(kc + 1) * P, :], W=[wdA[kc][:]], queue="pool")
        for cc in range(NFC):
            if cc + 2 < NFC:
                load_wup(cc + 2)
            wb = wbf6[cc % 3]
            dg = dg9[cc % 2]
            gp_ = gpad[cc % 2]
            for tap in range(9):
                ts("dve", dg[:, tap, :], IDN, dw_fm[:, tap * NFC + cc:tap * NFC + cc + 1], ALU.mult)
            psgs = [PS[2], PS[3]]
            for piece in range(2):
                psg = psgs[piece]
                t0 = main0 + piece * 512
                for kc in range(8):
                    mm(psg[:, :], wb[:, kc, 1, :], h2T[:, kc, t0:t0 + 512], start=(kc == 0), stop=(kc == 7))
                cp("act", gp_[:, 1 + piece * 8:9 + piece * 8, 1:65], hview(psg[:, :], "p (r c) -> p r c", c=64))
            psh = PS[4]
            for kc in range(8):
                mm(psh[:, 0:64], wb[:, kc, 1, :], h2T[:, kc, halo0:halo0 + 64], start=(kc == 0), stop=(kc == 7))
            cp("act", gp_[:, halo_row, 1:65], psh[:, 0:64])
            psvs = [PS[0], PS[1]]
            for piece in range(2):
                t0 = main0 + piece * 512
                for kc in range(8):
                    mm(psvs[piece][:, :], wb[:, kc, 0, :], h2T[:, kc, t0:t0 + 512], start=(kc == 0), stop=(kc == 7))
            for piece in range(2):
                psc = PS[5 + piece]
                R0_ = 1 + piece * 8
                for tap in range(9):
                    kh, kw = divmod(tap, 3)
                    mm(hview(psc[:, :], "p (r c) -> p r c", c=64), dg[:, tap, :],
                       gp_[:, R0_ + kh - 1:R0_ + kh - 1 + 8, kw:kw + 64], start=(tap == 0), stop=(tap == 8))
                act(sg[piece][:], psc[:, :], AF.Silu)
                tt("dve", actT[cc][:, piece * 512:(piece + 1) * 512], psvs[piece][:, :], sg[piece][:], ALU.mult)
        if hf == 0:
            dump("actT", V(actT[0], actT_raw[:, 0:4, 0:512]), [P, 4, 512], BF16)
        K.barrier()
        FF.cur = ff_mark
        wdB_raw = FF.alloc([P, NFC - NWA, D_MODEL], BF16, "wdB")
        wdB = [Tile(wdB_raw[:, kc, :], f"wdB{kc}") for kc in range(NFC - NWA)]
        wd_bf = wdA + wdB
        ot = [FF.tile([P, D_MODEL], F32, f"ot{i}") for i in range(2)]
        tmp6 = FF.tile([P, D_MODEL], F32, "tmp6")
        junk7 = FF.tile([P, D_MODEL], BF16, "junk7")
        sq7 = FF.tile([P, 16], F32, "sq7")
        sq7b = FF.tile([P, 16], F32, "sq7b")
        for kc in range(NWA, NFC):
            K.dma(wd_bf[kc].ap, wdown_d[kc * P:(kc + 1) * P, :], W=[wd_bf[kc][:]], queue="pool")
        for grp in range(2):
            for kc in range(NFC):
                for j in range(4):
                    i = grp * 4 + j
                    for half in range(2):
                        mm(PS[2 * j + half][:, :], actT[kc][:, i * P:(i + 1) * P],
                           wd_bf[kc][:, half * 512:(half + 1) * 512], start=(kc == 0), stop=(kc == NFC - 1))
            for j in range(4):
                i = grp * 4 + j
                l = hf * 8 + i
                psA, psB = PS[2 * j], PS[2 * j + 1]
                c1, c2 = sq7[:, i:i + 1], sq7b[:, i:i + 1]
                act(junk7[:, 0:512], psA[:, :], AF.Square, accum=c1)
                act(junk7[:, 512:1024], psB[:, :], AF.Square, accum=c2)
                tt("dve", c1, c1, c2, ALU.add)
                ts("dve", c1, c1, 1.0 / D_MODEL, ALU.mult, EPS, ALU.add)
                tt("pool", c1, c1, c_mhalf, ALU.pow)
                stt(tmp6[:, 0:512], psA[:, :], c1, GT2[:, 0:512], ALU.mult, ALU.mult)
                stt(tmp6[:, 512:1024], psB[:, :], c1, GT2[:, 512:1024], ALU.mult, ALU.mult)
                o_t = ot[i % 2]
                tt("pool", o_t[:], tmp6[:], X1[l][:], ALU.add)
                K.dma(out_d[l * P:(l + 1) * P, :], o_t.ap[:], R=[o_t[:]], is_output=True)
        K.barrier()
    return finish()


G_IDN, G_ONES, G_NEG1, G_MF, G_MB, G_CB, G_IND0, G_IND1, G_NM_SL, G_NM_SU, G_PM_LI, G_PM_UI = range(12)
NG = 12
BIG = 30000.0


def _gconst():
    t = np.arange(P)
    a = t[:, None]
    b = t[None, :]
    same = (a // 64) == (b // 64)
    g = np.zeros((NG, P, P), np.float32)
    g[G_IDN] = np.eye(P)
    g[G_ONES] = 1.0
    g[G_NEG1] = -1.0
    g[G_MF] = (same & (a <= b))
    g[G_MB] = (same & (a >= b))
    g[G_CB] = same
    g[G_IND0] = (a < 64) * np.ones((1, P))
    g[G_IND1] = (a >= 64) * np.ones((1, P))
    g[G_NM_SL] = np.where(same & (a > b), 0.0, -BIG)
    g[G_NM_SU] = np.where(same & (a < b), 0.0, -BIG)
    g[G_PM_LI] = np.where(same & (a >= b), 0.0, BIG)
    g[G_PM_UI] = np.where(same & (a <= b), 0.0, BIG)
    return np.ascontiguousarray(g.transpose(1, 0, 2))


def _dft_consts():
    L, Fd = SEQ, 128
    sc = 1.0 / np.sqrt(L * Fd)
    n = np.arange(Fd)
    ang = 2.0 * np.pi * ((n[:, None] * n[None, :]) % Fd) / Fd
    cfsf = np.concatenate([np.cos(ang) * sc, -np.sin(ang) * sc], axis=1).astype(np.float32)
    l = np.arange(L, dtype=np.int64)
    prod = (l[:, None] * l[None, :]) % L
    angL = 2.0 * np.pi * prod / L
    c = np.cos(angL).astype(np.float32)
    s = np.sin(angL).astype(np.float32)
    tab = np.stack([c, s], axis=0)
    tab = tab.reshape(2, 16, 128, 4, 512)
    tab = tab.transpose(3, 1, 2, 0, 4)
    return cfsf, np.ascontiguousarray(tab).astype(ml_dtypes.bfloat16)


_CACHE = {}


def _host_consts():
    if "c" not in _CACHE:
        cfsf, tab = _dft_consts()
        _CACHE["c"] = (_gconst(), cfsf, tab)
    return _CACHE["c"]


def _relayout_win(w):
    wp = np.zeros((D_MODEL, 21 * P), np.float32)
    wp[:, :IN_W] = w
    return np.ascontiguousarray(wp.reshape(8, P, 21, P).transpose(2, 1, 0, 3))


def _relayout_wup(w):
    return np.ascontiguousarray(w.reshape(8, P, 2, NFC, P).transpose(3, 1, 0, 2, 4))


def make_in_maps(inputs, n_cores=8):
    gconst, cfsf, tab = _host_consts()
    f = lambda a: np.ascontiguousarray(np.asarray(a, dtype=np.float32))
    x = f(inputs["x"]); c = f(inputs["c"]); ctx = f(inputs["ctx"]); c_ctx = f(inputs["c_ctx"])
    shared = {
        "w_ada": f(inputs["w_ada"][0]), "w_in": _relayout_win(f(inputs["w_in"][0])), "w_out": f(inputs["w_out"][0]),
        "w_up": _relayout_wup(f(inputs["w_up"][0])), "w_down": f(inputs["w_down"][0]),
        "convw": f(inputs["w_qkv_conv"][0]).reshape(36, P),
        "dwcw": f(inputs["w_dwc"][0]).reshape(198, P),
        "g_gdn": f(inputs["g_gdn"][0]).reshape(1, P),
        "a_log": f(inputs["a_log"][0]).reshape(8), "dt_bias": f(inputs["dt_bias"][0]).reshape(8),
        "w_fourier": f(inputs["w_fourier"][0]),
        "gconst": gconst, "cfsf": cfsf, "dft_tab": tab,
    }
    maps = []
    for b in range(n_cores):
        vecs = np.concatenate([
            f(inputs["b_ada"][0]).reshape(48, P), f(inputs["g_pre_mix"][0]).reshape(8, P),
            f(inputs["g_post_mix"][0]).reshape(8, P), f(inputs["g_pre_ffn"][0]).reshape(8, P),
            f(inputs["g_post_ffn"][0]).reshape(8, P), c[b].reshape(8, P), c_ctx.reshape(8, P)], axis=0)
        m = dict(shared)
        m["x"] = x[b]
        m["ctx"] = ctx[b]
        m["vecs"] = np.ascontiguousarray(vecs)
        maps.append(m)
    return maps


def kernel(**inputs):
    nc, _ = build_program()
    maps = make_in_maps(inputs, 8)
    res = run_bass_kernel_spmd(nc, maps, core_ids=list(range(8)))
    return np.stack([np.asarray(r["out"], dtype=np.float32) for r in res.results], axis=0)
```

```python
import os
import numpy as np
import ml_dtypes
import concourse.bass as bass
import concourse.mybir as mybir
from concourse.bass_utils import run_bass_kernel_spmd

F32 = mybir.dt.float32
BF16 = mybir.dt.bfloat16
AF = mybir.ActivationFunctionType
ALU = mybir.AluOpType

P = 128
D_MODEL = 1024
SEQ = 2048
CTX = 256
NTOK = SEQ + CTX
NT = NTOK // P
NLT = SEQ // P
IN_W = 2576
D_FF = 2816
NFC = D_FF // P
EPS = 1e-6
EPOCH = 12000
NDMA = 24


class Tile:
    def __init__(self, ap, name="", psum=False):
        self.ap = ap
        self.name = name
        self.w = None
        self.r = {}
        self.psum = psum

    def __getitem__(self, idx):
        return V(self, self.ap[idx])

    def v(self, ap):
        return V(self, ap)


class V:
    def __init__(self, tile, a):
        self.tile = tile
        self.a = a

    def __getitem__(self, idx):
        return V(self.tile, self.a[idx])

    def bc(self, shape):
        return V(self.tile, self.a.broadcast_to(list(shape)))

    def bitcast(self, dt):
        return V(self.tile, self.a.bitcast(dt))


class _Eng:
    def __init__(self, K, name, h):
        self.K = K
        self.name = name
        self.h = h
        self.sem = None
        self.cnt = 0
        self.seen = {}
        self.nsem = 0
        self.ninst = 0

    def newsem(self):
        self.sem = self.K.nc.alloc_semaphore(f"s_{self.name}_{self.nsem}")
        self.nsem += 1
        self.cnt = 0


class Kern:
    def __init__(self, nc):
        self.nc = nc
        self.E = {}
        for name, h in (("pe", nc.tensor), ("dve", nc.vector), ("act", nc.scalar),
                        ("pool", nc.gpsimd), ("sp", nc.sync)):
            e = _Eng(self, name, h)
            if name != "sp":
                e.newsem()
            self.E[name] = e
        self.dpool = [[nc.alloc_semaphore(f"s_dma_{i}"), 0] for i in range(NDMA)]
        self.drr = 0
        self.out_tokens = []

    def _deps(self, e, W, R):
        need = []
        for v in R:
            t = v.tile
            if t.w is not None:
                need.append((t.w, "raw"))
            if t.psum:
                for tok in t.r.values():
                    if tok[2] is not e:
                        need.append((tok, "rar"))
        for v in W:
            t = v.tile
            if t.w is not None:
                need.append((t.w, "waw"))
            for tok in t.r.values():
                need.append((tok, "war"))
        for (sem, val, src), kind in need:
            if src is e:
                if e.name == "pe":
                    continue
            if e.seen.get(id(sem), 0) >= val:
                continue
            e.h.wait_ge(sem, val)
            e.seen[id(sem)] = val

    def _post(self, tok, W, R):
        sem = tok[0]
        for v in R:
            v.tile.r[id(sem)] = tok
        for v in W:
            v.tile.w = tok
            v.tile.r = {}

    def emit(self, eng, fn, W=(), R=()):
        e = self.E[eng]
        self._deps(e, W, R)
        if e.cnt >= EPOCH:
            e.newsem()
        inst = fn(e.h)
        e.cnt += 1
        e.ninst += 1
        inst.then_inc(e.sem, 1)
        tok = (e.sem, e.cnt, e)
        self._post(tok, W, R)
        return tok

    def dma(self, out, in_, W=(), R=(), is_output=False, queue="sp"):
        e = self.E[queue]
        self._deps(e, W, R)
        slot = self.dpool[self.drr]
        self.drr = (self.drr + 1) % NDMA
        sem, n = slot
        if n > 0 and e.seen.get(id(sem), 0) < 16 * n:
            e.h.wait_ge(sem, 16 * n)
            e.seen[id(sem)] = 16 * n
        e.h.dma_start(out=out, in_=in_).then_inc(sem, 16)
        slot[1] = n + 1
        tok = (sem, 16 * (n + 1), None)
        self._post(tok, W, R)
        if is_output:
            self.out_tokens.append(tok)
        return tok

    def barrier(self):
        names = ["pe", "dve", "act", "pool"]
        for a in names:
            ea = self.E[a]
            for b in names + ["sp"]:
                if a == b:
                    continue
                eb = self.E[b]
                if b == "sp":
                    continue
                if eb.cnt == 0:
                    continue
                if ea.seen.get(id(eb.sem), 0) >= eb.cnt:
                    continue
                ea.h.wait_ge(eb.sem, eb.cnt)
                ea.seen[id(eb.sem)] = eb.cnt
            for sem, n in self.dpool:
                if n > 0 and ea.seen.get(id(sem), 0) < 16 * n:
                    ea.h.wait_ge(sem, 16 * n)
                    ea.seen[id(sem)] = 16 * n
        sp = self.E["sp"]
        for b in names:
            eb = self.E[b]
            if eb.cnt and sp.seen.get(id(eb.sem), 0) < eb.cnt:
                sp.h.wait_ge(eb.sem, eb.cnt)
                sp.seen[id(eb.sem)] = eb.cnt
        for sem, n in self.dpool:
            if n > 0 and sp.seen.get(id(sem), 0) < 16 * n:
                sp.h.wait_ge(sem, 16 * n)
                sp.seen[id(sem)] = 16 * n

    def finish(self):
        sp = self.E["sp"]
        for sem, val, _ in self.out_tokens:
            if sp.seen.get(id(sem), 0) < val:
                sp.h.wait_ge(sem, val)
                sp.seen[id(sem)] = val


class Arena:
    def __init__(self, nc, K, start, size, name):
        self.nc = nc
        self.K = K
        self.start = start
        self.end = start + size
        self.cur = start
        self.name = name
        self.n = 0

    def alloc(self, shape, dt, name=None):
        esz = 4 if dt == F32 else 2
        nbytes = esz
        for s in shape[1:]:
            nbytes *= s
        off = (self.cur + 63) // 64 * 64
        if off + nbytes > self.end:
            raise RuntimeError(f"arena {self.name} overflow allocating {name} {shape}: "
                               f"{off - self.start}+{nbytes} > {self.end - self.start}")
        self.cur = off + nbytes
        self.n += 1
        return self.nc.alloc_sbuf_tensor_at(f"{self.name}_{name or 't'}_{self.n}", list(shape), dt, offset=off)

    def tile(self, shape, dt, name=None):
        return Tile(self.alloc(shape, dt, name), name or "t")

    def reset(self, barrier=True):
        if barrier:
            self.K.barrier()
        self.cur = self.start


KB = 1024


def build_program(debug=None, stop_after=None):
    debug = debug or []
    nc = bass.Bass("TRN2", target_bir_lowering=False)
    K = Kern(nc)
    base = (nc.sbuf_base + 63) // 64 * 64
    avail = nc.sbuf_top - base
    R0 = Arena(nc, K, base, 16 * KB, "R0")
    R1 = Arena(nc, K, base + 16 * KB, 64 * KB, "R1")
    R2 = Arena(nc, K, base + 80 * KB, 16 * KB, "R2")
    R3 = Arena(nc, K, base + 96 * KB, 16 * KB, "R3")
    R4 = Arena(nc, K, base + 112 * KB, 16 * KB, "R4")
    R5 = Arena(nc, K, base + 128 * KB, 54 * KB, "R5")
    R6 = Arena(nc, K, base + 182 * KB, avail - 182 * KB, "R6")
    dbg_out = {}

    def dram(name, shape, dt=F32, kind="ExternalInput"):
        return nc.dram_tensor(name, list(shape), dt, kind=kind).ap()

    x_d = dram("x", [SEQ, D_MODEL])
    ctx_d = dram("ctx", [CTX, D_MODEL])
    wada_d = dram("w_ada", [D_MODEL, 6 * D_MODEL])
    win_d = dram("w_in", [21, P, 8, P])
    wout_d = dram("w_out", [D_MODEL, D_MODEL])
    wup_d = dram("w_up", [NFC, P, 8, 2, P])
    wdown_d = dram("w_down", [D_FF, D_MODEL])
    vecs_d = dram("vecs", [96, P])
    convw_d = dram("convw", [36, P])
    dwcw_d = dram("dwcw", [198, P])
    ggdn_d = dram("g_gdn", [1, P])
    alog_d = dram("a_log", [8])
    dtb_d = dram("dt_bias", [8])
    wf_d = dram("w_fourier", [4, P, P])
    gconst_d = dram("gconst", [P, NG, P])
    cfsf_d = dram("cfsf", [P, 256])
    tab_d = dram("dft_tab", [4, 16, P, 2, 512], BF16)
    out_d = dram("out", [SEQ, D_MODEL], kind="ExternalOutput")

    PS = [Tile(nc.alloc_psum_tensor(f"ps{i}", [P, 512], F32), f"ps{i}", psum=True) for i in range(8)]

    def dump(name, view, shape, dt=F32):
        if name not in debug:
            return
        d = dram("dbg_" + name, shape, dt, kind="ExternalOutput")
        dbg_out[name] = (shape, dt)
        K.dma(d, view.a, R=[view], is_output=True)

    def finish():
        K.finish()
        return nc, dbg_out

    def mm(out, lhsT, rhs, start=True, stop=True):
        K.emit("pe", lambda e: e.matmul(out.a, lhsT=lhsT.a, rhs=rhs.a, start=start, stop=stop),
               W=[out], R=[lhsT, rhs])

    def tr(out, in_, ident):
        K.emit("pe", lambda e: e.transpose(out.a, in_.a, ident.a), W=[out], R=[in_, ident])

    def act(out, in_, func, scale=1.0, bias=None, accum=None):
        R = [in_]
        kw = {}
        if isinstance(scale, V):
            R.append(scale)
            kw["scale"] = scale.a
        else:
            kw["scale"] = float(scale)
        if bias is not None:
            R.append(bias)
            kw["bias"] = bias.a
        W = [out]
        if accum is not None:
            W.append(accum)
            kw["accum_out"] = accum.a
        K.emit("act", lambda e: e.activation(out.a, in_.a, func, **kw), W=W, R=R)

    def ts(eng, out, in0, s1, op0, s2=None, op1=None):
        R = [in0]
        a1 = s1.a if isinstance(s1, V) else float(s1)
        if isinstance(s1, V):
            R.append(s1)
        a2 = None
        if s2 is not None:
            a2 = s2.a if isinstance(s2, V) else float(s2)
            if isinstance(s2, V):
                R.append(s2)
        if op1 is None:
            K.emit(eng, lambda e: e.tensor_scalar(out.a, in0.a, a1, None, op0), W=[out], R=R)
        else:
            K.emit(eng, lambda e: e.tensor_scalar(out.a, in0.a, a1, a2, op0, op1), W=[out], R=R)

    def tt(eng, out, in0, in1, op):
        K.emit(eng, lambda e: e.tensor_tensor(out.a, in0.a, in1.a, op), W=[out], R=[in0, in1])

    def stt(out, in0, s, in1, op0, op1):
        R = [in0, in1]
        a = s.a if isinstance(s, V) else float(s)
        if isinstance(s, V):
            R.append(s)
        K.emit("dve", lambda e: e.scalar_tensor_tensor(out.a, in0.a, a, in1.a, op0, op1), W=[out], R=R)

    def cp(eng, out, in_):
        if eng == "act":
            act(out, in_, AF.Identity)
        else:
            K.emit(eng, lambda e: e.tensor_copy(out.a, in_.a), W=[out], R=[in_])

    def memset(eng, out, val):
        K.emit(eng, lambda e: e.memset(out.a, val), W=[out])

    def hview(view, pat, **kw):
        return V(view.tile, view.a.rearrange(pat, **kw))

    def bc_mid(view, n):
        sh = list(view.a.shape)
        return V(view.tile, view.a.unsqueeze(1).broadcast_to([sh[0], n, sh[1]]))

    def bc_last(view, n):
        sh = list(view.a.shape)
        return V(view.tile, view.a.unsqueeze(len(sh)).broadcast_to(sh + [n]))

    def ps4(ps):
        return hview(ps[:, :], "p (h d) -> p h d", d=P)

    GC = R0.tile([P, NG, P], F32, "gconst")
    K.dma(GC.ap[:], gconst_d, W=[GC[:]])

    def gcv(i):
        return GC[:, i, :]

    IDN = gcv(G_IDN)
    ONES = gcv(G_ONES)
    NEG1 = gcv(G_NEG1)
    idn_bf = R0.tile([P, P], BF16, "idn_bf")
    ones_bf = R0.tile([P, P], BF16, "ones_bf")
    cp("dve", idn_bf[:], IDN)
    cp("dve", ones_bf[:], ONES)
    cst = R0.tile([P, 8], F32, "cst")
    memset("dve", cst[:, 0:1], EPS)
    memset("dve", cst[:, 1:2], -0.5)
    memset("dve", cst[:, 2:3], 1.0)
    c_eps = cst[:, 0:1]
    c_mhalf = cst[:, 1:2]
    c_one = cst[:, 2:3]
    if stop_after == -3:
        dump("idn", idn_bf[:], [P, P], BF16)
        return finish()

    vec_fm = R0.tile([P, 96], F32, "vec_fm")
    cw_fm = R0.tile([P, 36], F32, "cw_fm")
    dw_fm = R0.tile([P, 198], F32, "dw_fm")
    gg_fm = R0.tile([P, 1], F32, "gg_fm")
    st_v = R1.tile([96, P], F32, "st_v")
    st_c = R1.tile([36, P], F32, "st_c")
    st_d1 = R1.tile([P, P], F32, "st_d1")
    st_d2 = R1.tile([70, P], F32, "st_d2")
    st_g = R1.tile([1, P], F32, "st_g")
    K.dma(st_v.ap[:], vecs_d, W=[st_v[:]])
    K.dma(st_c.ap[:], convw_d, W=[st_c[:]])
    K.dma(st_d1.ap[:], dwcw_d[0:128, :], W=[st_d1[:]])
    K.dma(st_d2.ap[:], dwcw_d[128:198, :], W=[st_d2[:]])
    K.dma(st_g.ap[:], ggdn_d, W=[st_g[:]])
    ps = PS[0]
    tr(ps[:, 0:96], st_v[:], GC[0:96, G_IDN, 0:96])
    cp("dve", vec_fm[:], ps[:, 0:96])
    ps = PS[1]
    tr(ps[:, 0:36], st_c[:], GC[0:36, G_IDN, 0:36])
    tr(ps[:, 36:37], st_g[:], GC[0:1, G_IDN, 0:1])
    cp("dve", cw_fm[:], ps[:, 0:36])
    cp("dve", gg_fm[:], ps[:, 36:37])
    ps = PS[2]
    tr(ps[:, 0:128], st_d1[:], IDN)
    tr(ps[:, 128:198], st_d2[:], GC[0:70, G_IDN, 0:70])
    cp("dve", dw_fm[:], ps[:, 0:198])
    if stop_after == -2:
        dump("dw_fm", dw_fm[:], [P, 198])
        return finish()

    ab_c = R0.tile([P, 16], F32, "ab_c")
    K.dma(ab_c.ap[:, 0:8], dtb_d.partition_broadcast(P), W=[ab_c[:]])
    K.dma(ab_c.ap[:, 8:16], alog_d.partition_broadcast(P), W=[ab_c[:]])
    act(ab_c[:, 8:16], ab_c[:, 8:16], AF.Exp)
    ts("dve", ab_c[:, 8:16], ab_c[:, 8:16], -1.0, ALU.mult)

    silu_c = R0.tile([P, 8, 2], F32, "silu_c")
    act(silu_c[:, :, 0], vec_fm[:, 80:88], AF.Silu)
    act(silu_c[:, :, 1], vec_fm[:, 88:96], AF.Silu)
    if stop_after == -1:
        dump("silu_c", silu_c[:], [P, 8, 2])
        dump("ab_c", ab_c[:], [P, 16])
        return finish()

    modA = R0.tile([P, 16, 2], F32, "modA")
    modB = R0.tile([P, 32, 2], F32, "modB")
    wada_v = wada_d.rearrange("(kc p) n -> p kc n", p=P)
    wa = [R5.tile([P, 8, 512], F32, f"wa{i}") for i in range(2)]
    sA1 = R0.tile([P, 8, 2], F32, "sA1")
    sA2 = R0.tile([P, 8], F32, "sA2")
    gtg = R0.tile([P, 16], F32, "gtg")

    def mod_groups(g0, g1, psm, jbase):
        for gidx in range(g0, g1):
            w = wa[gidx % 2]
            K.dma(w.ap[:], wada_v[:, :, gidx * 512:(gidx + 1) * 512], W=[w[:]])
            for cc in range(4):
                j = gidx * 4 + cc - jbase
                for kc in range(8):
                    mm(psm[:, 2 * j:2 * j + 2], w[:, kc, cc * 128:(cc + 1) * 128], silu_c[:, kc, :],
                       start=(kc == 0), stop=(kc == 7))
                if cc % 2 == 1:
                    yield

    for _ in mod_groups(0, 4, PS[3], 0):
        pass
    tt("dve", modA[:], hview(PS[3][:, 0:32], "p (j v) -> p j v", v=2), bc_last(vec_fm[:, 0:16], 2), ALU.add)
    stt(sA1[:], modA[:, 8:16, :], 1.0, bc_last(vec_fm[:, 48:56], 2), ALU.add, ALU.mult)

    def mod_rest():
        yield from mod_groups(4, 12, PS[2], 16)
        tt("dve", modB[:], hview(PS[2][:, 0:64], "p (j v) -> p j v", v=2), bc_last(vec_fm[:, 16:48], 2), ALU.add)
        stt(sA2[:], modB[:, 16:24, 0], 1.0, vec_fm[:, 64:72], ALU.add, ALU.mult)
        tt("dve", gtg[:, 0:8], modB[:, 0:8, 0], vec_fm[:, 56:64], ALU.mult)
        tt("dve", gtg[:, 8:16], modB[:, 24:32, 0], vec_fm[:, 72:80], ALU.mult)
        dump("mod", modB[:], [P, 32, 2])

    mod_rest_gen = mod_rest()
    if stop_after in (-0.5, 0):
        for _ in mod_rest_gen:
            pass
        return finish()

    def make_GT(j, GTt, dgs, psa, psb_):
        for half in range(2):
            psb = (psa, psb_)[half]
            for c4 in range(4):
                c = half * 4 + c4
                dgt = dgs[c % 2]
                ts("dve", dgt[:], IDN, gtg[:, j * 8 + c:j * 8 + c + 1], ALU.mult)
                mm(psb[:, c4 * 128:(c4 + 1) * 128], ONES, dgt[:])
            cp("act", GTt[:, half * 512:(half + 1) * 512], psb[:])

    R1.reset()
    if stop_after == 0:
        return finish()

    TOKBLK = [(0, 256)] + [(256 + 512 * b, 512) for b in range(4)]
    hT_raw = R1.alloc([P, 8, NTOK], BF16, "hT")
    hTb = [Tile(hT_raw[:, :, o:o + n], f"hT{b}") for b, (o, n) in enumerate(TOKBLK)]

    def hT_view(tok0, n):
        for b, (o, nn) in enumerate(TOKBLK):
            if o <= tok0 and tok0 + n <= o + nn:
                return hTb[b], tok0 - o
        raise AssertionError

    ssq = R0.tile([P, NT], F32, "ssq")
    rstd = R0.tile([P, NT], F32, "rstd")
    xts = [R1.tile([P, D_MODEL], F32, f"xt{i}") for i in range(3)]
    xns = [R1.tile([P, D_MODEL], BF16, f"xn{i}") for i in range(2)]
    junk = R1.tile([P, D_MODEL], BF16, "junk")

    def norm_tile(xt_v, xn, ssq_col, rstd_col, junk_t):
        act(junk_t[:], xt_v, AF.Square, accum=ssq_col)
        ts("dve", rstd_col, ssq_col, 1.0 / D_MODEL, ALU.mult, EPS, ALU.add)
        tt("pool", rstd_col, rstd_col, c_mhalf, ALU.pow)
        ts("dve", xn[:], xt_v, rstd_col, ALU.mult)

    def transpose_mod(xn, pst, dst_fn, sc_fn, bi_fn, use_act=True):
        pst_bf = V(pst, pst.ap[:].bitcast(BF16))
        for kc in range(8):
            tr(pst_bf[:, kc * 128:(kc + 1) * 128], xn[:, kc * 128:(kc + 1) * 128], idn_bf[:])
        for kc in range(8):
            if use_act:
                act(dst_fn(kc), pst_bf[:, kc * 128:(kc + 1) * 128], AF.Identity, scale=sc_fn(kc), bias=bi_fn(kc))
            else:
                ts("dve", dst_fn(kc), pst_bf[:, kc * 128:(kc + 1) * 128], sc_fn(kc), ALU.mult, bi_fn(kc), ALU.add)

    def p1_stageA(ti):
        xt = xts[ti % 3]
        src = ctx_d[ti * P:(ti + 1) * P, :] if ti < 2 else x_d[(ti - 2) * P:(ti - 1) * P, :]
        K.dma(xt.ap[:], src, W=[xt[:]])
        norm_tile(xt[:], xns[ti % 2], ssq[:, ti:ti + 1], rstd[:, ti:ti + 1], junk)

    def p1_stageB(ti):
        vsel = 1 if ti < 2 else 0
        hb, off = hT_view(ti * P, P)
        transpose_mod(xns[ti % 2], PS[ti % 2],
                      lambda kc: hb[:, kc, off:off + P],
                      lambda kc: sA1[:, kc, vsel:vsel + 1],
                      lambda kc: modA[:, kc, vsel:vsel + 1], use_act=(ti % 2 == 0))

    p1_stageA(0)
    for ti in range(NT):
        if ti + 1 < NT:
            p1_stageA(ti + 1)
        p1_stageB(ti)
        next(mod_rest_gen, None)
    for _ in mod_rest_gen:
        pass
    dump("hT", hTb[1][:], [P, 8, 512], BF16)
    if stop_after == 1:
        return finish()

    R5.reset(barrier=False)
    uT_raw = R3.alloc([P, 4, SEQ], BF16, "uT")
    szT_raw = R4.alloc([P, 4, SEQ], BF16, "szT")
    qT_raw = R2.alloc([P, 4, SEQ], BF16, "qT")
    kT_raw = R5.alloc([P, 4, NTOK], BF16, "kT")
    ktok_raw = R5.alloc([P, NT, 4, P], BF16, "ktok")
    vtok_raw = R5.alloc([P, NT, 4, P], BF16, "vtok")
    uT = [Tile(uT_raw[:, g, :], f"uT{g}") for g in range(4)]
    szT = Tile(szT_raw, "szT")
    qT = [Tile(qT_raw[:, g, :], f"qT{g}") for g in range(4)]
    kT = [Tile(kT_raw[:, g, :], f"kT{g}") for g in range(4)]
    ktok = Tile(ktok_raw, "ktok")
    vtok = Tile(vtok_raw, "vtok")
    abtok = R0.tile([P, NT, 16], F32, "abtok")

    K.barrier()
    R1.cur = R1.start + 8 * NTOK * 2
    wbf = [R6.tile([P, 8, P], BF16, f"wbf{i}") for i in range(3)]
    PADW = NTOK + 4
    pads = [R6.tile([P, PADW], BF16, f"pad{i}") for i in range(2)]
    dgc = [R1.tile([P, 3, P], BF16, f"dgc{i}") for i in range(2)]
    sil = R1.tile([P, NTOK], F32, "sil")
    sqs = [R1.tile([P, 512], BF16, f"sq{i}") for i in range(2)]
    lnt = R1.tile([P, NTOK], F32, "lnt")
    vT_tmp = V(sil, sil.ap[:].bitcast(BF16)[:, 0:NTOK])
    for pd in pads:
        memset("pool", pd[:], 0.0)

    def padcol(tok):
        return 1 + tok if tok < CTX else 3 + tok

    NCH = 21

    def load_w(j):
        w = wbf[j % 3]
        K.dma(w.ap[:], win_d[j], W=[w[:]], queue="pool")

    load_w(0)
    load_w(1)
    state = {"mmrr": 0, "evrr": 0}

    def kind_of(j):
        return "uqkvz"[j // 4] if j < 20 else "ab"

    def main_part(j):
        if j + 2 < NCH:
            load_w(j + 2)
        wb = wbf[j % 3]
        kind = kind_of(j)
        g = j % 4
        if kind == "ab":
            psab = PS[7]
            for t in range(NT):
                hb, off = hT_view(t * P, P)
                for kc in range(8):
                    mm(psab[:, t * 16:(t + 1) * 16], hb[:, kc, off:off + P], wb[:, kc, 0:16],
                       start=(kc == 0), stop=(kc == 7))
            cp("dve", abtok[:], hview(psab[:, 0:NT * 16], "p (t c) -> p t c", c=16))
            return
        blocks = list(range(5)) if kind in "kv" else list(range(1, 5))
        pad = pads[j % 2]
        if kind in "qkv":
            dg = dgc[j % 2]
            cj = j - 4
            for tap in range(3):
                ts("dve", dg[:, tap, :], IDN, cw_fm[:, tap * 12 + cj:tap * 12 + cj + 1], ALU.mult)
        yield
        for b_ in blocks:
            o, n = TOKBLK[b_]
            psa = PS[state["mmrr"] % 4]
            state["mmrr"] += 1
            for kc in range(8):
                mm(psa[:, 0:n], wb[:, kc, :], hTb[b_][:, kc, :], start=(kc == 0), stop=(kc == 7))
            if kind == "u":
                cp("dve", uT[g][:, o - CTX:o - CTX + n], psa[:, 0:n])
            elif kind == "z":
                act(szT[:, g, o - CTX:o - CTX + n], psa[:, 0:n], AF.Silu)
            else:
                pc = padcol(o)
                cp("act" if state["evrr"] % 2 == 0 else "dve", pad[:, pc:pc + n], psa[:, 0:n])
                state["evrr"] += 1
            yield

    def post_a(j):
        kind = kind_of(j)
        if kind not in "qkv":
            return
        blocks = list(range(5)) if kind in "kv" else list(range(1, 5))
        pad = pads[j % 2]
        dg = dgc[j % 2]
        for b_ in blocks:
            o, n = TOKBLK[b_]
            psc = PS[5 + (b_ % 2)]
            pc = padcol(o)
            for tap in range(3):
                mm(psc[:, 0:n], dg[:, tap, :], pad[:, pc + tap - 1:pc + tap - 1 + n],
                   start=(tap == 0), stop=(tap == 2))
            if kind == "v":
                act(vT_tmp[:, o:o + n], psc[:, 0:n], AF.Silu)
            else:
                act(sil[:, o:o + n], psc[:, 0:n], AF.Silu)
            yield
        if kind == "v":
            return
        for b_ in blocks:
            o, n = TOKBLK[b_]
            sq = sqs[b_ % 2]
            tt("dve", sq[:, 0:n], sil[:, o:o + n], sil[:, o:o + n], ALU.mult)
            psn = PS[4] if b_ % 2 == 0 else PS[7]
            mm(psn[:, 0:n], ones_bf[:], sq[:, 0:n])
            act(lnt[:, o:o + n], psn[:, 0:n], AF.Ln, bias=c_eps)
            yield

    def post_b(j):
        kind = kind_of(j)
        if kind not in "qkv":
            return
        g = j % 4
        blocks = list(range(5)) if kind in "kv" else list(range(1, 5))
        if kind == "v":
            src = vT_tmp
            dst = vtok
        else:
            scale = float(P) ** -0.5 if kind == "q" else 1.0
            for b_ in blocks:
                o, n = TOKBLK[b_]
                act(lnt[:, o:o + n], lnt[:, o:o + n], AF.Exp, scale=-0.5)
                if kind == "q":
                    stt(qT[g][:, o - CTX:o - CTX + n], sil[:, o:o + n], scale, lnt[:, o:o + n],
                        ALU.mult, ALU.mult)
                else:
                    stt(kT[g][:, o:o + n], sil[:, o:o + n], scale, lnt[:, o:o + n],
                        ALU.mult, ALU.mult)
                yield
            if kind == "q":
                return
            src = kT[g]
            dst = ktok
        for t0 in (0, 8, 16):
            nt = min(8, NT - t0)
            pst = PS[5 + ((t0 // 8) % 2)]
            pst_bf = V(pst, pst.ap[:].bitcast(BF16))
            for i in range(nt):
                t = t0 + i
                tr(pst_bf[:, i * P:(i + 1) * P], src[:, t * P:(t + 1) * P], idn_bf[:])
            cp("dve", dst[:, t0:t0 + nt, g, :], hview(pst_bf[:, 0:nt * P], "p (t d) -> p t d", d=P))
            yield

    def gen_or_empty(fn, *a):
        r_ = fn(*a)
        return r_ if r_ is not None else iter(())

    def chain2(*gens):
        for g_ in gens:
            yield from g_

    def rr_weighted(threads):
        live = [[g_, w_] for g_, w_ in threads]
        while live:
            for ent in list(live):
                for _ in range(ent[1]):
                    try:
                        next(ent[0])
                    except StopIteration:
                        live.remove(ent)
                        break

    for _ in gen_or_empty(main_part, 0):
        pass
    P2_ORDER = os.environ.get("P2_ORDER", "mab")
    for j in range(NCH):
        parts = {"a": gen_or_empty(post_a, j), "b": gen_or_empty(post_b, j),
                 "m": gen_or_empty(main_part, j + 1) if j + 1 < NCH else iter(())}
        for key in P2_ORDER:
            for _ in parts[key]:
                pass
    dump("uT", V(uT[0], uT_raw[:, :, 0:512]), [P, 4, 512], BF16)
    dump("szT", szT[:, :, 0:512], [P, 4, 512], BF16)
    dump("qT", V(qT[0], qT_raw[:, :, 0:512]), [P, 4, 512], BF16)
    dump("kT", V(kT[0], kT_raw[:, :, 0:768]), [P, 4, 768], BF16)
    dump("ktok", ktok[:, 0:4, :, :], [P, 4, 4, P], BF16)
    dump("vtok", vtok[:, 0:4, :, :], [P, 4, 4, P], BF16)
    dump("abtok", abtok[:], [P, NT, 16])
    R1.reset()
    R6.reset(barrier=False)
    if stop_after == 2:
        return finish()

    def sc8(name):
        return R6.tile([P, NT, 8], F32, name)

    beta = sc8("beta")
    gg = sc8("g")
    Gs = sc8("Gs")
    Gam = sc8("Gam")
    epsl = sc8("epsl")
    nGb = sc8("nGb")
    xs_t = sc8("xs")
    ax_t = sc8("ax")
    egl = R6.tile([P, NT, 2, 8], F32, "egl")
    act(beta[:], abtok[:, :, 0:8], AF.Sigmoid)
    tt("dve", xs_t[:], abtok[:, :, 8:16], bc_mid(ab_c[:, 0:8], NT), ALU.add)
    stt(ax_t[:], xs_t[:], -1.0, xs_t[:], ALU.mult, ALU.max)
    act(ax_t[:], ax_t[:], AF.Exp, scale=-1.0)
    act(ax_t[:], ax_t[:], AF.Ln, bias=c_one)
    stt(xs_t[:], xs_t[:], 0.0, ax_t[:], ALU.max, ALU.add)
    tt("dve", gg[:], xs_t[:], bc_mid(ab_c[:, 8:16], NT), ALU.mult)
    psG, psGL, psE = PS[0], PS[1], PS[2]
    for t in range(NT):
        for d in range(2):
            mm(psG[:, t * 8 + d * 4:t * 8 + d * 4 + 4], gcv(G_MF if d == 0 else G_MB), gg[:, t, d * 4:d * 4 + 4])
        mm(psGL[:, t * 8:(t + 1) * 8], gcv(G_CB), gg[:, t, :])
        for c in range(2):
            mm(psE[:, (t * 2 + c) * 8:(t * 2 + c + 1) * 8], gcv(G_IND0 if c == 0 else G_IND1), gg[:, t, :])
    cp("act", Gs[:], hview(psG[:, 0:NT * 8], "p (t c) -> p t c", c=8))
    act(Gam[:], Gs[:], AF.Exp)
    tt("dve", epsl[:], hview(psGL[:, 0:NT * 8], "p (t c) -> p t c", c=8), Gs[:], ALU.subtract)
    act(epsl[:], epsl[:], AF.Exp)
    ts("dve", nGb[:], Gam[:], -1.0, ALU.mult)
    act(egl[:], hview(psE[:, 0:NT * 16], "p (t c e) -> p t c e", c=2, e=8), AF.Exp)
    dump("g", gg[:], [P, NT, 8])
    dump("beta", beta[:], [P, NT, 8])
    dump("Gs", Gs[:], [P, NT, 8])
    dump("epsl", epsl[:], [P, NT, 8])
    dump("egl", egl[:], [P, NT, 2, 8])
    if stop_after == 2.5:
        return finish()

    F32R = mybir.dt.float32r
    cr = R6.tile([P, 2, P], F32, "ones_neg1_r")
    cp("dve", V(cr, cr.ap[:, 0, :].bitcast(F32R)), ONES)
    cp("dve", V(cr, cr.ap[:, 1, :].bitcast(F32R)), NEG1)
    ONES_r = cr[:, 0, :]
    NEG1_r = cr[:, 1, :]

    def rr(view):
        return V(view.tile, view.a.bitcast(F32R))

    def f4(name, reg=R1):
        return reg.tile([P, 4, P], F32, name)

    Ttmp = [dict(tA=f4(f"tA{d}"), Qa=f4(f"Qa{d}"), QTa=f4(f"QTa{d}"), Qb=f4(f"Qb{d}"), QTb=f4(f"QTb{d}"))
            for d in range(2)]
    TTs = [[f4(f"TT{d}{i}") for i in range(2)] for d in range(2)]
    attnTs = [[R6.tile([P, 4, P], BF16, f"attnT{d}{i}") for i in range(2)] for d in range(2)]
    qdecs = [[R6.tile([P, 4, P], BF16, f"qdec{d}{i}") for i in range(2)] for d in range(2)]
    rhs2s = [f4(f"rhs2_{d}") for d in range(2)]
    OF_raw = R1.alloc([P, NLT, 4, P], F32, "OF")
    OF = [Tile(OF_raw[:, l], f"OF{l}") for l in range(NLT)]
    vns = [R0.tile([P, 4, P], BF16, f"vn{d}") for d in range(2)]
    vdcs = [[(R0 if (d, c) != (1, 1) else R6).tile([P, 4, P], BF16, f"vd{d}{c}") for c in range(2)] for d in range(2)]
    S = [f4(f"S{d}", R6) for d in range(2)]
    Sb = [R6.tile([P, 4, P], BF16, f"Sb{d}") for d in range(2)]
    for d in range(2):
        memset("pool", S[d][:], 0.0)
        memset("pool", Sb[d][:], 0.0)
        memset("pool", vns[d][:], 0.0)
        memset("pool", Ttmp[d]["tA"][:], 0.0)
        cp("dve", rr(rhs2s[d][:]), Ttmp[d]["tA"][:])
        for c in range(2):
            memset("pool", vdcs[d][c][:], 0.0)
    TPSd = [[PS[0], PS[1], PS[2]], [PS[3], PS[4], PS[5]]]
    SPS = [PS[6], PS[7]]

    def t_phase(t, d, slot):
        lat = t >= 2
        l = t - 2
        Mdir = gcv(G_MF if d == 0 else G_MB)
        NMA = gcv(G_NM_SL if d == 0 else G_NM_SU)
        PMT = gcv(G_PM_UI if d == 0 else G_PM_LI)
        gcol = gg[:, t, d * 4:d * 4 + 4]
        bcol = beta[:, t, d * 4:d * 4 + 4]
        tk = slice(t * P, (t + 1) * P)
        TT = TTs[d][slot]
        aT = attnTs[d][slot]
        qd = qdecs[d][slot]
        tm = Ttmp[d]
        tA, Qa, QTa, Qb_, QTb_ = tm["tA"], tm["Qa"], tm["QTa"], tm["Qb"], tm["QTb"]
        dA, dT_ = QTb_, Qb_
        p0, p1, p2 = TPSd[d]
        tt("dve", rr(Qa[:]), bc_mid(Mdir, 4), bc_last(gcol, P), ALU.mult)
        mm(p0[:, :], rr(NEG1_r), hview(rr(Qa[:]), "p h d -> p (h d)"), start=True, stop=False)
        for h in range(4):
            mm(p0[:, h * P:(h + 1) * P], rr(Qa[:, h, :]), rr(ONES_r), start=False, stop=(h == 3))
        for h in range(4):
            mm(p1[:, h * P:(h + 1) * P], kT[h][:, tk], kT[h][:, tk])
        yield
        stt(rr(dA[:]), ps4(p0), 0.0, bc_mid(NMA, 4), ALU.min, ALU.add)
        act(rr(dA[:]), dA[:], AF.Exp)
        if lat:
            stt(rr(dT_[:]), ps4(p0), 0.0, bc_mid(PMT, 4), ALU.max, ALU.add)
            act(rr(dT_[:]), dT_[:], AF.Exp, scale=-1.0)
        yield
        tt("dve", tA[:], ps4(p1), dA[:], ALU.mult)
        tt("dve", rr(Qa[:]), tA[:], bc_last(bcol, P), ALU.mult)
        for h in range(4):
            tr(p0[:, h * P:(h + 1) * P], Qa[:, h, :], IDN)
        if lat:
            for h in range(4):
                mm(p1[:, h * P:(h + 1) * P], kT[h][:, tk], qT[h][:, l * P:(l + 1) * P])
        yield
        cp("act", rr(QTa[:]), ps4(p0))
        stt(rr(TT[:]), QTa[:], -1.0, bc_mid(IDN, 4), ALU.mult, ALU.add)
        if lat:
            tt("dve", aT[:], ps4(p1), dT_[:], ALU.mult)
            tt("dve", rr(QTb_[:]), bc_mid(IDN, 4), bc_last(Gam[:, t, d * 4:d * 4 + 4], P), ALU.mult)
        yield
        for h in range(4):
            mm(p0[:, h * P:(h + 1) * P], rr(QTa[:, h, :]), rr(Qa[:, h, :]))
        for h in range(4):
            mm(p1[:, h * P:(h + 1) * P], rr(Qa[:, h, :]), rr(QTa[:, h, :]))
        if lat:
            mm(p2[:, :], rr(ONES_r), hview(rr(QTb_[:]), "p h d -> p (h d)"))
        yield
        cp("act", rr(Qb_[:]), ps4(p0))
        cp("act", rr(QTb_[:]), ps4(p1))
        if lat:
            tt("dve", qd[:], ps4(p2), V(qT[0], qT_raw[:, :, l * P:(l + 1) * P]), ALU.mult)
        yield
        Q, QT = Qb_, QTb_
        for k in range(1, 6):
            Qn, QTn = (Qa, QTa) if k % 2 == 1 else (Qb_, QTb_)
            if k < 5:
                for h in range(4):
                    mm(p0[:, h * P:(h + 1) * P], rr(QT[:, h, :]), rr(Q[:, h, :]))
            if k < 4:
                for h in range(4):
                    mm(p1[:, h * P:(h + 1) * P], rr(Q[:, h, :]), rr(QT[:, h, :]))
            for h in range(4):
                mm(p2[:, h * P:(h + 1) * P], rr(Q[:, h, :]), rr(TT[:, h, :]))
            yield
            if k < 5:
                cp("act", rr(Qn[:]), ps4(p0))
            if k < 4:
                cp("act", rr(QTn[:]), ps4(p1))
            tt("dve", rr(TT[:]), ps4(p2), TT[:], ALU.add)
            yield
            Q, QT = Qn, QTn
        tt("dve", rr(TT[:]), TT[:], bc_last(bcol, P), ALU.mult)
        yield

    def scan(t, d, slot):
        lat = t >= 2
        l = t - 2
        psX = SPS[d]
        psS = psX
        TT = TTs[d][slot]
        aT = attnTs[d][slot]
        qd = qdecs[d][slot]
        rhs2, vn, vdc = rhs2s[d], vns[d], vdcs[d]
        tk = slice(t * P, (t + 1) * P)
        for c in ((0, 1) if d == 0 else (1, 0)):
            r0 = c * 64
            rows = slice(r0, r0 + 64)
            for h in range(4):
                mm(psX[:, h * P:(h + 1) * P], kT[h][:, tk], Sb[d][:, h, :])
            yield
            for h in range(4):
                stt(rr(rhs2[rows, h, :]), psX[rows, h * P:(h + 1) * P], nGb[rows, t, d * 4 + h:d * 4 + h + 1],
                    vtok[rows, t, h, :], ALU.mult, ALU.add)
            yield
            for h in range(4):
                mm(psX[:, h * P:(h + 1) * P], rr(TT[:, h, :]), rr(rhs2[:, h, :]))
            yield
            psX4 = hview(psX[rows, :], "p (h d) -> p h d", d=P)
            cp("act", vn[rows], psX4)
            for h in range(4):
                act(vdc[c][rows, h, :], psX[rows, h * P:(h + 1) * P], AF.Identity,
                    scale=epsl[rows, t, d * 4 + h:d * 4 + h + 1])
            yield
            if lat:
                for h in range(4):
                    mm(psX[:, h * P:(h + 1) * P], qd[:, h, :], Sb[d][:, h, :], start=True, stop=False)
                    mm(psX[:, h * P:(h + 1) * P], aT[:, h, :], vn[:, h, :], start=False, stop=True)
                yield
                if (d == 0) != (l < NLT // 2):
                    tt("dve", OF[l][rows], psX4, OF[l][rows], ALU.add)
                else:
                    cp("act", OF[l][rows], psX4)
            for h in range(4):
                mm(psS[:, h * P:(h + 1) * P], ktok[:, t, h, :], vdc[c][:, h, :])
            yield
            for h in range(4):
                stt(S[d][:, h, :], S[d][:, h, :], egl[:, t, c, d * 4 + h:d * 4 + h + 1],
                    psS[:, h * P:(h + 1) * P], ALU.mult, ALU.add)
            cp("act", Sb[d][:], S[d][:])
            yield

    orders = [list(range(NT)), [1, 0] + list(range(NT - 1, 1, -1))]
    nsteps = NT
    if "GDN_TILES" in os.environ:
        nsteps = int(os.environ["GDN_TILES"])
        orders[1] = [1, 0] + [None] * NT

    def chain(*gens):
        for g_ in gens:
            yield from g_

    def run_weighted(threads):
        live = [[g_, w_] for g_, w_ in threads]
        while live:
            for ent in list(live):
                for _ in range(ent[1]):
                    try:
                        next(ent[0])
                    except StopIteration:
                        live.remove(ent)
                        break

    def tgen(i, d):
        if i < nsteps and orders[d][i] is not None:
            return t_phase(orders[d][i], d, i % 2)
        return iter(())

    run_weighted([(tgen(0, 0), 1), (tgen(0, 1), 1)])
    TW = int(os.environ.get("GDN_TW", "1"))
    SW = int(os.environ.get("GDN_SW", "1"))
    for i in range(nsteps):
        threads = []
        for d in range(2):
            if orders[d][i] is not None:
                threads.append((scan(orders[d][i], d, i % 2), SW))
            threads.append((tgen(i + 1, d), TW))
        run_weighted(threads)
        if i == 1:
            dump("S_ctx", S[0][:], [P, 4, P])
        if i == 2:
            dump("TT", TTs[0][0][:], [P, 4, P])
    dump("OF", V(OF[0], OF_raw[:, 0:4]), [P, 4, 4, P])
    if stop_after == 3:
        return finish()

    K.barrier()
    R6.reset(barrier=False)
    R5.reset(barrier=False)
    R1.cur = R1.start
    ydT_raw = qT_raw
    ydT = Tile(ydT_raw, "ydT")
    ssq4 = R6.tile([P, NLT, 4], F32, "ssq4")
    junk4 = R6.tile([P, P], BF16, "junk4")
    on_bf = [R6.tile([P, 4, P], BF16, f"on{i}") for i in range(2)]
    Vsb = R5.tile([P, NLT, 4, 256], BF16, "Vsb")
    yfT_raw = R5.alloc([P, 4, SEQ], BF16, "yfT")
    yfT = [Tile(yfT_raw[:, g, :], f"yfT{g}") for g in range(4)]
    cfsf = R6.tile([P, 256], F32, "cfsf")
    wf_sb = R6.tile([P, 4, P], F32, "wf_sb")
    CW = R6.tile([P, 4, 256], BF16, "CW")
    tabs = [R1.tile([P, 4, 2, 512], BF16, f"tab{i}") for i in range(3)]

    def yd_gen():
        for l in range(NLT):
            for h in range(4):
                act(junk4[:], OF[l][:, h, :], AF.Square, accum=ssq4[:, l, h:h + 1])
            if l % 4 == 3:
                yield
        ts("dve", ssq4[:], ssq4[:], 1.0 / P, ALU.mult, EPS, ALU.add)
        tt("pool", ssq4[:], ssq4[:], V(cst, cst.ap[:, 1:2].unsqueeze(2).broadcast_to([P, NLT, 4])), ALU.pow)
        yield
        for l in range(NLT):
            on = on_bf[l % 2]
            for h in range(4):
                ts("dve", on[:, h, :], OF[l][:, h, :], ssq4[:, l, h:h + 1], ALU.mult)
            pst = PS[l % 2]
            pst_bf = V(pst, pst.ap[:].bitcast(BF16))
            for h in range(4):
                tr(pst_bf[:, h * P:(h + 1) * P], on[:, h, :], idn_bf[:])
            yield
            stt(ydT[:, :, l * P:(l + 1) * P], hview(pst_bf[:, 0:4 * P], "p (h d) -> p h d", d=P), gg_fm[:, 0:1],
                szT[:, :, l * P:(l + 1) * P], ALU.mult, ALU.mult)
            yield

    def load_tab(idx):
        mb, l4 = divmod(idx, 4)
        tb = tabs[idx % 3]
        K.dma(tb.ap[:], tab_d[mb, l4 * 4:(l4 + 1) * 4].rearrange("t p c m -> p t c m"), W=[tb[:]])

    def fourier_gen():
        K.dma(cfsf.ap[:], cfsf_d, W=[cfsf[:]])
        K.dma(wf_sb.ap[:], wf_d.rearrange("g c d -> c g d"), W=[wf_sb[:]])
        load_tab(0)
        load_tab(1)
        for g in range(4):
            psw = PS[2 + g % 2]
            mm(psw[:, 0:P], cfsf[:, 0:P], wf_sb[:, g, :])
            mm(psw[:, P:2 * P], cfsf[:, P:2 * P], wf_sb[:, g, :])
            cp("dve", CW[:, g, :], psw[:, 0:256])
        yield
        for lt in range(NLT):
            for gp in range(2):
                psv = PS[2 + (lt * 2 + gp) % 2]
                for gi in range(2):
                    g = gp * 2 + gi
                    mm(psv[:, gi * 256:(gi + 1) * 256], uT[g][:, lt * P:(lt + 1) * P], CW[:, g, :])
                cp("act" if gp == 0 else "dve", Vsb[:, lt, gp * 2:gp * 2 + 2, :],
                   hview(psv[:, :], "p (g c) -> p g c", c=256))
            if lt % 2 == 1:
                yield
        for mb in range(4):
            pss = [PS[4 + g] for g in range(4)]
            for l4 in range(4):
                idx = mb * 4 + l4
                if idx + 2 < 16:
                    load_tab(idx + 2)
                tb = tabs[idx % 3]
                for li in range(4):
                    lt = l4 * 4 + li
                    for g in range(4):
                        mm(pss[g][:, :], Vsb[:, lt, g, 0:P], tb[:, li, 0, :], start=(lt == 0), stop=False)
                        mm(pss[g][:, :], Vsb[:, lt, g, P:2 * P], tb[:, li, 1, :], start=False,
                           stop=(lt == NLT - 1))
                    yield
            for g in range(4):
                cp("act" if g % 2 == 0 else "dve", yfT[g][:, mb * 512:(mb + 1) * 512], pss[g][:, :])
            yield

    if stop_after == 4:
        for _ in yd_gen():
            pass
        dump("ydT", ydT[:, :, 0:512], [P, 4, 512], BF16)
        return finish()
    run_weighted([(yd_gen(), 1), (fourier_gen(), 2)])
    dump("ydT", ydT[:, :, 0:512], [P, 4, 512], BF16)
    dump("yfT", V(yfT[0], yfT_raw[:, :, 0:512]), [P, 4, 512], BF16)
    R1.reset()
    R6.reset(barrier=False)
    if stop_after == 5:
        return finish()

    R3.reset(barrier=False)
    R4.reset(barrier=False)
    X1_raw = R1.alloc([P, NLT, D_MODEL], F32, "X1")
    X1 = [Tile(X1_raw[:, l, :], f"X1_{l}") for l in range(NLT)]
    wo_raw = R3.alloc([P, 8, D_MODEL], BF16, "wo_bf")
    wo_bf = [Tile(wo_raw[:, kc, :], f"wo{kc}") for kc in range(8)]
    GT1 = R4.tile([P, D_MODEL], F32, "GT1")
    xr = [R4.tile([P, D_MODEL], F32, f"xr{i}") for i in range(2)]
    tmp5 = R4.tile([P, D_MODEL], F32, "tmp5")
    junk5 = R6.tile([P, D_MODEL], BF16, "junk5")
    dgs5 = [R6.tile([P, P], F32, f"dgs5_{i}") for i in range(2)]
    ssq5 = R6.tile([P, NLT], F32, "ssq5")
    wout_v = wout_d.rearrange("(kc p) n -> p kc n", p=P)
    for kc in range(8):
        K.dma(wo_bf[kc].ap, wout_v[:, kc, :], W=[wo_bf[kc][:]], queue="pool")
    make_GT(0, GT1, dgs5, PS[4], PS[5])
    mixT = [yfT[g] for g in range(4)] + [V(ydT, ydT_raw[:, h, :]) for h in range(4)]

    ssq5b = R6.tile([P, NLT], F32, "ssq5b")
    for l in range(NLT):
        K.dma(xr[l % 2].ap[:], x_d[l * P:(l + 1) * P, :], W=[xr[l % 2][:]])
        psA, psB = PS[(l % 4) * 2], PS[(l % 4) * 2 + 1]
        for half, psx in enumerate((psA, psB)):
            for kc in range(8):
                mm(psx[:, :], mixT[kc][:, l * P:(l + 1) * P], wo_bf[kc][:, half * 512:(half + 1) * 512],
                   start=(kc == 0), stop=(kc == 7))
        act(junk5[:, 0:512], psA[:, :], AF.Square, accum=ssq5[:, l:l + 1])
        act(junk5[:, 512:1024], psB[:, :], AF.Square, accum=ssq5b[:, l:l + 1])
        tt("dve", ssq5[:, l:l + 1], ssq5[:, l:l + 1], ssq5b[:, l:l + 1], ALU.add)
        ts("dve", ssq5[:, l:l + 1], ssq5[:, l:l + 1], 1.0 / D_MODEL, ALU.mult, EPS, ALU.add)
        tt("pool", ssq5[:, l:l + 1], ssq5[:, l:l + 1], c_mhalf, ALU.pow)
        stt(tmp5[:, 0:512], psA[:, :], ssq5[:, l:l + 1], GT1[:, 0:512], ALU.mult, ALU.mult)
        stt(tmp5[:, 512:1024], psB[:, :], ssq5[:, l:l + 1], GT1[:, 512:1024], ALU.mult, ALU.mult)
        tt("pool", X1[l][:], tmp5[:], xr[l % 2][:], ALU.add)
    dump("X1", V(X1[0], X1_raw[:, 0:4, :]), [P, 4, D_MODEL])
    R2.reset()
    R3.reset(barrier=False)
    R4.reset(barrier=False)
    R5.reset(barrier=False)
    R6.reset(barrier=False)
    if stop_after == 6:
        return finish()

    FF = Arena(nc, K, base + 80 * KB, avail - 80 * KB, "FF")
    actT_raw = FF.alloc([P, NFC, 1024], BF16, "actT")
    actT = [Tile(actT_raw[:, c, :], f"actT{c}") for c in range(NFC)]
    GT2 = FF.tile([P, D_MODEL], F32, "GT2")
    ssq6 = FF.tile([P, 32], F32, "ssq6")
    ssq6b = FF.tile([P, 32], F32, "ssq6b")
    dgs6 = [FF.tile([P, P], F32, f"dgs6_{i}") for i in range(2)]
    make_GT(1, GT2, dgs6, PS[4], PS[5])
    NWA = 11
    wdA_raw = FF.alloc([P, NWA, D_MODEL], BF16, "wdA")
    wdA = [Tile(wdA_raw[:, kc, :], f"wdA{kc}") for kc in range(NWA)]
    ff_mark = FF.cur
    NLOC = 9 * P
    for hf in range(2):
        FF.cur = ff_mark
        h2T = FF.tile([P, 8, NLOC], BF16, "h2T")
        wbf6 = [FF.tile([P, 8, 2, P], BF16, f"wbf6_{i}") for i in range(3)]
        PR, PC = 18, 66
        gpad = [FF.tile([P, PR, PC], BF16, f"gpad{i}") for i in range(2)]
        dg9 = [FF.tile([P, 9, P], BF16, f"dg9_{i}") for i in range(2)]
        sg = [FF.tile([P, 512], F32, f"sg{i}") for i in range(2)]
        xn6 = [FF.tile([P, D_MODEL], BF16, f"xn6_{i}") for i in range(2)]
        junk6 = FF.tile([P, D_MODEL], BF16, "junk6")
        for gp_ in gpad:
            memset("pool", gp_[:], 0.0)
        tile0 = 0 if hf == 0 else 7
        def f_stageA(i):
            l = tile0 + i
            col = hf * 16 + i
            norm_tile(X1[l][:], xn6[i % 2], ssq6[:, col:col + 1], ssq6b[:, col:col + 1], junk6)

        def f_stageB(i):
            transpose_mod(xn6[i % 2], PS[i % 2],
                          lambda kc: h2T[:, kc, i * P:(i + 1) * P],
                          lambda kc: sA2[:, kc:kc + 1],
                          lambda kc: modB[:, 8 + kc, 0:1], use_act=(i % 2 == 0))

        f_stageA(0)
        for i in range(9):
            if i + 1 < 9:
                f_stageA(i + 1)
            f_stageB(i)
        main0 = 0 if hf == 0 else 128
        halo0 = 1024 if hf == 0 else 64
        halo_row = 17 if hf == 0 else 0

        def load_wup(cc):
            w = wbf6[cc % 3]
            K.dma(w.ap[:], wup_d[cc], W=[w[:]], queue="pool")

        load_wup(0)
        load_wup(1)
        if hf == 0:
            for kc in range(NWA):
                K.dma(wdA[kc].ap, wdown_d[kc * P:(kc + 1) * P, :], W=[wdA[kc][:]], queue="pool")
        for cc in range(NFC):
            if cc + 2 < NFC:
                load_wup(cc + 2)
            wb = wbf6[cc % 3]
            dg = dg9[cc % 2]
            gp_ = gpad[cc % 2]
            for tap in range(9):
                ts("dve", dg[:, tap, :], IDN, dw_fm[:, tap * NFC + cc:tap * NFC + cc + 1], ALU.mult)
            psgs = [PS[2], PS[3]]
            for piece in range(2):
                psg = psgs[piece]
                t0 = main0 + piece * 512
                for kc in range(8):
                    mm(psg[:, :], wb[:, kc, 1, :], h2T[:, kc, t0:t0 + 512], start=(kc == 0), stop=(kc == 7))
                cp("act", gp_[:, 1 + piece * 8:9 + piece * 8, 1:65], hview(psg[:, :], "p (r c) -> p r c", c=64))
            psh = PS[4]
            for kc in range(8):
                mm(psh[:, 0:64], wb[:, kc, 1, :], h2T[:, kc, halo0:halo0 + 64], start=(kc == 0), stop=(kc == 7))
            cp("act", gp_[:, halo_row, 1:65], psh[:, 0:64])
            psvs = [PS[0], PS[1]]
            for piece in range(2):
                t0 = main0 + piece * 512
                for kc in range(8):
                    mm(psvs[piece][:, :], wb[:, kc, 0, :], h2T[:, kc, t0:t0 + 512], start=(kc == 0), stop=(kc == 7))
            for piece in range(2):
                psc = PS[5 + piece]
                R0_ = 1 + piece * 8
                for tap in range(9):
                    kh, kw = divmod(tap, 3)
                    mm(hview(psc[:, :], "p (r c) -> p r c", c=64), dg[:, tap, :],
                       gp_[:, R0_ + kh - 1:R0_ + kh - 1 + 8, kw:kw + 64], start=(tap == 0), stop=(tap == 8))
                act(sg[piece][:], psc[:, :], AF.Silu)
                tt("dve", actT[cc][:, piece * 512:(piece + 1) * 512], psvs[piece][:, :], sg[piece][:], ALU.mult)
        if hf == 0:
            dump("actT", V(actT[0], actT_raw[:, 0:4, 0:512]), [P, 4, 512], BF16)
        K.barrier()
        FF.cur = ff_mark
        wdB_raw = FF.alloc([P, NFC - NWA, D_MODEL], BF16, "wdB")
        wdB = [Tile(wdB_raw[:, kc, :], f"wdB{kc}") for kc in range(NFC - NWA)]
        wd_bf = wdA + wdB
        ot = [FF.tile([P, D_MODEL], F32, f"ot{i}") for i in range(2)]
        tmp6 = FF.tile([P, D_MODEL], F32, "tmp6")
        junk7 = FF.tile([P, D_MODEL], BF16, "junk7")
        sq7 = FF.tile([P, 16], F32, "sq7")
        sq7b = FF.tile([P, 16], F32, "sq7b")
        for kc in range(NWA, NFC):
            K.dma(wd_bf[kc].ap, wdown_d[kc * P:(kc + 1) * P, :], W=[wd_bf[kc][:]], queue="pool")
        for grp in range(2):
            for kc in range(NFC):
                for j in range(4):
                    i = grp * 4 + j
                    for half in range(2):
                        mm(PS[2 * j + half][:, :], actT[kc][:, i * P:(i + 1) * P],
                           wd_bf[kc][:, half * 512:(half + 1) * 512], start=(kc == 0), stop=(kc == NFC - 1))
            for j in range(4):
                i = grp * 4 + j
                l = hf * 8 + i
                psA, psB = PS[2 * j], PS[2 * j + 1]
                c1, c2 = sq7[:, i:i + 1], sq7b[:, i:i + 1]
                act(junk7[:, 0:512], psA[:, :], AF.Square, accum=c1)
                act(junk7[:, 512:1024], psB[:, :], AF.Square, accum=c2)
                tt("dve", c1, c1, c2, ALU.add)
                ts("dve", c1, c1, 1.0 / D_MODEL, ALU.mult, EPS, ALU.add)
                tt("pool", c1, c1, c_mhalf, ALU.pow)
                stt(tmp6[:, 0:512], psA[:, :], c1, GT2[:, 0:512], ALU.mult, ALU.mult)
                stt(tmp6[:, 512:1024], psB[:, :], c1, GT2[:, 512:1024], ALU.mult, ALU.mult)
                o_t = ot[i % 2]
                tt("pool", o_t[:], tmp6[:], X1[l][:], ALU.add)
                K.dma(out_d[l * P:(l + 1) * P, :], o_t.ap[:], R=[o_t[:]], is_output=True)
        K.barrier()
    return finish()


G_IDN, G_ONES, G_NEG1, G_MF, G_MB, G_CB, G_IND0, G_IND1, G_NM_SL, G_NM_SU, G_PM_LI, G_PM_UI = range(12)
NG = 12
BIG = 30000.0


def _gconst():
    t = np.arange(P)
    a = t[:, None]
    b = t[None, :]
    same = (a // 64) == (b // 64)
    g = np.zeros((NG, P, P), np.float32)
    g[G_IDN] = np.eye(P)
    g[G_ONES] = 1.0
    g[G_NEG1] = -1.0
    g[G_MF] = (same & (a <= b))
    g[G_MB] = (same & (a >= b))
    g[G_CB] = same
    g[G_IND0] = (a < 64) * np.ones((1, P))
    g[G_IND1] = (a >= 64) * np.ones((1, P))
    g[G_NM_SL] = np.where(same & (a > b), 0.0, -BIG)
    g[G_NM_SU] = np.where(same & (a < b), 0.0, -BIG)
    g[G_PM_LI] = np.where(same & (a >= b), 0.0, BIG)
    g[G_PM_UI] = np.where(same & (a <= b), 0.0, BIG)
    return np.ascontiguousarray(g.transpose(1, 0, 2))


def _dft_consts():
    L, Fd = SEQ, 128
    sc = 1.0 / np.sqrt(L * Fd)
    n = np.arange(Fd)
    ang = 2.0 * np.pi * ((n[:, None] * n[None, :]) % Fd) / Fd
    cfsf = np.concatenate([np.cos(ang) * sc, -np.sin(ang) * sc], axis=1).astype(np.float32)
    l = np.arange(L, dtype=np.int64)
    prod = (l[:, None] * l[None, :]) % L
    angL = 2.0 * np.pi * prod / L
    c = np.cos(angL).astype(np.float32)
    s = np.sin(angL).astype(np.float32)
    tab = np.stack([c, s], axis=0)
    tab = tab.reshape(2, 16, 128, 4, 512)
    tab = tab.transpose(3, 1, 2, 0, 4)
    return cfsf, np.ascontiguousarray(tab).astype(ml_dtypes.bfloat16)


_CACHE = {}


def _host_consts():
    if "c" not in _CACHE:
        cfsf, tab = _dft_consts()
        _CACHE["c"] = (_gconst(), cfsf, tab)
    return _CACHE["c"]


def _relayout_win(w):
    wp = np.zeros((D_MODEL, 21 * P), np.float32)
    wp[:, :IN_W] = w
    return np.ascontiguousarray(wp.reshape(8, P, 21, P).transpose(2, 1, 0, 3))


def _relayout_wup(w):
    return np.ascontiguousarray(w.reshape(8, P, 2, NFC, P).transpose(3, 1, 0, 2, 4))


def make_in_maps(inputs, n_cores=8):
    gconst, cfsf, tab = _host_consts()
    f = lambda a: np.ascontiguousarray(np.asarray(a, dtype=np.float32))
    x = f(inputs["x"]); c = f(inputs["c"]); ctx = f(inputs["ctx"]); c_ctx = f(inputs["c_ctx"])
    shared = {
        "w_ada": f(inputs["w_ada"][0]), "w_in": _relayout_win(f(inputs["w_in"][0])), "w_out": f(inputs["w_out"][0]),
        "w_up": _relayout_wup(f(inputs["w_up"][0])), "w_down": f(inputs["w_down"][0]),
        "convw": f(inputs["w_qkv_conv"][0]).reshape(36, P),
        "dwcw": f(inputs["w_dwc"][0]).reshape(198, P),
        "g_gdn": f(inputs["g_gdn"][0]).reshape(1, P),
        "a_log": f(inputs["a_log"][0]).reshape(8), "dt_bias": f(inputs["dt_bias"][0]).reshape(8),
        "w_fourier": f(inputs["w_fourier"][0]),
        "gconst": gconst, "cfsf": cfsf, "dft_tab": tab,
    }
    maps = []
    for b in range(n_cores):
        vecs = np.concatenate([
            f(inputs["b_ada"][0]).reshape(48, P), f(inputs["g_pre_mix"][0]).reshape(8, P),
            f(inputs["g_post_mix"][0]).reshape(8, P), f(inputs["g_pre_ffn"][0]).reshape(8, P),
            f(inputs["g_post_ffn"][0]).reshape(8, P), c[b].reshape(8, P), c_ctx.reshape(8, P)], axis=0)
        m = dict(shared)
        m["x"] = x[b]
        m["ctx"] = ctx[b]
        m["vecs"] = np.ascontiguousarray(vecs)
        maps.append(m)
    return maps


def kernel(**inputs):
    nc, _ = build_program()
    maps = make_in_maps(inputs, 8)
    res = run_bass_kernel_spmd(nc, maps, core_ids=list(range(8)))
    return np.stack([np.asarray(r["out"], dtype=np.float32) for r in res.results], axis=0)
```

```python
import os
import numpy as np
import ml_dtypes
import concourse.bass as bass
import concourse.mybir as mybir
from concourse.bass_utils import run_bass_kernel_spmd

F32 = mybir.dt.float32
BF16 = mybir.dt.bfloat16
AF = mybir.ActivationFunctionType
ALU = mybir.AluOpType

P = 128
D_MODEL = 1024
SEQ = 2048
CTX = 256
NTOK = SEQ + CTX
NT = NTOK // P
NLT = SEQ // P
IN_W = 2576
D_FF = 2816
NFC = D_FF // P
EPS = 1e-6
EPOCH = 12000
NDMA = 12


class Tile:
    def __init__(self, ap, name="", psum=False):
        self.ap = ap
        self.name = name
        self.w = None
        self.r = {}
        self.psum = psum

    def __getitem__(self, idx):
        return V(self, self.ap[idx])

    def v(self, ap):
        return V(self, ap)


class V:
    def __init__(self, tile, a):
        self.tile = tile
        self.a = a

    def __getitem__(self, idx):
        return V(self.tile, self.a[idx])

    def bc(self, shape):
        return V(self.tile, self.a.broadcast_to(list(shape)))

    def bitcast(self, dt):
        return V(self.tile, self.a.bitcast(dt))


class _Eng:
    def __init__(self, K, name, h):
        self.K = K
        self.name = name
        self.h = h
        self.sem = None
        self.cnt = 0
        self.seen = {}
        self.nsem = 0
        self.ninst = 0

    def newsem(self):
        self.sem = self.K.nc.alloc_semaphore(f"s_{self.name}_{self.nsem}")
        self.nsem += 1
        self.cnt = 0


class Kern:
    def __init__(self, nc):
        self.nc = nc
        self.E = {}
        for name, h in (("pe", nc.tensor), ("dve", nc.vector), ("act", nc.scalar),
                        ("pool", nc.gpsimd), ("sp", nc.sync)):
            e = _Eng(self, name, h)
            if name != "sp":
                e.newsem()
            self.E[name] = e
        self.dpool = [[nc.alloc_semaphore(f"s_dma_{i}"), 0] for i in range(NDMA)]
        self.drr = 0
        self.out_tokens = []
        self.swsems = []
        self.sw_done = 0

    def _deps(self, e, W, R):
        need = []
        for v in R:
            t = v.tile
            if t.w is not None:
                need.append((t.w, "raw"))
            if t.psum:
                for tok in t.r.values():
                    if tok[2] is not e:
                        need.append((tok, "rar"))
        for v in W:
            t = v.tile
            if t.w is not None:
                need.append((t.w, "waw"))
            for tok in t.r.values():
                need.append((tok, "war"))
        for (sem, val, src), kind in need:
            if src is e:
                if e.name == "pe":
                    continue
            if e.seen.get(id(sem), 0) >= val:
                continue
            e.h.wait_ge(sem, val)
            e.seen[id(sem)] = val

    def _post(self, tok, W, R):
        sem = tok[0]
        for v in R:
            v.tile.r[id(sem)] = tok
        for v in W:
            v.tile.w = tok
            v.tile.r = {}

    def emit(self, eng, fn, W=(), R=()):
        e = self.E[eng]
        self._deps(e, W, R)
        if e.cnt >= EPOCH:
            e.newsem()
        inst = fn(e.h)
        e.cnt += 1
        e.ninst += 1
        inst.then_inc(e.sem, 1)
        tok = (e.sem, e.cnt, e)
        self._post(tok, W, R)
        return tok

    def dma(self, out, in_, W=(), R=(), is_output=False, queue="sp"):
        e = self.E[queue]
        self._deps(e, W, R)
        if queue == "pool":
            sem = self.nc.alloc_semaphore(f"s_sw_{len(self.swsems)}")
            self.swsems.append(sem)
            e.h.dma_start(out=out, in_=in_).then_inc(sem, 16)
            tok = (sem, 16, None)
            self._post(tok, W, R)
            return tok
        slot = self.dpool[self.drr]
        self.drr = (self.drr + 1) % NDMA
        sem, n = slot
        if n > 0 and e.seen.get(id(sem), 0) < 16 * n:
            e.h.wait_ge(sem, 16 * n)
            e.seen[id(sem)] = 16 * n
        e.h.dma_start(out=out, in_=in_).then_inc(sem, 16)
        slot[1] = n + 1
        tok = (sem, 16 * (n + 1), None)
        self._post(tok, W, R)
        if is_output:
            self.out_tokens.append(tok)
        return tok

    def barrier(self):
        names = ["pe", "dve", "act", "pool"]
        for a in names:
            ea = self.E[a]
            for b in names + ["sp"]:
                if a == b:
                    continue
                eb = self.E[b]
                if b == "sp":
                    continue
                if eb.cnt == 0:
                    continue
                if ea.seen.get(id(eb.sem), 0) >= eb.cnt:
                    continue
                ea.h.wait_ge(eb.sem, eb.cnt)
                ea.seen[id(eb.sem)] = eb.cnt
            for sem, n in self.dpool:
                if n > 0 and ea.seen.get(id(sem), 0) < 16 * n:
                    ea.h.wait_ge(sem, 16 * n)
                    ea.seen[id(sem)] = 16 * n
            for sem in self.swsems[self.sw_done:]:
                if ea.seen.get(id(sem), 0) < 16:
                    ea.h.wait_ge(sem, 16)
                    ea.seen[id(sem)] = 16
        sp = self.E["sp"]
        for b in names:
            eb = self.E[b]
            if eb.cnt and sp.seen.get(id(eb.sem), 0) < eb.cnt:
                sp.h.wait_ge(eb.sem, eb.cnt)
                sp.seen[id(eb.sem)] = eb.cnt
        for sem, n in self.dpool:
            if n > 0 and sp.seen.get(id(sem), 0) < 16 * n:
                sp.h.wait_ge(sem, 16 * n)
                sp.seen[id(sem)] = 16 * n
        for sem in self.swsems[self.sw_done:]:
            if sp.seen.get(id(sem), 0) < 16:
                sp.h.wait_ge(sem, 16)
                sp.seen[id(sem)] = 16
        self.sw_done = len(self.swsems)

    def finish(self):
        sp = self.E["sp"]
        for sem, val, _ in self.out_tokens:
            if sp.seen.get(id(sem), 0) < val:
                sp.h.wait_ge(sem, val)
                sp.seen[id(sem)] = val


class Arena:
    def __init__(self, nc, K, start, size, name):
        self.nc = nc
        self.K = K
        self.start = start
        self.end = start + size
        self.cur = start
        self.name = name
        self.n = 0

    def alloc(self, shape, dt, name=None):
        esz = 4 if dt == F32 else 2
        nbytes = esz
        for s in shape[1:]:
            nbytes *= s
        off = (self.cur + 63) // 64 * 64
        if off + nbytes > self.end:
            raise RuntimeError(f"arena {self.name} overflow allocating {name} {shape}: "
                               f"{off - self.start}+{nbytes} > {self.end - self.start}")
        self.cur = off + nbytes
        self.n += 1
        return self.nc.alloc_sbuf_tensor_at(f"{self.name}_{name or 't'}_{self.n}", list(shape), dt, offset=off)

    def tile(self, shape, dt, name=None):
        return Tile(self.alloc(shape, dt, name), name or "t")

    def reset(self, barrier=True):
        if barrier:
            self.K.barrier()
        self.cur = self.start


KB = 1024


def build_program(debug=None, stop_after=None):
    debug = debug or []
    nc = bass.Bass("TRN2", target_bir_lowering=False)
    K = Kern(nc)
    base = (nc.sbuf_base + 63) // 64 * 64
    avail = nc.sbuf_top - base
    R0 = Arena(nc, K, base, 16 * KB, "R0")
    R1 = Arena(nc, K, base + 16 * KB, 64 * KB, "R1")
    R2 = Arena(nc, K, base + 80 * KB, 16 * KB, "R2")
    R3 = Arena(nc, K, base + 96 * KB, 16 * KB, "R3")
    R4 = Arena(nc, K, base + 112 * KB, 16 * KB, "R4")
    R5 = Arena(nc, K, base + 128 * KB, 54 * KB, "R5")
    R6 = Arena(nc, K, base + 182 * KB, avail - 182 * KB, "R6")
    dbg_out = {}

    def dram(name, shape, dt=F32, kind="ExternalInput"):
        return nc.dram_tensor(name, list(shape), dt, kind=kind).ap()

    x_d = dram("x", [SEQ, D_MODEL])
    ctx_d = dram("ctx", [CTX, D_MODEL])
    wada_d = dram("w_ada", [D_MODEL, 6 * D_MODEL])
    win_d = dram("w_in", [21, P, 8, P])
    wout_d = dram("w_out", [D_MODEL, D_MODEL])
    wup_d = dram("w_up", [NFC, P, 8, 2, P])
    wdown_d = dram("w_down", [D_FF, D_MODEL])
    vecs_d = dram("vecs", [96, P])
    convw_d = dram("convw", [36, P])
    dwcw_d = dram("dwcw", [198, P])
    ggdn_d = dram("g_gdn", [1, P])
    alog_d = dram("a_log", [8])
    dtb_d = dram("dt_bias", [8])
    wf_d = dram("w_fourier", [4, P, P])
    gconst_d = dram("gconst", [P, NG, P])
    cfsf_d = dram("cfsf", [P, 256])
    tab_d = dram("dft_tab", [4, 16, P, 2, 512], BF16)
    out_d = dram("out", [SEQ, D_MODEL], kind="ExternalOutput")

    PS = [Tile(nc.alloc_psum_tensor(f"ps{i}", [P, 512], F32), f"ps{i}", psum=True) for i in range(8)]

    def dump(name, view, shape, dt=F32):
        if name not in debug:
            return
        d = dram("dbg_" + name, shape, dt, kind="ExternalOutput")
        dbg_out[name] = (shape, dt)
        K.dma(d, view.a, R=[view], is_output=True)

    def finish():
        K.finish()
        return nc, dbg_out

    def mm(out, lhsT, rhs, start=True, stop=True):
        K.emit("pe", lambda e: e.matmul(out.a, lhsT=lhsT.a, rhs=rhs.a, start=start, stop=stop),
               W=[out], R=[lhsT, rhs])

    def tr(out, in_, ident):
        K.emit("pe", lambda e: e.transpose(out.a, in_.a, ident.a), W=[out], R=[in_, ident])

    def act(out, in_, func, scale=1.0, bias=None, accum=None):
        R = [in_]
        kw = {}
        if isinstance(scale, V):
            R.append(scale)
            kw["scale"] = scale.a
        else:
            kw["scale"] = float(scale)
        if bias is not None:
            R.append(bias)
            kw["bias"] = bias.a
        W = [out]
        if accum is not None:
            W.append(accum)
            kw["accum_out"] = accum.a
        K.emit("act", lambda e: e.activation(out.a, in_.a, func, **kw), W=W, R=R)

    def ts(eng, out, in0, s1, op0, s2=None, op1=None):
        R = [in0]
        a1 = s1.a if isinstance(s1, V) else float(s1)
        if isinstance(s1, V):
            R.append(s1)
        a2 = None
        if s2 is not None:
            a2 = s2.a if isinstance(s2, V) else float(s2)
            if isinstance(s2, V):
                R.append(s2)
        if op1 is None:
            K.emit(eng, lambda e: e.tensor_scalar(out.a, in0.a, a1, None, op0), W=[out], R=R)
        else:
            K.emit(eng, lambda e: e.tensor_scalar(out.a, in0.a, a1, a2, op0, op1), W=[out], R=R)

    def tt(eng, out, in0, in1, op):
        K.emit(eng, lambda e: e.tensor_tensor(out.a, in0.a, in1.a, op), W=[out], R=[in0, in1])

    def stt(out, in0, s, in1, op0, op1):
        R = [in0, in1]
        a = s.a if isinstance(s, V) else float(s)
        if isinstance(s, V):
            R.append(s)
        K.emit("dve", lambda e: e.scalar_tensor_tensor(out.a, in0.a, a, in1.a, op0, op1), W=[out], R=R)

    def cp(eng, out, in_):
        if eng == "act":
            act(out, in_, AF.Identity)
        else:
            K.emit(eng, lambda e: e.tensor_copy(out.a, in_.a), W=[out], R=[in_])

    def memset(eng, out, val):
        K.emit(eng, lambda e: e.memset(out.a, val), W=[out])

    def hview(view, pat, **kw):
        return V(view.tile, view.a.rearrange(pat, **kw))

    def bc_mid(view, n):
        sh = list(view.a.shape)
        return V(view.tile, view.a.unsqueeze(1).broadcast_to([sh[0], n, sh[1]]))

    def bc_last(view, n):
        sh = list(view.a.shape)
        return V(view.tile, view.a.unsqueeze(len(sh)).broadcast_to(sh + [n]))

    def ps4(ps):
        return hview(ps[:, :], "p (h d) -> p h d", d=P)

    GC = R0.tile([P, NG, P], F32, "gconst")
    K.dma(GC.ap[:], gconst_d, W=[GC[:]])

    def gcv(i):
        return GC[:, i, :]

    IDN = gcv(G_IDN)
    ONES = gcv(G_ONES)
    NEG1 = gcv(G_NEG1)
    idn_bf = R0.tile([P, P], BF16, "idn_bf")
    ones_bf = R0.tile([P, P], BF16, "ones_bf")
    cp("dve", idn_bf[:], IDN)
    cp("dve", ones_bf[:], ONES)
    cst = R0.tile([P, 8], F32, "cst")
    memset("dve", cst[:, 0:1], EPS)
    memset("dve", cst[:, 1:2], -0.5)
    memset("dve", cst[:, 2:3], 1.0)
    c_eps = cst[:, 0:1]
    c_mhalf = cst[:, 1:2]
    c_one = cst[:, 2:3]
    if stop_after == -3:
        dump("idn", idn_bf[:], [P, P], BF16)
        return finish()

    vec_fm = R0.tile([P, 96], F32, "vec_fm")
    cw_fm = R0.tile([P, 36], F32, "cw_fm")
    dw_fm = R0.tile([P, 198], F32, "dw_fm")
    gg_fm = R0.tile([P, 1], F32, "gg_fm")
    st_v = R1.tile([96, P], F32, "st_v")
    st_c = R1.tile([36, P], F32, "st_c")
    st_d1 = R1.tile([P, P], F32, "st_d1")
    st_d2 = R1.tile([70, P], F32, "st_d2")
    st_g = R1.tile([1, P], F32, "st_g")
    K.dma(st_v.ap[:], vecs_d, W=[st_v[:]])
    K.dma(st_c.ap[:], convw_d, W=[st_c[:]])
    K.dma(st_d1.ap[:], dwcw_d[0:128, :], W=[st_d1[:]])
    K.dma(st_d2.ap[:], dwcw_d[128:198, :], W=[st_d2[:]])
    K.dma(st_g.ap[:], ggdn_d, W=[st_g[:]])
    ps = PS[0]
    tr(ps[:, 0:96], st_v[:], GC[0:96, G_IDN, 0:96])
    cp("dve", vec_fm[:], ps[:, 0:96])
    ps = PS[1]
    tr(ps[:, 0:36], st_c[:], GC[0:36, G_IDN, 0:36])
    tr(ps[:, 36:37], st_g[:], GC[0:1, G_IDN, 0:1])
    cp("dve", cw_fm[:], ps[:, 0:36])
    cp("dve", gg_fm[:], ps[:, 36:37])
    ps = PS[2]
    tr(ps[:, 0:128], st_d1[:], IDN)
    tr(ps[:, 128:198], st_d2[:], GC[0:70, G_IDN, 0:70])
    cp("dve", dw_fm[:], ps[:, 0:198])
    if stop_after == -2:
        dump("dw_fm", dw_fm[:], [P, 198])
        return finish()

    ab_c = R0.tile([P, 16], F32, "ab_c")
    K.dma(ab_c.ap[:, 0:8], dtb_d.partition_broadcast(P), W=[ab_c[:]])
    K.dma(ab_c.ap[:, 8:16], alog_d.partition_broadcast(P), W=[ab_c[:]])
    act(ab_c[:, 8:16], ab_c[:, 8:16], AF.Exp)
    ts("dve", ab_c[:, 8:16], ab_c[:, 8:16], -1.0, ALU.mult)

    silu_c = R0.tile([P, 8, 2], F32, "silu_c")
    act(silu_c[:, :, 0], vec_fm[:, 80:88], AF.Silu)
    act(silu_c[:, :, 1], vec_fm[:, 88:96], AF.Silu)
    if stop_after == -1:
        dump("silu_c", silu_c[:], [P, 8, 2])
        dump("ab_c", ab_c[:], [P, 16])
        return finish()

    modA = R0.tile([P, 16, 2], F32, "modA")
    modB = R0.tile([P, 32, 2], F32, "modB")
    wada_v = wada_d.rearrange("(kc p) n -> p kc n", p=P)
    wa = [R5.tile([P, 8, 512], F32, f"wa{i}") for i in range(2)]
    sA1 = R0.tile([P, 8, 2], F32, "sA1")
    sA2 = R0.tile([P, 8], F32, "sA2")
    gtg = R0.tile([P, 16], F32, "gtg")

    def mod_groups(g0, g1, psm, jbase):
        for gidx in range(g0, g1):
            w = wa[gidx % 2]
            K.dma(w.ap[:], wada_v[:, :, gidx * 512:(gidx + 1) * 512], W=[w[:]])
            for cc in range(4):
                j = gidx * 4 + cc - jbase
                for kc in range(8):
                    mm(psm[:, 2 * j:2 * j + 2], w[:, kc, cc * 128:(cc + 1) * 128], silu_c[:, kc, :],
                       start=(kc == 0), stop=(kc == 7))
                if cc % 2 == 1:
                    yield

    for _ in mod_groups(0, 4, PS[3], 0):
        pass
    tt("dve", modA[:], hview(PS[3][:, 0:32], "p (j v) -> p j v", v=2), bc_last(vec_fm[:, 0:16], 2), ALU.add)
    stt(sA1[:], modA[:, 8:16, :], 1.0, bc_last(vec_fm[:, 48:56], 2), ALU.add, ALU.mult)

    def mod_rest():
        yield from mod_groups(4, 12, PS[2], 16)
        tt("dve", modB[:], hview(PS[2][:, 0:64], "p (j v) -> p j v", v=2), bc_last(vec_fm[:, 16:48], 2), ALU.add)
        stt(sA2[:], modB[:, 16:24, 0], 1.0, vec_fm[:, 64:72], ALU.add, ALU.mult)
        tt("dve", gtg[:, 0:8], modB[:, 0:8, 0], vec_fm[:, 56:64], ALU.mult)
        tt("dve", gtg[:, 8:16], modB[:, 24:32, 0], vec_fm[:, 72:80], ALU.mult)
        dump("mod", modB[:], [P, 32, 2])

    mod_rest_gen = mod_rest()
    if stop_after in (-0.5, 0):
        for _ in mod_rest_gen:
            pass
        return finish()

    def make_GT(j, GTt, dgs, psa, psb_):
        for half in range(2):
            psb = (psa, psb_)[half]
            for c4 in range(4):
                c = half * 4 + c4
                dgt = dgs[c % 2]
                ts("dve", dgt[:], IDN, gtg[:, j * 8 + c:j * 8 + c + 1], ALU.mult)
                mm(psb[:, c4 * 128:(c4 + 1) * 128], ONES, dgt[:])
            cp("act", GTt[:, half * 512:(half + 1) * 512], psb[:])

    R1.reset()
    if stop_after == 0:
        return finish()

    TOKBLK = [(0, 256)] + [(256 + 512 * b, 512) for b in range(4)]
    hT_raw = R1.alloc([P, 8, NTOK], BF16, "hT")
    hTb = [Tile(hT_raw[:, :, o:o + n], f"hT{b}") for b, (o, n) in enumerate(TOKBLK)]

    def hT_view(tok0, n):
        for b, (o, nn) in enumerate(TOKBLK):
            if o <= tok0 and tok0 + n <= o + nn:
                return hTb[b], tok0 - o
        raise AssertionError

    ssq = R0.tile([P, NT], F32, "ssq")
    rstd = R0.tile([P, NT], F32, "rstd")
    xts = [R1.tile([P, D_MODEL], F32, f"xt{i}") for i in range(3)]
    xns = [R1.tile([P, D_MODEL], BF16, f"xn{i}") for i in range(2)]
    junk = R1.tile([P, D_MODEL], BF16, "junk")

    def norm_tile(xt_v, xn, ssq_col, rstd_col, junk_t):
        act(junk_t[:], xt_v, AF.Square, accum=ssq_col)
        ts("dve", rstd_col, ssq_col, 1.0 / D_MODEL, ALU.mult, EPS, ALU.add)
        tt("pool", rstd_col, rstd_col, c_mhalf, ALU.pow)
        ts("dve", xn[:], xt_v, rstd_col, ALU.mult)

    def transpose_mod(xn, pst, dst_fn, sc_fn, bi_fn, use_act=True):
        pst_bf = V(pst, pst.ap[:].bitcast(BF16))
        for kc in range(8):
            tr(pst_bf[:, kc * 128:(kc + 1) * 128], xn[:, kc * 128:(kc + 1) * 128], idn_bf[:])
        for kc in range(8):
            if use_act:
                act(dst_fn(kc), pst_bf[:, kc * 128:(kc + 1) * 128], AF.Identity, scale=sc_fn(kc), bias=bi_fn(kc))
            else:
                ts("dve", dst_fn(kc), pst_bf[:, kc * 128:(kc + 1) * 128], sc_fn(kc), ALU.mult, bi_fn(kc), ALU.add)

    def p1_stageA(ti):
        xt = xts[ti % 3]
        src = ctx_d[ti * P:(ti + 1) * P, :] if ti < 2 else x_d[(ti - 2) * P:(ti - 1) * P, :]
        K.dma(xt.ap[:], src, W=[xt[:]])
        norm_tile(xt[:], xns[ti % 2], ssq[:, ti:ti + 1], rstd[:, ti:ti + 1], junk)

    def p1_stageB(ti):
        vsel = 1 if ti < 2 else 0
        hb, off = hT_view(ti * P, P)
        transpose_mod(xns[ti % 2], PS[ti % 2],
                      lambda kc: hb[:, kc, off:off + P],
                      lambda kc: sA1[:, kc, vsel:vsel + 1],
                      lambda kc: modA[:, kc, vsel:vsel + 1], use_act=(ti % 2 == 0))

    p1_stageA(0)
    for ti in range(NT):
        if ti + 1 < NT:
            p1_stageA(ti + 1)
        p1_stageB(ti)
        next(mod_rest_gen, None)
    for _ in mod_rest_gen:
        pass
    dump("hT", hTb[1][:], [P, 8, 512], BF16)
    if stop_after == 1:
        return finish()

    R5.reset(barrier=False)
    uT_raw = R3.alloc([P, 4, SEQ], BF16, "uT")
    szT_raw = R4.alloc([P, 4, SEQ], BF16, "szT")
    qT_raw = R2.alloc([P, 4, SEQ], BF16, "qT")
    kT_raw = R5.alloc([P, 4, NTOK], BF16, "kT")
    ktok_raw = R5.alloc([P, NT, 4, P], BF16, "ktok")
    vtok_raw = R5.alloc([P, NT, 4, P], BF16, "vtok")
    uT = [Tile(uT_raw[:, g, :], f"uT{g}") for g in range(4)]
    szT = Tile(szT_raw, "szT")
    qT = [Tile(qT_raw[:, g, :], f"qT{g}") for g in range(4)]
    kT = [Tile(kT_raw[:, g, :], f"kT{g}") for g in range(4)]
    ktok = Tile(ktok_raw, "ktok")
    vtok = Tile(vtok_raw, "vtok")
    abtok = R0.tile([P, NT, 16], F32, "abtok")

    K.barrier()
    R1.cur = R1.start + 8 * NTOK * 2
    wbf = [R6.tile([P, 8, P], BF16, f"wbf{i}") for i in range(3)]
    PADW = NTOK + 4
    pads = [R6.tile([P, PADW], BF16, f"pad{i}") for i in range(2)]
    dgc = [R1.tile([P, 3, P], BF16, f"dgc{i}") for i in range(2)]
    sil = R1.tile([P, NTOK], F32, "sil")
    sqs = [R1.tile([P, 512], BF16, f"sq{i}") for i in range(2)]
    lnt = R1.tile([P, NTOK], F32, "lnt")
    vT_tmp = V(sil, sil.ap[:].bitcast(BF16)[:, 0:NTOK])
    for pd in pads:
        memset("pool", pd[:], 0.0)

    def padcol(tok):
        return 1 + tok if tok < CTX else 3 + tok

    NCH = 21

    def load_w(j):
        w = wbf[j % 3]
        K.dma(w.ap[:], win_d[j], W=[w[:]], queue="pool")

    load_w(0)
    load_w(1)
    state = {"mmrr": 0, "evrr": 0}

    def kind_of(j):
        return "uqkvz"[j // 4] if j < 20 else "ab"

    def main_part(j):
        if j + 2 < NCH:
            load_w(j + 2)
        wb = wbf[j % 3]
        kind = kind_of(j)
        g = j % 4
        if kind == "ab":
            psab = PS[7]
            for t in range(NT):
                hb, off = hT_view(t * P, P)
                for kc in range(8):
                    mm(psab[:, t * 16:(t + 1) * 16], hb[:, kc, off:off + P], wb[:, kc, 0:16],
                       start=(kc == 0), stop=(kc == 7))
            cp("dve", abtok[:], hview(psab[:, 0:NT * 16], "p (t c) -> p t c", c=16))
            return
        blocks = list(range(5)) if kind in "kv" else list(range(1, 5))
        pad = pads[j % 2]
        if kind in "qkv":
            dg = dgc[j % 2]
            cj = j - 4
            for tap in range(3):
                ts("dve", dg[:, tap, :], IDN, cw_fm[:, tap * 12 + cj:tap * 12 + cj + 1], ALU.mult)
        yield
        for b_ in blocks:
            o, n = TOKBLK[b_]
            psa = PS[state["mmrr"] % 4]
            state["mmrr"] += 1
            for kc in range(8):
                mm(psa[:, 0:n], wb[:, kc, :], hTb[b_][:, kc, :], start=(kc == 0), stop=(kc == 7))
            if kind == "u":
                cp("dve", uT[g][:, o - CTX:o - CTX + n], psa[:, 0:n])
            elif kind == "z":
                act(szT[:, g, o - CTX:o - CTX + n], psa[:, 0:n], AF.Silu)
            else:
                pc = padcol(o)
                cp("act" if state["evrr"] % 2 == 0 else "dve", pad[:, pc:pc + n], psa[:, 0:n])
                state["evrr"] += 1
            yield

    def post_a(j):
        kind = kind_of(j)
        if kind not in "qkv":
            return
        blocks = list(range(5)) if kind in "kv" else list(range(1, 5))
        pad = pads[j % 2]
        dg = dgc[j % 2]
        for b_ in blocks:
            o, n = TOKBLK[b_]
            psc = PS[5 + (b_ % 2)]
            pc = padcol(o)
            for tap in range(3):
                mm(psc[:, 0:n], dg[:, tap, :], pad[:, pc + tap - 1:pc + tap - 1 + n],
                   start=(tap == 0), stop=(tap == 2))
            if kind == "v":
                act(vT_tmp[:, o:o + n], psc[:, 0:n], AF.Silu)
            else:
                act(sil[:, o:o + n], psc[:, 0:n], AF.Silu)
            yield
        if kind == "v":
            return
        for b_ in blocks:
            o, n = TOKBLK[b_]
            sq = sqs[b_ % 2]
            tt("dve", sq[:, 0:n], sil[:, o:o + n], sil[:, o:o + n], ALU.mult)
            psn = PS[4] if b_ % 2 == 0 else PS[7]
            mm(psn[:, 0:n], ones_bf[:], sq[:, 0:n])
            act(lnt[:, o:o + n], psn[:, 0:n], AF.Ln, bias=c_eps)
            yield

    def post_b(j):
        kind = kind_of(j)
        if kind not in "qkv":
            return
        g = j % 4
        blocks = list(range(5)) if kind in "kv" else list(range(1, 5))
        if kind == "v":
            src = vT_tmp
            dst = vtok
        else:
            scale = float(P) ** -0.5 if kind == "q" else 1.0
            for b_ in blocks:
                o, n = TOKBLK[b_]
                act(lnt[:, o:o + n], lnt[:, o:o + n], AF.Exp, scale=-0.5)
                if kind == "q":
                    stt(qT[g][:, o - CTX:o - CTX + n], sil[:, o:o + n], scale, lnt[:, o:o + n],
                        ALU.mult, ALU.mult)
                else:
                    stt(kT[g][:, o:o + n], sil[:, o:o + n], scale, lnt[:, o:o + n],
                        ALU.mult, ALU.mult)
                yield
            if kind == "q":
                return
            src = kT[g]
            dst = ktok
        for t0 in (0, 8, 16):
            nt = min(8, NT - t0)
            pst = PS[5 + ((t0 // 8) % 2)]
            pst_bf = V(pst, pst.ap[:].bitcast(BF16))
            for i in range(nt):
                t = t0 + i
                tr(pst_bf[:, i * P:(i + 1) * P], src[:, t * P:(t + 1) * P], idn_bf[:])
            cp("dve", dst[:, t0:t0 + nt, g, :], hview(pst_bf[:, 0:nt * P], "p (t d) -> p t d", d=P))
            yield

    def gen_or_empty(fn, *a):
        r_ = fn(*a)
        return r_ if r_ is not None else iter(())

    def chain2(*gens):
        for g_ in gens:
            yield from g_

    def rr_weighted(threads):
        live = [[g_, w_] for g_, w_ in threads]
        while live:
            for ent in list(live):
                for _ in range(ent[1]):
                    try:
                        next(ent[0])
                    except StopIteration:
                        live.remove(ent)
                        break

    for _ in gen_or_empty(main_part, 0):
        pass
    P2_ORDER = os.environ.get("P2_ORDER", "mab")
    for j in range(NCH):
        parts = {"a": gen_or_empty(post_a, j), "b": gen_or_empty(post_b, j),
                 "m": gen_or_empty(main_part, j + 1) if j + 1 < NCH else iter(())}
        for key in P2_ORDER:
            for _ in parts[key]:
                pass
    dump("uT", V(uT[0], uT_raw[:, :, 0:512]), [P, 4, 512], BF16)
    dump("szT", szT[:, :, 0:512], [P, 4, 512], BF16)
    dump("qT", V(qT[0], qT_raw[:, :, 0:512]), [P, 4, 512], BF16)
    dump("kT", V(kT[0], kT_raw[:, :, 0:768]), [P, 4, 768], BF16)
    dump("ktok", ktok[:, 0:4, :, :], [P, 4, 4, P], BF16)
    dump("vtok", vtok[:, 0:4, :, :], [P, 4, 4, P], BF16)
    dump("abtok", abtok[:], [P, NT, 16])
    R1.reset()
    R6.reset(barrier=False)
    if stop_after == 2:
        return finish()

    def sc8(name):
        return R6.tile([P, NT, 8], F32, name)

    beta = sc8("beta")
    gg = sc8("g")
    Gs = sc8("Gs")
    Gam = sc8("Gam")
    epsl = sc8("epsl")
    nGb = sc8("nGb")
    xs_t = sc8("xs")
    ax_t = sc8("ax")
    egl = R6.tile([P, NT, 2, 8], F32, "egl")
    act(beta[:], abtok[:, :, 0:8], AF.Sigmoid)
    tt("dve", xs_t[:], abtok[:, :, 8:16], bc_mid(ab_c[:, 0:8], NT), ALU.add)
    stt(ax_t[:], xs_t[:], -1.0, xs_t[:], ALU.mult, ALU.max)
    act(ax_t[:], ax_t[:], AF.Exp, scale=-1.0)
    act(ax_t[:], ax_t[:], AF.Ln, bias=c_one)
    stt(xs_t[:], xs_t[:], 0.0, ax_t[:], ALU.max, ALU.add)
    tt("dve", gg[:], xs_t[:], bc_mid(ab_c[:, 8:16], NT), ALU.mult)
    psG, psGL, psE = PS[0], PS[1], PS[2]
    for t in range(NT):
        for d in range(2):
            mm(psG[:, t * 8 + d * 4:t * 8 + d * 4 + 4], gcv(G_MF if d == 0 else G_MB), gg[:, t, d * 4:d * 4 + 4])
        mm(psGL[:, t * 8:(t + 1) * 8], gcv(G_CB), gg[:, t, :])
        for c in range(2):
            mm(psE[:, (t * 2 + c) * 8:(t * 2 + c + 1) * 8], gcv(G_IND0 if c == 0 else G_IND1), gg[:, t, :])
    cp("act", Gs[:], hview(psG[:, 0:NT * 8], "p (t c) -> p t c", c=8))
    act(Gam[:], Gs[:], AF.Exp)
    tt("dve", epsl[:], hview(psGL[:, 0:NT * 8], "p (t c) -> p t c", c=8), Gs[:], ALU.subtract)
    act(epsl[:], epsl[:], AF.Exp)
    ts("dve", nGb[:], Gam[:], -1.0, ALU.mult)
    act(egl[:], hview(psE[:, 0:NT * 16], "p (t c e) -> p t c e", c=2, e=8), AF.Exp)
    dump("g", gg[:], [P, NT, 8])
    dump("beta", beta[:], [P, NT, 8])
    dump("Gs", Gs[:], [P, NT, 8])
    dump("epsl", epsl[:], [P, NT, 8])
    dump("egl", egl[:], [P, NT, 2, 8])
    if stop_after == 2.5:
        return finish()

    F32R = mybir.dt.float32r
    cr = R6.tile([P, 2, P], F32, "ones_neg1_r")
    cp("dve", V(cr, cr.ap[:, 0, :].bitcast(F32R)), ONES)
    cp("dve", V(cr, cr.ap[:, 1, :].bitcast(F32R)), NEG1)
    ONES_r = cr[:, 0, :]
    NEG1_r = cr[:, 1, :]

    def rr(view):
        return V(view.tile, view.a.bitcast(F32R))

    def f4(name, reg=R1):
        return reg.tile([P, 4, P], F32, name)

    Ttmp = [dict(tA=f4(f"tA{d}"), Qa=f4(f"Qa{d}"), QTa=f4(f"QTa{d}"), Qb=f4(f"Qb{d}"), QTb=f4(f"QTb{d}"))
            for d in range(2)]
    TTs = [[f4(f"TT{d}{i}") for i in range(2)] for d in range(2)]
    attnTs = [[R6.tile([P, 4, P], BF16, f"attnT{d}{i}") for i in range(2)] for d in range(2)]
    qdecs = [[R6.tile([P, 4, P], BF16, f"qdec{d}{i}") for i in range(2)] for d in range(2)]
    rhs2s = [f4(f"rhs2_{d}") for d in range(2)]
    OF_raw = R1.alloc([P, NLT, 4, P], F32, "OF")
    OF = [Tile(OF_raw[:, l], f"OF{l}") for l in range(NLT)]
    vns = [R0.tile([P, 4, P], BF16, f"vn{d}") for d in range(2)]
    vdcs = [[(R0 if (d, c) != (1, 1) else R6).tile([P, 4, P], BF16, f"vd{d}{c}") for c in range(2)] for d in range(2)]
    S = [f4(f"S{d}", R6) for d in range(2)]
    Sb = [R6.tile([P, 4, P], BF16, f"Sb{d}") for d in range(2)]
    for d in range(2):
        memset("pool", S[d][:], 0.0)
        memset("pool", Sb[d][:], 0.0)
        memset("pool", vns[d][:], 0.0)
        memset("pool", Ttmp[d]["tA"][:], 0.0)
        cp("dve", rr(rhs2s[d][:]), Ttmp[d]["tA"][:])
        for c in range(2):
            memset("pool", vdcs[d][c][:], 0.0)
    TPSd = [[PS[0], PS[1], PS[2]], [PS[3], PS[4], PS[5]]]
    SPS = [PS[6], PS[7]]

    def t_phase(t, d, slot):
        lat = t >= 2
        l = t - 2
        Mdir = gcv(G_MF if d == 0 else G_MB)
        NMA = gcv(G_NM_SL if d == 0 else G_NM_SU)
        PMT = gcv(G_PM_UI if d == 0 else G_PM_LI)
        gcol = gg[:, t, d * 4:d * 4 + 4]
        bcol = beta[:, t, d * 4:d * 4 + 4]
        tk = slice(t * P, (t + 1) * P)
        TT = TTs[d][slot]
        aT = attnTs[d][slot]
        qd = qdecs[d][slot]
        tm = Ttmp[d]
        tA, Qa, QTa, Qb_, QTb_ = tm["tA"], tm["Qa"], tm["QTa"], tm["Qb"], tm["QTb"]
        dA, dT_ = QTb_, Qb_
        p0, p1, p2 = TPSd[d]
        tt("dve", rr(Qa[:]), bc_mid(Mdir, 4), bc_last(gcol, P), ALU.mult)
        mm(p0[:, :], rr(NEG1_r), hview(rr(Qa[:]), "p h d -> p (h d)"), start=True, stop=False)
        for h in range(4):
            mm(p0[:, h * P:(h + 1) * P], rr(Qa[:, h, :]), rr(ONES_r), start=False, stop=(h == 3))
        for h in range(4):
            mm(p1[:, h * P:(h + 1) * P], kT[h][:, tk], kT[h][:, tk])
        yield
        stt(rr(dA[:]), ps4(p0), 0.0, bc_mid(NMA, 4), ALU.min, ALU.add)
        act(rr(dA[:]), dA[:], AF.Exp)
        if lat:
            stt(rr(dT_[:]), ps4(p0), 0.0, bc_mid(PMT, 4), ALU.max, ALU.add)
            act(rr(dT_[:]), dT_[:], AF.Exp, scale=-1.0)
        yield
        tt("dve", tA[:], ps4(p1), dA[:], ALU.mult)
        tt("dve", rr(Qa[:]), tA[:], bc_last(bcol, P), ALU.mult)
        for h in range(4):
            tr(p0[:, h * P:(h + 1) * P], Qa[:, h, :], IDN)
        if lat:
            for h in range(4):
                mm(p1[:, h * P:(h + 1) * P], kT[h][:, tk], qT[h][:, l * P:(l + 1) * P])
        yield
        cp("act", rr(QTa[:]), ps4(p0))
        stt(rr(TT[:]), QTa[:], -1.0, bc_mid(IDN, 4), ALU.mult, ALU.add)
        if lat:
            tt("dve", aT[:], ps4(p1), dT_[:], ALU.mult)
            tt("dve", rr(QTb_[:]), bc_mid(IDN, 4), bc_last(Gam[:, t, d * 4:d * 4 + 4], P), ALU.mult)
        yield
        for h in range(4):
            mm(p0[:, h * P:(h + 1) * P], rr(QTa[:, h, :]), rr(Qa[:, h, :]))
        for h in range(4):
            mm(p1[:, h * P:(h + 1) * P], rr(Qa[:, h, :]), rr(QTa[:, h, :]))
        if lat:
            mm(p2[:, :], rr(ONES_r), hview(rr(QTb_[:]), "p h d -> p (h d)"))
        yield
        cp("act", rr(Qb_[:]), ps4(p0))
        cp("act", rr(QTb_[:]), ps4(p1))
        if lat:
            tt("dve", qd[:], ps4(p2), V(qT[0], qT_raw[:, :, l * P:(l + 1) * P]), ALU.mult)
        yield
        Q, QT = Qb_, QTb_
        for k in range(1, 6):
            Qn, QTn = (Qa, QTa) if k % 2 == 1 else (Qb_, QTb_)
            if k < 5:
                for h in range(4):
                    mm(p0[:, h * P:(h + 1) * P], rr(QT[:, h, :]), rr(Q[:, h, :]))
            if k < 4:
                for h in range(4):
                    mm(p1[:, h * P:(h + 1) * P], rr(Q[:, h, :]), rr(QT[:, h, :]))
            for h in range(4):
                mm(p2[:, h * P:(h + 1) * P], rr(Q[:, h, :]), rr(TT[:, h, :]))
            yield
            if k < 5:
                cp("act", rr(Qn[:]), ps4(p0))
            if k < 4:
                cp("act", rr(QTn[:]), ps4(p1))
            tt("dve", rr(TT[:]), ps4(p2), TT[:], ALU.add)
            yield
            Q, QT = Qn, QTn
        tt("dve", rr(TT[:]), TT[:], bc_last(bcol, P), ALU.mult)
        yield

    def scan(t, d, slot):
        lat = t >= 2
        l = t - 2
        psX = SPS[d]
        psS = psX
        TT = TTs[d][slot]
        aT = attnTs[d][slot]
        qd = qdecs[d][slot]
        rhs2, vn, vdc = rhs2s[d], vns[d], vdcs[d]
        tk = slice(t * P, (t + 1) * P)
        for c in ((0, 1) if d == 0 else (1, 0)):
            r0 = c * 64
            rows = slice(r0, r0 + 64)
            for h in range(4):
                mm(psX[:, h * P:(h + 1) * P], kT[h][:, tk], Sb[d][:, h, :])
            yield
            for h in range(4):
                stt(rr(rhs2[rows, h, :]), psX[rows, h * P:(h + 1) * P], nGb[rows, t, d * 4 + h:d * 4 + h + 1],
                    vtok[rows, t, h, :], ALU.mult, ALU.add)
            yield
            for h in range(4):
                mm(psX[:, h * P:(h + 1) * P], rr(TT[:, h, :]), rr(rhs2[:, h, :]))
            yield
            psX4 = hview(psX[rows, :], "p (h d) -> p h d", d=P)
            cp("act", vn[rows], psX4)
            for h in range(4):
                act(vdc[c][rows, h, :], psX[rows, h * P:(h + 1) * P], AF.Identity,
                    scale=epsl[rows, t, d * 4 + h:d * 4 + h + 1])
            yield
            if lat:
                for h in range(4):
                    mm(psX[:, h * P:(h + 1) * P], qd[:, h, :], Sb[d][:, h, :], start=True, stop=False)
                    mm(psX[:, h * P:(h + 1) * P], aT[:, h, :], vn[:, h, :], start=False, stop=True)
                yield
                if (d == 0) != (l < NLT // 2):
                    tt("dve", OF[l][rows], psX4, OF[l][rows], ALU.add)
                else:
                    cp("act", OF[l][rows], psX4)
            for h in range(4):
                mm(psS[:, h * P:(h + 1) * P], ktok[:, t, h, :], vdc[c][:, h, :])
            yield
            for h in range(4):
                stt(S[d][:, h, :], S[d][:, h, :], egl[:, t, c, d * 4 + h:d * 4 + h + 1],
                    psS[:, h * P:(h + 1) * P], ALU.mult, ALU.add)
            cp("act", Sb[d][:], S[d][:])
            yield

    orders = [list(range(NT)), [1, 0] + list(range(NT - 1, 1, -1))]
    nsteps = NT
    if "GDN_TILES" in os.environ:
        nsteps = int(os.environ["GDN_TILES"])
        orders[1] = [1, 0] + [None] * NT

    def chain(*gens):
        for g_ in gens:
            yield from g_

    def run_weighted(threads):
        live = [[g_, w_] for g_, w_ in threads]
        while live:
            for ent in list(live):
                for _ in range(ent[1]):
                    try:
                        next(ent[0])
                    except StopIteration:
                        live.remove(ent)
                        break

    def tgen(i, d):
        if i < nsteps and orders[d][i] is not None:
            return t_phase(orders[d][i], d, i % 2)
        return iter(())

    run_weighted([(tgen(0, 0), 1), (tgen(0, 1), 1)])
    TW = int(os.environ.get("GDN_TW", "1"))
    SW = int(os.environ.get("GDN_SW", "1"))
    for i in range(nsteps):
        threads = []
        for d in range(2):
            if orders[d][i] is not None:
                threads.append((scan(orders[d][i], d, i % 2), SW))
            threads.append((tgen(i + 1, d), TW))
        run_weighted(threads)
        if i == 1:
            dump("S_ctx", S[0][:], [P, 4, P])
        if i == 2:
            dump("TT", TTs[0][0][:], [P, 4, P])
    dump("OF", V(OF[0], OF_raw[:, 0:4]), [P, 4, 4, P])
    if stop_after == 3:
        return finish()

    K.barrier()
    R6.reset(barrier=False)
    R5.reset(barrier=False)
    R1.cur = R1.start
    ydT_raw = qT_raw
    ydT = Tile(ydT_raw, "ydT")
    ssq4 = R6.tile([P, NLT, 4], F32, "ssq4")
    junk4 = R6.tile([P, P], BF16, "junk4")
    on_bf = [R6.tile([P, 4, P], BF16, f"on{i}") for i in range(2)]
    Vsb = R5.tile([P, NLT, 4, 256], BF16, "Vsb")
    yfT_raw = R5.alloc([P, 4, SEQ], BF16, "yfT")
    yfT = [Tile(yfT_raw[:, g, :], f"yfT{g}") for g in range(4)]
    cfsf = R6.tile([P, 256], F32, "cfsf")
    wf_sb = R6.tile([P, 4, P], F32, "wf_sb")
    CW = R6.tile([P, 4, 256], BF16, "CW")
    tabs = [R1.tile([P, 4, 2, 512], BF16, f"tab{i}") for i in range(3)]

    def yd_gen():
        for l in range(NLT):
            for h in range(4):
                act(junk4[:], OF[l][:, h, :], AF.Square, accum=ssq4[:, l, h:h + 1])
            if l % 4 == 3:
                yield
        ts("dve", ssq4[:], ssq4[:], 1.0 / P, ALU.mult, EPS, ALU.add)
        tt("pool", ssq4[:], ssq4[:], V(cst, cst.ap[:, 1:2].unsqueeze(2).broadcast_to([P, NLT, 4])), ALU.pow)
        yield
        for l in range(NLT):
            on = on_bf[l % 2]
            for h in range(4):
                ts("dve", on[:, h, :], OF[l][:, h, :], ssq4[:, l, h:h + 1], ALU.mult)
            pst = PS[l % 2]
            pst_bf = V(pst, pst.ap[:].bitcast(BF16))
            for h in range(4):
                tr(pst_bf[:, h * P:(h + 1) * P], on[:, h, :], idn_bf[:])
            yield
            stt(ydT[:, :, l * P:(l + 1) * P], hview(pst_bf[:, 0:4 * P], "p (h d) -> p h d", d=P), gg_fm[:, 0:1],
                szT[:, :, l * P:(l + 1) * P], ALU.mult, ALU.mult)
            yield

    def load_tab(idx):
        mb, l4 = divmod(idx, 4)
        tb = tabs[idx % 3]
        K.dma(tb.ap[:], tab_d[mb, l4 * 4:(l4 + 1) * 4].rearrange("t p c m -> p t c m"), W=[tb[:]])

    def fourier_gen():
        K.dma(cfsf.ap[:], cfsf_d, W=[cfsf[:]])
        K.dma(wf_sb.ap[:], wf_d.rearrange("g c d -> c g d"), W=[wf_sb[:]])
        load_tab(0)
        load_tab(1)
        for g in range(4):
            psw = PS[2 + g % 2]
            mm(psw[:, 0:P], cfsf[:, 0:P], wf_sb[:, g, :])
            mm(psw[:, P:2 * P], cfsf[:, P:2 * P], wf_sb[:, g, :])
            cp("dve", CW[:, g, :], psw[:, 0:256])
        yield
        for lt in range(NLT):
            for gp in range(2):
                psv = PS[2 + (lt * 2 + gp) % 2]
                for gi in range(2):
                    g = gp * 2 + gi
                    mm(psv[:, gi * 256:(gi + 1) * 256], uT[g][:, lt * P:(lt + 1) * P], CW[:, g, :])
                cp("act" if gp == 0 else "dve", Vsb[:, lt, gp * 2:gp * 2 + 2, :],
                   hview(psv[:, :], "p (g c) -> p g c", c=256))
            if lt % 2 == 1:
                yield
        for mb in range(4):
            pss = [PS[4 + g] for g in range(4)]
            for l4 in range(4):
                idx = mb * 4 + l4
                if idx + 2 < 16:
                    load_tab(idx + 2)
                tb = tabs[idx % 3]
                for li in range(4):
                    lt = l4 * 4 + li
                    for g in range(4):
                        mm(pss[g][:, :], Vsb[:, lt, g, 0:P], tb[:, li, 0, :], start=(lt == 0), stop=False)
                        mm(pss[g][:, :], Vsb[:, lt, g, P:2 * P], tb[:, li, 1, :], start=False,
                           stop=(lt == NLT - 1))
                    yield
            for g in range(4):
                cp("act" if g % 2 == 0 else "dve", yfT[g][:, mb * 512:(mb + 1) * 512], pss[g][:, :])
            yield

    if stop_after == 4:
        for _ in yd_gen():
            pass
        dump("ydT", ydT[:, :, 0:512], [P, 4, 512], BF16)
        return finish()
    run_weighted([(yd_gen(), 1), (fourier_gen(), 2)])
    dump("ydT", ydT[:, :, 0:512], [P, 4, 512], BF16)
    dump("yfT", V(yfT[0], yfT_raw[:, :, 0:512]), [P, 4, 512], BF16)
    R1.reset()
    R6.reset(barrier=False)
    if stop_after == 5:
        return finish()

    R3.reset(barrier=False)
    R4.reset(barrier=False)
    X1_raw = R1.alloc([P, NLT, D_MODEL], F32, "X1")
    X1 = [Tile(X1_raw[:, l, :], f"X1_{l}") for l in range(NLT)]
    wo_raw = R3.alloc([P, 8, D_MODEL], BF16, "wo_bf")
    wo_bf = [Tile(wo_raw[:, kc, :], f"wo{kc}") for kc in range(8)]
    GT1 = R4.tile([P, D_MODEL], F32, "GT1")
    xr = [R4.tile([P, D_MODEL], F32, f"xr{i}") for i in range(2)]
    tmp5 = R4.tile([P, D_MODEL], F32, "tmp5")
    junk5 = R6.tile([P, D_MODEL], BF16, "junk5")
    dgs5 = [R6.tile([P, P], F32, f"dgs5_{i}") for i in range(2)]
    ssq5 = R6.tile([P, NLT], F32, "ssq5")
    wout_v = wout_d.rearrange("(kc p) n -> p kc n", p=P)
    K.dma(wo_raw[:], wout_v, W=[wo_bf[kc][:] for kc in range(8)], queue="pool")
    make_GT(0, GT1, dgs5, PS[4], PS[5])
    mixT = [yfT[g] for g in range(4)] + [V(ydT, ydT_raw[:, h, :]) for h in range(4)]

    ssq5b = R6.tile([P, NLT], F32, "ssq5b")
    for l in range(NLT):
        K.dma(xr[l % 2].ap[:], x_d[l * P:(l + 1) * P, :], W=[xr[l % 2][:]])
        psA, psB = PS[(l % 4) * 2], PS[(l % 4) * 2 + 1]
        for half, psx in enumerate((psA, psB)):
            for kc in range(8):
                mm(psx[:, :], mixT[kc][:, l * P:(l + 1) * P], wo_bf[kc][:, half * 512:(half + 1) * 512],
                   start=(kc == 0), stop=(kc == 7))
        act(junk5[:, 0:512], psA[:, :], AF.Square, accum=ssq5[:, l:l + 1])
        act(junk5[:, 512:1024], psB[:, :], AF.Square, accum=ssq5b[:, l:l + 1])
        tt("dve", ssq5[:, l:l + 1], ssq5[:, l:l + 1], ssq5b[:, l:l + 1], ALU.add)
        ts("dve", ssq5[:, l:l + 1], ssq5[:, l:l + 1], 1.0 / D_MODEL, ALU.mult, EPS, ALU.add)
        tt("pool", ssq5[:, l:l + 1], ssq5[:, l:l + 1], c_mhalf, ALU.pow)
        stt(tmp5[:, 0:512], psA[:, :], ssq5[:, l:l + 1], GT1[:, 0:512], ALU.mult, ALU.mult)
        stt(tmp5[:, 512:1024], psB[:, :], ssq5[:, l:l + 1], GT1[:, 512:1024], ALU.mult, ALU.mult)
        tt("pool", X1[l][:], tmp5[:], xr[l % 2][:], ALU.add)
    dump("X1", V(X1[0], X1_raw[:, 0:4, :]), [P, 4, D_MODEL])
    R2.reset()
    R3.reset(barrier=False)
    R4.reset(barrier=False)
    R5.reset(barrier=False)
    R6.reset(barrier=False)
    if stop_after == 6:
        return finish()

    FF = Arena(nc, K, base + 80 * KB, avail - 80 * KB, "FF")
    actT_raw = FF.alloc([P, NFC, 1024], BF16, "actT")
    actT = [Tile(actT_raw[:, c, :], f"actT{c}") for c in range(NFC)]
    GT2 = FF.tile([P, D_MODEL], F32, "GT2")
    ssq6 = FF.tile([P, 32], F32, "ssq6")
    ssq6b = FF.tile([P, 32], F32, "ssq6b")
    dgs6 = [FF.tile([P, P], F32, f"dgs6_{i}") for i in range(2)]
    make_GT(1, GT2, dgs6, PS[4], PS[5])
    NWA = 11
    wdA_raw = FF.alloc([P, NWA, D_MODEL], BF16, "wdA")
    wdA = [Tile(wdA_raw[:, kc, :], f"wdA{kc}") for kc in range(NWA)]
    ff_mark = FF.cur
    NLOC = 9 * P
    for hf in range(2):
        FF.cur = ff_mark
        h2T = FF.tile([P, 8, NLOC], BF16, "h2T")
        wbf6 = [FF.tile([P, 8, 2, P], BF16, f"wbf6_{i}") for i in range(3)]
        PR, PC = 18, 66
        gpad = [FF.tile([P, PR, PC], BF16, f"gpad{i}") for i in range(2)]
        dg9 = [FF.tile([P, 9, P], BF16, f"dg9_{i}") for i in range(2)]
        sg = [FF.tile([P, 512], F32, f"sg{i}") for i in range(2)]
        xn6 = [FF.tile([P, D_MODEL], BF16, f"xn6_{i}") for i in range(2)]
        junk6 = FF.tile([P, D_MODEL], BF16, "junk6")
        for gp_ in gpad:
            memset("pool", gp_[:], 0.0)
        tile0 = 0 if hf == 0 else 7
        def f_stageA(i):
            l = tile0 + i
            col = hf * 16 + i
            norm_tile(X1[l][:], xn6[i % 2], ssq6[:, col:col + 1], ssq6b[:, col:col + 1], junk6)

        def f_stageB(i):
            transpose_mod(xn6[i % 2], PS[i % 2],
                          lambda kc: h2T[:, kc, i * P:(i + 1) * P],
                          lambda kc: sA2[:, kc:kc + 1],
                          lambda kc: modB[:, 8 + kc, 0:1], use_act=(i % 2 == 0))

        f_stageA(0)
        for i in range(9):
            if i + 1 < 9:
                f_stageA(i + 1)
            f_stageB(i)
        main0 = 0 if hf == 0 else 128
        halo0 = 1024 if hf == 0 else 64
        halo_row = 17 if hf == 0 else 0

        def load_wup(cc):
            w = wbf6[cc % 3]
            K.dma(w.ap[:], wup_d[cc], W=[w[:]], queue="pool")

        load_wup(0)
        load_wup(1)
        if hf == 0:
            K.dma(wdA_raw[:], wdown_d[0:NWA * P, :].rearrange("(kc p) n -> p kc n", p=P),
                  W=[wdA[kc][:] for kc in range(NWA)], queue="pool")
        for cc in range(NFC):
            if cc + 2 < NFC:
                load_wup(cc + 2)
            wb = wbf6[cc % 3]
            dg = dg9[cc % 2]
            gp_ = gpad[cc % 2]
            for tap in range(9):
                ts("dve", dg[:, tap, :], IDN, dw_fm[:, tap * NFC + cc:tap * NFC + cc + 1], ALU.mult)
            psgs = [PS[2], PS[3]]
            for piece in range(2):
                psg = psgs[piece]
                t0 = main0 + piece * 512
                for kc in range(8):
                    mm(psg[:, :], wb[:, kc, 1, :], h2T[:, kc, t0:t0 + 512], start=(kc == 0), stop=(kc == 7))
                cp("act", gp_[:, 1 + piece * 8:9 + piece * 8, 1:65], hview(psg[:, :], "p (r c) -> p r c", c=64))
            psh = PS[4]
            for kc in range(8):
                mm(psh[:, 0:64], wb[:, kc, 1, :], h2T[:, kc, halo0:halo0 + 64], start=(kc == 0), stop=(kc == 7))
            cp("act", gp_[:, halo_row, 1:65], psh[:, 0:64])
            psvs = [PS[0], PS[1]]
            for piece in range(2):
                t0 = main0 + piece * 512
                for kc in range(8):
                    mm(psvs[piece][:, :], wb[:, kc, 0, :], h2T[:, kc, t0:t0 + 512], start=(kc == 0), stop=(kc == 7))
            for piece in range(2):
                psc = PS[5 + piece]
                R0_ = 1 + piece * 8
                for tap in range(9):
                    kh, kw = divmod(tap, 3)
                    mm(hview(psc[:, :], "p (r c) -> p r c", c=64), dg[:, tap, :],
                       gp_[:, R0_ + kh - 1:R0_ + kh - 1 + 8, kw:kw + 64], start=(tap == 0), stop=(tap == 8))
                act(sg[piece][:], psc[:, :], AF.Silu)
                tt("dve", actT[cc][:, piece * 512:(piece + 1) * 512], psvs[piece][:, :], sg[piece][:], ALU.mult)
        if hf == 0:
            dump("actT", V(actT[0], actT_raw[:, 0:4, 0:512]), [P, 4, 512], BF16)
        K.barrier()
        FF.cur = ff_mark
        wdB_raw = FF.alloc([P, NFC - NWA, D_MODEL], BF16, "wdB")
        wdB = [Tile(wdB_raw[:, kc, :], f"wdB{kc}") for kc in range(NFC - NWA)]
        wd_bf = wdA + wdB
        ot = [FF.tile([P, D_MODEL], F32, f"ot{i}") for i in range(2)]
        tmp6 = FF.tile([P, D_MODEL], F32, "tmp6")
        junk7 = FF.tile([P, D_MODEL], BF16, "junk7")
        sq7 = FF.tile([P, 16], F32, "sq7")
        sq7b = FF.tile([P, 16], F32, "sq7b")
        for k0, k1 in ((0, 4), (4, NFC - NWA)):
            K.dma(wdB_raw[:, k0:k1, :],
                  wdown_d[(NWA + k0) * P:(NWA + k1) * P, :].rearrange("(kc p) n -> p kc n", p=P),
                  W=[wdB[kc][:] for kc in range(k0, k1)], queue="pool")
        for grp in range(2):
            for kc in range(NFC):
                for j in range(4):
                    i = grp * 4 + j
                    for half in range(2):
                        mm(PS[2 * j + half][:, :], actT[kc][:, i * P:(i + 1) * P],
                           wd_bf[kc][:, half * 512:(half + 1) * 512], start=(kc == 0), stop=(kc == NFC - 1))
            for j in range(4):
                i = grp * 4 + j
                l = hf * 8 + i
                psA, psB = PS[2 * j], PS[2 * j + 1]
                c1, c2 = sq7[:, i:i + 1], sq7b[:, i:i + 1]
                act(junk7[:, 0:512], psA[:, :], AF.Square, accum=c1)
                act(junk7[:, 512:1024], psB[:, :], AF.Square, accum=c2)
                tt("dve", c1, c1, c2, ALU.add)
                ts("dve", c1, c1, 1.0 / D_MODEL, ALU.mult, EPS, ALU.add)
                tt("pool", c1, c1, c_mhalf, ALU.pow)
                stt(tmp6[:, 0:512], psA[:, :], c1, GT2[:, 0:512], ALU.mult, ALU.mult)
                stt(tmp6[:, 512:1024], psB[:, :], c1, GT2[:, 512:1024], ALU.mult, ALU.mult)
                o_t = ot[i % 2]
                tt("pool", o_t[:], tmp6[:], X1[l][:], ALU.add)
                K.dma(out_d[l * P:(l + 1) * P, :], o_t.ap[:], R=[o_t[:]], is_output=True)
        K.barrier()
    return finish()


G_IDN, G_ONES, G_NEG1, G_MF, G_MB, G_CB, G_IND0, G_IND1, G_NM_SL, G_NM_SU, G_PM_LI, G_PM_UI = range(12)
NG = 12
BIG = 30000.0


def _gconst():
    t = np.arange(P)
    a = t[:, None]
    b = t[None, :]
    same = (a // 64) == (b // 64)
    g = np.zeros((NG, P, P), np.float32)
    g[G_IDN] = np.eye(P)
    g[G_ONES] = 1.0
    g[G_NEG1] = -1.0
    g[G_MF] = (same & (a <= b))
    g[G_MB] = (same & (a >= b))
    g[G_CB] = same
    g[G_IND0] = (a < 64) * np.ones((1, P))
    g[G_IND1] = (a >= 64) * np.ones((1, P))
    g[G_NM_SL] = np.where(same & (a > b), 0.0, -BIG)
    g[G_NM_SU] = np.where(same & (a < b), 0.0, -BIG)
    g[G_PM_LI] = np.where(same & (a >= b), 0.0, BIG)
    g[G_PM_UI] = np.where(same & (a <= b), 0.0, BIG)
    return np.ascontiguousarray(g.transpose(1, 0, 2))


def _dft_consts():
    L, Fd = SEQ, 128
    sc = 1.0 / np.sqrt(L * Fd)
    n = np.arange(Fd)
    ang = 2.0 * np.pi * ((n[:, None] * n[None, :]) % Fd) / Fd
    cfsf = np.concatenate([np.cos(ang) * sc, -np.sin(ang) * sc], axis=1).astype(np.float32)
    l = np.arange(L, dtype=np.int64)
    prod = (l[:, None] * l[None, :]) % L
    angL = 2.0 * np.pi * prod / L
    c = np.cos(angL).astype(np.float32)
    s = np.sin(angL).astype(np.float32)
    tab = np.stack([c, s], axis=0)
    tab = tab.reshape(2, 16, 128, 4, 512)
    tab = tab.transpose(3, 1, 2, 0, 4)
    return cfsf, np.ascontiguousarray(tab).astype(ml_dtypes.bfloat16)


_CACHE = {}


def _host_consts():
    if "c" not in _CACHE:
        cfsf, tab = _dft_consts()
        _CACHE["c"] = (_gconst(), cfsf, tab)
    return _CACHE["c"]


def _relayout_win(w):
    wp = np.zeros((D_MODEL, 21 * P), np.float32)
    wp[:, :IN_W] = w
    return np.ascontiguousarray(wp.reshape(8, P, 21, P).transpose(2, 1, 0, 3))


def _relayout_wup(w):
    return np.ascontiguousarray(w.reshape(8, P, 2, NFC, P).transpose(3, 1, 0, 2, 4))


def make_in_maps(inputs, n_cores=8):
    gconst, cfsf, tab = _host_consts()
    f = lambda a: np.ascontiguousarray(np.asarray(a, dtype=np.float32))
    x = f(inputs["x"]); c = f(inputs["c"]); ctx = f(inputs["ctx"]); c_ctx = f(inputs["c_ctx"])
    shared = {
        "w_ada": f(inputs["w_ada"][0]), "w_in": _relayout_win(f(inputs["w_in"][0])), "w_out": f(inputs["w_out"][0]),
        "w_up": _relayout_wup(f(inputs["w_up"][0])), "w_down": f(inputs["w_down"][0]),
        "convw": f(inputs["w_qkv_conv"][0]).reshape(36, P),
        "dwcw": f(inputs["w_dwc"][0]).reshape(198, P),
        "g_gdn": f(inputs["g_gdn"][0]).reshape(1, P),
        "a_log": f(inputs["a_log"][0]).reshape(8), "dt_bias": f(inputs["dt_bias"][0]).reshape(8),
        "w_fourier": f(inputs["w_fourier"][0]),
        "gconst": gconst, "cfsf": cfsf, "dft_tab": tab,
    }
    maps = []
    for b in range(n_cores):
        vecs = np.concatenate([
            f(inputs["b_ada"][0]).reshape(48, P), f(inputs["g_pre_mix"][0]).reshape(8, P),
            f(inputs["g_post_mix"][0]).reshape(8, P), f(inputs["g_pre_ffn"][0]).reshape(8, P),
            f(inputs["g_post_ffn"][0]).reshape(8, P), c[b].reshape(8, P), c_ctx.reshape(8, P)], axis=0)
        m = dict(shared)
        m["x"] = x[b]
        m["ctx"] = ctx[b]
        m["vecs"] = np.ascontiguousarray(vecs)
        maps.append(m)
    return maps


def kernel(**inputs):
    nc, _ = build_program()
    maps = make_in_maps(inputs, 8)
    res = run_bass_kernel_spmd(nc, maps, core_ids=list(range(8)))
    return np.stack([np.asarray(r["out"], dtype=np.float32) for r in res.results], axis=0)
```

```python
import os
import numpy as np
import ml_dtypes
import concourse.bass as bass
import concourse.mybir as mybir
from concourse.bass_utils import run_bass_kernel_spmd

F32 = mybir.dt.float32
BF16 = mybir.dt.bfloat16
AF = mybir.ActivationFunctionType
ALU = mybir.AluOpType

P = 128
D_MODEL = 1024
SEQ = 2048
CTX = 256
NTOK = SEQ + CTX
NT = NTOK // P
NLT = SEQ // P
IN_W = 2576
D_FF = 2816
NFC = D_FF // P
EPS = 1e-6
EPOCH = 12000
NDMA = 12


class Tile:
    def __init__(self, ap, name="", psum=False):
        self.ap = ap
        self.name = name
        self.w = None
        self.r = {}
        self.psum = psum

    def __getitem__(self, idx):
        return V(self, self.ap[idx])

    def v(self, ap):
        return V(self, ap)


class V:
    def __init__(self, tile, a):
        self.tile = tile
        self.a = a

    def __getitem__(self, idx):
        return V(self.tile, self.a[idx])

    def bc(self, shape):
        return V(self.tile, self.a.broadcast_to(list(shape)))

    def bitcast(self, dt):
        return V(self.tile, self.a.bitcast(dt))


class _Eng:
    def __init__(self, K, name, h):
        self.K = K
        self.name = name
        self.h = h
        self.sem = None
        self.cnt = 0
        self.seen = {}
        self.nsem = 0
        self.ninst = 0

    def newsem(self):
        self.sem = self.K.nc.alloc_semaphore(f"s_{self.name}_{self.nsem}")
        self.nsem += 1
        self.cnt = 0


class Kern:
    def __init__(self, nc):
        self.nc = nc
        self.E = {}
        for name, h in (("pe", nc.tensor), ("dve", nc.vector), ("act", nc.scalar),
                        ("pool", nc.gpsimd), ("sp", nc.sync)):
            e = _Eng(self, name, h)
            if name != "sp":
                e.newsem()
            self.E[name] = e
        self.dpool = [[nc.alloc_semaphore(f"s_dma_{i}"), 0] for i in range(NDMA)]
        self.drr = 0
        self.out_tokens = []
        self.swsems = []
        self.sw_done = 0

    def _deps(self, e, W, R):
        need = []
        for v in R:
            t = v.tile
            if t.w is not None:
                need.append((t.w, "raw"))
            if t.psum:
                for tok in t.r.values():
                    if tok[2] is not e:
                        need.append((tok, "rar"))
        for v in W:
            t = v.tile
            if t.w is not None:
                need.append((t.w, "waw"))
            for tok in t.r.values():
                need.append((tok, "war"))
        for (sem, val, src), kind in need:
            if src is e:
                if e.name == "pe":
                    continue
            if e.seen.get(id(sem), 0) >= val:
                continue
            e.h.wait_ge(sem, val)
            e.seen[id(sem)] = val

    def _post(self, tok, W, R):
        sem = tok[0]
        for v in R:
            v.tile.r[id(sem)] = tok
        for v in W:
            v.tile.w = tok
            v.tile.r = {}

    def emit(self, eng, fn, W=(), R=()):
        e = self.E[eng]
        self._deps(e, W, R)
        if e.cnt >= EPOCH:
            e.newsem()
        inst = fn(e.h)
        e.cnt += 1
        e.ninst += 1
        inst.then_inc(e.sem, 1)
        tok = (e.sem, e.cnt, e)
        self._post(tok, W, R)
        return tok

    def dma(self, out, in_, W=(), R=(), is_output=False, queue="sp"):
        e = self.E[queue]
        self._deps(e, W, R)
        if queue == "pool":
            sem = self.nc.alloc_semaphore(f"s_sw_{len(self.swsems)}")
            self.swsems.append(sem)
            e.h.dma_start(out=out, in_=in_).then_inc(sem, 16)
            tok = (sem, 16, None)
            self._post(tok, W, R)
            return tok
        slot = self.dpool[self.drr]
        self.drr = (self.drr + 1) % NDMA
        sem, n = slot
        if n > 0 and e.seen.get(id(sem), 0) < 16 * n:
            e.h.wait_ge(sem, 16 * n)
            e.seen[id(sem)] = 16 * n
        e.h.dma_start(out=out, in_=in_).then_inc(sem, 16)
        slot[1] = n + 1
        tok = (sem, 16 * (n + 1), None)
        self._post(tok, W, R)
        if is_output:
            self.out_tokens.append(tok)
        return tok

    def barrier(self):
        names = ["pe", "dve", "act", "pool"]
        for a in names:
            ea = self.E[a]
            for b in names + ["sp"]:
                if a == b:
                    continue
                eb = self.E[b]
                if b == "sp":
                    continue
                if eb.cnt == 0:
                    continue
                if ea.seen.get(id(eb.sem), 0) >= eb.cnt:
                    continue
                ea.h.wait_ge(eb.sem, eb.cnt)
                ea.seen[id(eb.sem)] = eb.cnt
            for sem, n in self.dpool:
                if n > 0 and ea.seen.get(id(sem), 0) < 16 * n:
                    ea.h.wait_ge(sem, 16 * n)
                    ea.seen[id(sem)] = 16 * n
            for sem in self.swsems[self.sw_done:]:
                if ea.seen.get(id(sem), 0) < 16:
                    ea.h.wait_ge(sem, 16)
                    ea.seen[id(sem)] = 16
        sp = self.E["sp"]
        for b in names:
            eb = self.E[b]
            if eb.cnt and sp.seen.get(id(eb.sem), 0) < eb.cnt:
                sp.h.wait_ge(eb.sem, eb.cnt)
                sp.seen[id(eb.sem)] = eb.cnt
        for sem, n in self.dpool:
            if n > 0 and sp.seen.get(id(sem), 0) < 16 * n:
                sp.h.wait_ge(sem, 16 * n)
                sp.seen[id(sem)] = 16 * n
        for sem in self.swsems[self.sw_done:]:
            if sp.seen.get(id(sem), 0) < 16:
                sp.h.wait_ge(sem, 16)
                sp.seen[id(sem)] = 16
        self.sw_done = len(self.swsems)

    def finish(self):
        sp = self.E["sp"]
        for sem, val, _ in self.out_tokens:
            if sp.seen.get(id(sem), 0) < val:
                sp.h.wait_ge(sem, val)
                sp.seen[id(sem)] = val


class Arena:
    def __init__(self, nc, K, start, size, name):
        self.nc = nc
        self.K = K
        self.start = start
        self.end = start + size
        self.cur = start
        self.name = name
        self.n = 0

    def alloc(self, shape, dt, name=None):
        esz = 4 if dt == F32 else 2
        nbytes = esz
        for s in shape[1:]:
            nbytes *= s
        off = (self.cur + 63) // 64 * 64
        if off + nbytes > self.end:
            raise RuntimeError(f"arena {self.name} overflow allocating {name} {shape}: "
                               f"{off - self.start}+{nbytes} > {self.end - self.start}")
        self.cur = off + nbytes
        self.n += 1
        return self.nc.alloc_sbuf_tensor_at(f"{self.name}_{name or 't'}_{self.n}", list(shape), dt, offset=off)

    def tile(self, shape, dt, name=None):
        return Tile(self.alloc(shape, dt, name), name or "t")

    def reset(self, barrier=True):
        if barrier:
            self.K.barrier()
        self.cur = self.start


KB = 1024


def build_program(debug=None, stop_after=None):
    debug = debug or []
    nc = bass.Bass("TRN2", target_bir_lowering=False)
    K = Kern(nc)
    base = (nc.sbuf_base + 63) // 64 * 64
    avail = nc.sbuf_top - base
    R0 = Arena(nc, K, base, 16 * KB, "R0")
    R1 = Arena(nc, K, base + 16 * KB, 64 * KB, "R1")
    R2 = Arena(nc, K, base + 80 * KB, 16 * KB, "R2")
    R3 = Arena(nc, K, base + 96 * KB, 16 * KB, "R3")
    R4 = Arena(nc, K, base + 112 * KB, 16 * KB, "R4")
    R5 = Arena(nc, K, base + 128 * KB, 54 * KB, "R5")
    R6 = Arena(nc, K, base + 182 * KB, avail - 182 * KB, "R6")
    dbg_out = {}

    def dram(name, shape, dt=F32, kind="ExternalInput"):
        return nc.dram_tensor(name, list(shape), dt, kind=kind).ap()

    x_d = dram("x", [SEQ, D_MODEL])
    ctx_d = dram("ctx", [CTX, D_MODEL])
    wada_d = dram("w_ada", [D_MODEL, 6 * D_MODEL])
    win_d = dram("w_in", [21, P, 8, P])
    wout_d = dram("w_out", [D_MODEL, D_MODEL])
    wup_d = dram("w_up", [NFC, P, 8, 2, P])
    wdown_d = dram("w_down", [D_FF, D_MODEL])
    vecs_d = dram("vecs", [96, P])
    convw_d = dram("convw", [36, P])
    dwcw_d = dram("dwcw", [198, P])
    ggdn_d = dram("g_gdn", [1, P])
    alog_d = dram("a_log", [8])
    dtb_d = dram("dt_bias", [8])
    wf_d = dram("w_fourier", [4, P, P])
    gconst_d = dram("gconst", [P, NG, P])
    cfsf_d = dram("cfsf", [P, 256])
    tab_d = dram("dft_tab", [4, 16, P, 2, 512], BF16)
    out_d = dram("out", [SEQ, D_MODEL], kind="ExternalOutput")

    PS = [Tile(nc.alloc_psum_tensor(f"ps{i}", [P, 512], F32), f"ps{i}", psum=True) for i in range(8)]

    def dump(name, view, shape, dt=F32):
        if name not in debug:
            return
        d = dram("dbg_" + name, shape, dt, kind="ExternalOutput")
        dbg_out[name] = (shape, dt)
        K.dma(d, view.a, R=[view], is_output=True)

    def finish():
        K.finish()
        return nc, dbg_out

    def mm(out, lhsT, rhs, start=True, stop=True):
        K.emit("pe", lambda e: e.matmul(out.a, lhsT=lhsT.a, rhs=rhs.a, start=start, stop=stop),
               W=[out], R=[lhsT, rhs])

    def tr(out, in_, ident):
        K.emit("pe", lambda e: e.transpose(out.a, in_.a, ident.a), W=[out], R=[in_, ident])

    def act(out, in_, func, scale=1.0, bias=None, accum=None):
        R = [in_]
        kw = {}
        if isinstance(scale, V):
            R.append(scale)
            kw["scale"] = scale.a
        else:
            kw["scale"] = float(scale)
        if bias is not None:
            R.append(bias)
            kw["bias"] = bias.a
        W = [out]
        if accum is not None:
            W.append(accum)
            kw["accum_out"] = accum.a
        K.emit("act", lambda e: e.activation(out.a, in_.a, func, **kw), W=W, R=R)

    def ts(eng, out, in0, s1, op0, s2=None, op1=None):
        R = [in0]
        a1 = s1.a if isinstance(s1, V) else float(s1)
        if isinstance(s1, V):
            R.append(s1)
        a2 = None
        if s2 is not None:
            a2 = s2.a if isinstance(s2, V) else float(s2)
            if isinstance(s2, V):
                R.append(s2)
        if op1 is None:
            K.emit(eng, lambda e: e.tensor_scalar(out.a, in0.a, a1, None, op0), W=[out], R=R)
        else:
            K.emit(eng, lambda e: e.tensor_scalar(out.a, in0.a, a1, a2, op0, op1), W=[out], R=R)

    def tt(eng, out, in0, in1, op):
        K.emit(eng, lambda e: e.tensor_tensor(out.a, in0.a, in1.a, op), W=[out], R=[in0, in1])

    def stt(out, in0, s, in1, op0, op1):
        R = [in0, in1]
        a = s.a if isinstance(s, V) else float(s)
        if isinstance(s, V):
            R.append(s)
        K.emit("dve", lambda e: e.scalar_tensor_tensor(out.a, in0.a, a, in1.a, op0, op1), W=[out], R=R)

    def cp(eng, out, in_):
        if eng == "act":
            act(out, in_, AF.Identity)
        else:
            K.emit(eng, lambda e: e.tensor_copy(out.a, in_.a), W=[out], R=[in_])

    def memset(eng, out, val):
        K.emit(eng, lambda e: e.memset(out.a, val), W=[out])

    def hview(view, pat, **kw):
        return V(view.tile, view.a.rearrange(pat, **kw))

    def bc_mid(view, n):
        sh = list(view.a.shape)
        return V(view.tile, view.a.unsqueeze(1).broadcast_to([sh[0], n, sh[1]]))

    def bc_last(view, n):
        sh = list(view.a.shape)
        return V(view.tile, view.a.unsqueeze(len(sh)).broadcast_to(sh + [n]))

    def ps4(ps):
        return hview(ps[:, :], "p (h d) -> p h d", d=P)

    GC = R0.tile([P, NG, P], F32, "gconst")
    K.dma(GC.ap[:], gconst_d, W=[GC[:]])

    def gcv(i):
        return GC[:, i, :]

    IDN = gcv(G_IDN)
    ONES = gcv(G_ONES)
    NEG1 = gcv(G_NEG1)
    idn_bf = R0.tile([P, P], BF16, "idn_bf")
    ones_bf = R0.tile([P, P], BF16, "ones_bf")
    cp("dve", idn_bf[:], IDN)
    cp("dve", ones_bf[:], ONES)
    cst = R0.tile([P, 8], F32, "cst")
    memset("dve", cst[:, 0:1], EPS)
    memset("dve", cst[:, 1:2], -0.5)
    memset("dve", cst[:, 2:3], 1.0)
    c_eps = cst[:, 0:1]
    c_mhalf = cst[:, 1:2]
    c_one = cst[:, 2:3]
    if stop_after == -3:
        dump("idn", idn_bf[:], [P, P], BF16)
        return finish()

    vec_fm = R0.tile([P, 96], F32, "vec_fm")
    cw_fm = R0.tile([P, 36], F32, "cw_fm")
    dw_fm = R0.tile([P, 198], F32, "dw_fm")
    gg_fm = R0.tile([P, 1], F32, "gg_fm")
    st_v = R1.tile([96, P], F32, "st_v")
    st_c = R1.tile([36, P], F32, "st_c")
    st_d1 = R1.tile([P, P], F32, "st_d1")
    st_d2 = R1.tile([70, P], F32, "st_d2")
    st_g = R1.tile([1, P], F32, "st_g")
    K.dma(st_v.ap[:], vecs_d, W=[st_v[:]])
    K.dma(st_c.ap[:], convw_d, W=[st_c[:]])
    K.dma(st_d1.ap[:], dwcw_d[0:128, :], W=[st_d1[:]])
    K.dma(st_d2.ap[:], dwcw_d[128:198, :], W=[st_d2[:]])
    K.dma(st_g.ap[:], ggdn_d, W=[st_g[:]])
    ps = PS[0]
    tr(ps[:, 0:96], st_v[:], GC[0:96, G_IDN, 0:96])
    cp("dve", vec_fm[:], ps[:, 0:96])
    ps = PS[1]
    tr(ps[:, 0:36], st_c[:], GC[0:36, G_IDN, 0:36])
    tr(ps[:, 36:37], st_g[:], GC[0:1, G_IDN, 0:1])
    cp("dve", cw_fm[:], ps[:, 0:36])
    cp("dve", gg_fm[:], ps[:, 36:37])
    ps = PS[2]
    tr(ps[:, 0:128], st_d1[:], IDN)
    tr(ps[:, 128:198], st_d2[:], GC[0:70, G_IDN, 0:70])
    cp("dve", dw_fm[:], ps[:, 0:198])
    if stop_after == -2:
        dump("dw_fm", dw_fm[:], [P, 198])
        return finish()

    ab_c = R0.tile([P, 16], F32, "ab_c")
    K.dma(ab_c.ap[:, 0:8], dtb_d.partition_broadcast(P), W=[ab_c[:]])
    K.dma(ab_c.ap[:, 8:16], alog_d.partition_broadcast(P), W=[ab_c[:]])
    act(ab_c[:, 8:16], ab_c[:, 8:16], AF.Exp)
    ts("dve", ab_c[:, 8:16], ab_c[:, 8:16], -1.0, ALU.mult)

    silu_c = R0.tile([P, 8, 2], F32, "silu_c")
    act(silu_c[:, :, 0], vec_fm[:, 80:88], AF.Silu)
    act(silu_c[:, :, 1], vec_fm[:, 88:96], AF.Silu)
    if stop_after == -1:
        dump("silu_c", silu_c[:], [P, 8, 2])
        dump("ab_c", ab_c[:], [P, 16])
        return finish()

    modA = R0.tile([P, 16, 2], F32, "modA")
    modB = R0.tile([P, 32, 2], F32, "modB")
    wada_v = wada_d.rearrange("(kc p) n -> p kc n", p=P)
    wa = [R5.tile([P, 8, 512], F32, f"wa{i}") for i in range(2)]
    sA1 = R0.tile([P, 8, 2], F32, "sA1")
    sA2 = R0.tile([P, 8], F32, "sA2")
    gtg = R0.tile([P, 16], F32, "gtg")

    def mod_groups(g0, g1, psm, jbase):
        for gidx in range(g0, g1):
            w = wa[gidx % 2]
            K.dma(w.ap[:], wada_v[:, :, gidx * 512:(gidx + 1) * 512], W=[w[:]])
            for cc in range(4):
                j = gidx * 4 + cc - jbase
                for kc in range(8):
                    mm(psm[:, 2 * j:2 * j + 2], w[:, kc, cc * 128:(cc + 1) * 128], silu_c[:, kc, :],
                       start=(kc == 0), stop=(kc == 7))
                if cc % 2 == 1:
                    yield

    for _ in mod_groups(0, 4, PS[3], 0):
        pass
    tt("dve", modA[:], hview(PS[3][:, 0:32], "p (j v) -> p j v", v=2), bc_last(vec_fm[:, 0:16], 2), ALU.add)
    stt(sA1[:], modA[:, 8:16, :], 1.0, bc_last(vec_fm[:, 48:56], 2), ALU.add, ALU.mult)

    def mod_rest():
        yield from mod_groups(4, 12, PS[2], 16)
        tt("dve", modB[:], hview(PS[2][:, 0:64], "p (j v) -> p j v", v=2), bc_last(vec_fm[:, 16:48], 2), ALU.add)
        stt(sA2[:], modB[:, 16:24, 0], 1.0, vec_fm[:, 64:72], ALU.add, ALU.mult)
        tt("dve", gtg[:, 0:8], modB[:, 0:8, 0], vec_fm[:, 56:64], ALU.mult)
        tt("dve", gtg[:, 8:16], modB[:, 24:32, 0], vec_fm[:, 72:80], ALU.mult)
        dump("mod", modB[:], [P, 32, 2])

    mod_rest_gen = mod_rest()
    if stop_after in (-0.5, 0):
        for _ in mod_rest_gen:
            pass
        return finish()

    def make_GT(j, GTt, dgs, psa, psb_):
        for half in range(2):
            psb = (psa, psb_)[half]
            for c4 in range(4):
                c = half * 4 + c4
                dgt = dgs[c % 2]
                ts("dve", dgt[:], IDN, gtg[:, j * 8 + c:j * 8 + c + 1], ALU.mult)
                mm(psb[:, c4 * 128:(c4 + 1) * 128], ONES, dgt[:])
            cp("act", GTt[:, half * 512:(half + 1) * 512], psb[:])

    R1.reset()
    if stop_after == 0:
        return finish()

    TOKBLK = [(0, 256)] + [(256 + 512 * b, 512) for b in range(4)]
    hT_raw = R1.alloc([P, 8, NTOK], BF16, "hT")
    hTb = [Tile(hT_raw[:, :, o:o + n], f"hT{b}") for b, (o, n) in enumerate(TOKBLK)]

    def hT_view(tok0, n):
        for b, (o, nn) in enumerate(TOKBLK):
            if o <= tok0 and tok0 + n <= o + nn:
                return hTb[b], tok0 - o
        raise AssertionError

    ssq = R0.tile([P, NT], F32, "ssq")
    rstd = R0.tile([P, NT], F32, "rstd")
    xts = [R1.tile([P, D_MODEL], F32, f"xt{i}") for i in range(3)]
    xns = [R1.tile([P, D_MODEL], BF16, f"xn{i}") for i in range(2)]
    junk = R1.tile([P, D_MODEL], BF16, "junk")

    def norm_tile(xt_v, xn, ssq_col, rstd_col, junk_t):
        act(junk_t[:], xt_v, AF.Square, accum=ssq_col)
        ts("dve", rstd_col, ssq_col, 1.0 / D_MODEL, ALU.mult, EPS, ALU.add)
        tt("pool", rstd_col, rstd_col, c_mhalf, ALU.pow)
        ts("dve", xn[:], xt_v, rstd_col, ALU.mult)

    def transpose_mod(xn, pst, dst_fn, sc_fn, bi_fn, use_act=True):
        pst_bf = V(pst, pst.ap[:].bitcast(BF16))
        for kc in range(8):
            tr(pst_bf[:, kc * 128:(kc + 1) * 128], xn[:, kc * 128:(kc + 1) * 128], idn_bf[:])
        for kc in range(8):
            if use_act:
                act(dst_fn(kc), pst_bf[:, kc * 128:(kc + 1) * 128], AF.Identity, scale=sc_fn(kc), bias=bi_fn(kc))
            else:
                ts("dve", dst_fn(kc), pst_bf[:, kc * 128:(kc + 1) * 128], sc_fn(kc), ALU.mult, bi_fn(kc), ALU.add)

    def p1_stageA(ti):
        xt = xts[ti % 3]
        src = ctx_d[ti * P:(ti + 1) * P, :] if ti < 2 else x_d[(ti - 2) * P:(ti - 1) * P, :]
        K.dma(xt.ap[:], src, W=[xt[:]])
        norm_tile(xt[:], xns[ti % 2], ssq[:, ti:ti + 1], rstd[:, ti:ti + 1], junk)

    def p1_stageB(ti):
        vsel = 1 if ti < 2 else 0
        hb, off = hT_view(ti * P, P)
        transpose_mod(xns[ti % 2], PS[ti % 2],
                      lambda kc: hb[:, kc, off:off + P],
                      lambda kc: sA1[:, kc, vsel:vsel + 1],
                      lambda kc: modA[:, kc, vsel:vsel + 1], use_act=(ti % 2 == 0))

    p1_stageA(0)
    for ti in range(NT):
        if ti + 1 < NT:
            p1_stageA(ti + 1)
        p1_stageB(ti)
        next(mod_rest_gen, None)
    for _ in mod_rest_gen:
        pass
    dump("hT", hTb[1][:], [P, 8, 512], BF16)
    if stop_after == 1:
        return finish()

    R5.reset(barrier=False)
    uT_raw = R3.alloc([P, 4, SEQ], BF16, "uT")
    szT_raw = R4.alloc([P, 4, SEQ], BF16, "szT")
    qT_raw = R2.alloc([P, 4, SEQ], BF16, "qT")
    kT_raw = R5.alloc([P, 4, NTOK], BF16, "kT")
    ktok_raw = R5.alloc([P, NT, 4, P], BF16, "ktok")
    vtok_raw = R5.alloc([P, NT, 4, P], BF16, "vtok")
    uT = [Tile(uT_raw[:, g, :], f"uT{g}") for g in range(4)]
    szT = Tile(szT_raw, "szT")
    qT = [Tile(qT_raw[:, g, :], f"qT{g}") for g in range(4)]
    kT = [Tile(kT_raw[:, g, :], f"kT{g}") for g in range(4)]
    ktok = Tile(ktok_raw, "ktok")
    vtok = Tile(vtok_raw, "vtok")
    abtok = R0.tile([P, NT, 16], F32, "abtok")

    K.barrier()
    R1.cur = R1.start + 8 * NTOK * 2
    wbf = [R6.tile([P, 8, P], BF16, f"wbf{i}") for i in range(3)]
    PADW = NTOK + 4
    pads = [R6.tile([P, PADW], BF16, f"pad{i}") for i in range(2)]
    dgc = [R1.tile([P, 3, P], BF16, f"dgc{i}") for i in range(2)]
    sil = R1.tile([P, NTOK], F32, "sil")
    sqs = [R1.tile([P, 512], BF16, f"sq{i}") for i in range(2)]
    lnt = R1.tile([P, NTOK], F32, "lnt")
    vT_tmp = V(sil, sil.ap[:].bitcast(BF16)[:, 0:NTOK])
    for pd in pads:
        memset("pool", pd[:], 0.0)

    def padcol(tok):
        return 1 + tok if tok < CTX else 3 + tok

    NCH = 21

    def load_w(j):
        w = wbf[j % 3]
        K.dma(w.ap[:], win_d[j], W=[w[:]], queue="pool")

    load_w(0)
    load_w(1)
    state = {"mmrr": 0, "evrr": 0}

    def kind_of(j):
        return "uqkvz"[j // 4] if j < 20 else "ab"

    def main_part(j):
        if j + 2 < NCH:
            load_w(j + 2)
        wb = wbf[j % 3]
        kind = kind_of(j)
        g = j % 4
        if kind == "ab":
            psab = PS[7]
            for t in range(NT):
                hb, off = hT_view(t * P, P)
                for kc in range(8):
                    mm(psab[:, t * 16:(t + 1) * 16], hb[:, kc, off:off + P], wb[:, kc, 0:16],
                       start=(kc == 0), stop=(kc == 7))
            cp("dve", abtok[:], hview(psab[:, 0:NT * 16], "p (t c) -> p t c", c=16))
            return
        blocks = list(range(5)) if kind in "kv" else list(range(1, 5))
        pad = pads[j % 2]
        if kind in "qkv":
            dg = dgc[j % 2]
            cj = j - 4
            for tap in range(3):
                ts("dve", dg[:, tap, :], IDN, cw_fm[:, tap * 12 + cj:tap * 12 + cj + 1], ALU.mult)
        yield
        for b_ in blocks:
            o, n = TOKBLK[b_]
            psa = PS[state["mmrr"] % 4]
            state["mmrr"] += 1
            for kc in range(8):
                mm(psa[:, 0:n], wb[:, kc, :], hTb[b_][:, kc, :], start=(kc == 0), stop=(kc == 7))
            if kind == "u":
                cp("dve", uT[g][:, o - CTX:o - CTX + n], psa[:, 0:n])
            elif kind == "z":
                act(szT[:, g, o - CTX:o - CTX + n], psa[:, 0:n], AF.Silu)
            else:
                pc = padcol(o)
                cp("act" if state["evrr"] % 2 == 0 else "dve", pad[:, pc:pc + n], psa[:, 0:n])
                state["evrr"] += 1
            yield

    def post_a(j):
        kind = kind_of(j)
        if kind not in "qkv":
            return
        blocks = list(range(5)) if kind in "kv" else list(range(1, 5))
        pad = pads[j % 2]
        dg = dgc[j % 2]
        for b_ in blocks:
            o, n = TOKBLK[b_]
            psc = PS[5 + (b_ % 2)]
            pc = padcol(o)
            for tap in range(3):
                mm(psc[:, 0:n], dg[:, tap, :], pad[:, pc + tap - 1:pc + tap - 1 + n],
                   start=(tap == 0), stop=(tap == 2))
            if kind == "v":
                act(vT_tmp[:, o:o + n], psc[:, 0:n], AF.Silu)
            else:
                act(sil[:, o:o + n], psc[:, 0:n], AF.Silu)
            yield
        if kind == "v":
            return
        for b_ in blocks:
            o, n = TOKBLK[b_]
            sq = sqs[b_ % 2]
            tt("dve", sq[:, 0:n], sil[:, o:o + n], sil[:, o:o + n], ALU.mult)
            psn = PS[4] if b_ % 2 == 0 else PS[7]
            mm(psn[:, 0:n], ones_bf[:], sq[:, 0:n])
            act(lnt[:, o:o + n], psn[:, 0:n], AF.Ln, bias=c_eps)
            yield

    def post_b(j):
        kind = kind_of(j)
        if kind not in "qkv":
            return
        g = j % 4
        blocks = list(range(5)) if kind in "kv" else list(range(1, 5))
        if kind == "v":
            src = vT_tmp
            dst = vtok
        else:
            scale = float(P) ** -0.5 if kind == "q" else 1.0
            for b_ in blocks:
                o, n = TOKBLK[b_]
                act(lnt[:, o:o + n], lnt[:, o:o + n], AF.Exp, scale=-0.5)
                if kind == "q":
                    stt(qT[g][:, o - CTX:o - CTX + n], sil[:, o:o + n], scale, lnt[:, o:o + n],
                        ALU.mult, ALU.mult)
                else:
                    stt(kT[g][:, o:o + n], sil[:, o:o + n], scale, lnt[:, o:o + n],
                        ALU.mult, ALU.mult)
                yield
            if kind == "q":
                return
            src = kT[g]
            dst = ktok
        for t0 in (0, 8, 16):
            nt = min(8, NT - t0)
            pst = PS[5 + ((t0 // 8) % 2)]
            pst_bf = V(pst, pst.ap[:].bitcast(BF16))
            for i in range(nt):
                t = t0 + i
                tr(pst_bf[:, i * P:(i + 1) * P], src[:, t * P:(t + 1) * P], idn_bf[:])
            cp("dve", dst[:, t0:t0 + nt, g, :], hview(pst_bf[:, 0:nt * P], "p (t d) -> p t d", d=P))
            yield

    def gen_or_empty(fn, *a):
        r_ = fn(*a)
        return r_ if r_ is not None else iter(())

    def chain2(*gens):
        for g_ in gens:
            yield from g_

    def rr_weighted(threads):
        live = [[g_, w_] for g_, w_ in threads]
        while live:
            for ent in list(live):
                for _ in range(ent[1]):
                    try:
                        next(ent[0])
                    except StopIteration:
                        live.remove(ent)
                        break

    for _ in gen_or_empty(main_part, 0):
        pass
    P2_ORDER = os.environ.get("P2_ORDER", "mab")
    for j in range(NCH):
        parts = {"a": gen_or_empty(post_a, j), "b": gen_or_empty(post_b, j),
                 "m": gen_or_empty(main_part, j + 1) if j + 1 < NCH else iter(())}
        for key in P2_ORDER:
            if key.isdigit():
                for _ in range(int(key)):
                    next(parts["m"], None)
                continue
            for _ in parts[key]:
                pass
    dump("uT", V(uT[0], uT_raw[:, :, 0:512]), [P, 4, 512], BF16)
    dump("szT", szT[:, :, 0:512], [P, 4, 512], BF16)
    dump("qT", V(qT[0], qT_raw[:, :, 0:512]), [P, 4, 512], BF16)
    dump("kT", V(kT[0], kT_raw[:, :, 0:768]), [P, 4, 768], BF16)
    dump("ktok", ktok[:, 0:4, :, :], [P, 4, 4, P], BF16)
    dump("vtok", vtok[:, 0:4, :, :], [P, 4, 4, P], BF16)
    dump("abtok", abtok[:], [P, NT, 16])
    R1.reset()
    R6.reset(barrier=False)
    if stop_after == 2:
        return finish()

    def sc8(name):
        return R6.tile([P, NT, 8], F32, name)

    beta = sc8("beta")
    gg = sc8("g")
    Gs = sc8("Gs")
    Gam = sc8("Gam")
    epsl = sc8("epsl")
    nGb = sc8("nGb")
    xs_t = sc8("xs")
    ax_t = sc8("ax")
    egl = R6.tile([P, NT, 2, 8], F32, "egl")
    act(beta[:], abtok[:, :, 0:8], AF.Sigmoid)
    tt("dve", xs_t[:], abtok[:, :, 8:16], bc_mid(ab_c[:, 0:8], NT), ALU.add)
    stt(ax_t[:], xs_t[:], -1.0, xs_t[:], ALU.mult, ALU.max)
    act(ax_t[:], ax_t[:], AF.Exp, scale=-1.0)
    act(ax_t[:], ax_t[:], AF.Ln, bias=c_one)
    stt(xs_t[:], xs_t[:], 0.0, ax_t[:], ALU.max, ALU.add)
    tt("dve", gg[:], xs_t[:], bc_mid(ab_c[:, 8:16], NT), ALU.mult)
    psG, psGL, psE = PS[0], PS[1], PS[2]
    for t in range(NT):
        for d in range(2):
            mm(psG[:, t * 8 + d * 4:t * 8 + d * 4 + 4], gcv(G_MF if d == 0 else G_MB), gg[:, t, d * 4:d * 4 + 4])
        mm(psGL[:, t * 8:(t + 1) * 8], gcv(G_CB), gg[:, t, :])
        for c in range(2):
            mm(psE[:, (t * 2 + c) * 8:(t * 2 + c + 1) * 8], gcv(G_IND0 if c == 0 else G_IND1), gg[:, t, :])
    cp("act", Gs[:], hview(psG[:, 0:NT * 8], "p (t c) -> p t c", c=8))
    act(Gam[:], Gs[:], AF.Exp)
    tt("dve", epsl[:], hview(psGL[:, 0:NT * 8], "p (t c) -> p t c", c=8), Gs[:], ALU.subtract)
    act(epsl[:], epsl[:], AF.Exp)
    ts("dve", nGb[:], Gam[:], -1.0, ALU.mult)
    act(egl[:], hview(psE[:, 0:NT * 16], "p (t c e) -> p t c e", c=2, e=8), AF.Exp)
    dump("g", gg[:], [P, NT, 8])
    dump("beta", beta[:], [P, NT, 8])
    dump("Gs", Gs[:], [P, NT, 8])
    dump("epsl", epsl[:], [P, NT, 8])
    dump("egl", egl[:], [P, NT, 2, 8])
    if stop_after == 2.5:
        return finish()

    F32R = mybir.dt.float32r
    cr = R6.tile([P, 2, P], F32, "ones_neg1_r")
    cp("dve", V(cr, cr.ap[:, 0, :].bitcast(F32R)), ONES)
    cp("dve", V(cr, cr.ap[:, 1, :].bitcast(F32R)), NEG1)
    ONES_r = cr[:, 0, :]
    NEG1_r = cr[:, 1, :]

    def rr(view):
        return V(view.tile, view.a.bitcast(F32R))

    def f4(name, reg=R1):
        return reg.tile([P, 4, P], F32, name)

    Ttmp = [dict(tA=f4(f"tA{d}"), Qa=f4(f"Qa{d}"), QTa=f4(f"QTa{d}"), Qb=f4(f"Qb{d}"), QTb=f4(f"QTb{d}"))
            for d in range(2)]
    TTs = [[f4(f"TT{d}{i}") for i in range(2)] for d in range(2)]
    attnTs = [[R6.tile([P, 4, P], BF16, f"attnT{d}{i}") for i in range(2)] for d in range(2)]
    qdecs = [[R6.tile([P, 4, P], BF16, f"qdec{d}{i}") for i in range(2)] for d in range(2)]
    rhs2s = [f4(f"rhs2_{d}") for d in range(2)]
    OF_raw = R1.alloc([P, NLT, 4, P], F32, "OF")
    OF = [Tile(OF_raw[:, l], f"OF{l}") for l in range(NLT)]
    vns = [R0.tile([P, 4, P], BF16, f"vn{d}") for d in range(2)]
    vdcs = [[(R0 if (d, c) != (1, 1) else R6).tile([P, 4, P], BF16, f"vd{d}{c}") for c in range(2)] for d in range(2)]
    S = [f4(f"S{d}", R6) for d in range(2)]
    Sb = [R6.tile([P, 4, P], BF16, f"Sb{d}") for d in range(2)]
    for d in range(2):
        memset("pool", S[d][:], 0.0)
        memset("pool", Sb[d][:], 0.0)
        memset("pool", vns[d][:], 0.0)
        memset("pool", Ttmp[d]["tA"][:], 0.0)
        cp("dve", rr(rhs2s[d][:]), Ttmp[d]["tA"][:])
        for c in range(2):
            memset("pool", vdcs[d][c][:], 0.0)
    TPSd = [[PS[0], PS[1], PS[2]], [PS[3], PS[4], PS[5]]]
    SPS = [PS[6], PS[7]]

    def t_phase(t, d, slot):
        lat = t >= 2
        l = t - 2
        Mdir = gcv(G_MF if d == 0 else G_MB)
        NMA = gcv(G_NM_SL if d == 0 else G_NM_SU)
        PMT = gcv(G_PM_UI if d == 0 else G_PM_LI)
        gcol = gg[:, t, d * 4:d * 4 + 4]
        bcol = beta[:, t, d * 4:d * 4 + 4]
        tk = slice(t * P, (t + 1) * P)
        TT = TTs[d][slot]
        aT = attnTs[d][slot]
        qd = qdecs[d][slot]
        tm = Ttmp[d]
        tA, Qa, QTa, Qb_, QTb_ = tm["tA"], tm["Qa"], tm["QTa"], tm["Qb"], tm["QTb"]
        dA, dT_ = QTb_, Qb_
        p0, p1, p2 = TPSd[d]
        tt("dve", rr(Qa[:]), bc_mid(Mdir, 4), bc_last(gcol, P), ALU.mult)
        mm(p0[:, :], rr(NEG1_r), hview(rr(Qa[:]), "p h d -> p (h d)"), start=True, stop=False)
        for h in range(4):
            mm(p0[:, h * P:(h + 1) * P], rr(Qa[:, h, :]), rr(ONES_r), start=False, stop=(h == 3))
        for h in range(4):
            mm(p1[:, h * P:(h + 1) * P], kT[h][:, tk], kT[h][:, tk])
        yield
        stt(rr(dA[:]), ps4(p0), 0.0, bc_mid(NMA, 4), ALU.min, ALU.add)
        act(rr(dA[:]), dA[:], AF.Exp)
        if lat:
            stt(rr(dT_[:]), ps4(p0), 0.0, bc_mid(PMT, 4), ALU.max, ALU.add)
            act(rr(dT_[:]), dT_[:], AF.Exp, scale=-1.0)
        yield
        tt("dve", tA[:], ps4(p1), dA[:], ALU.mult)
        tt("dve", rr(Qa[:]), tA[:], bc_last(bcol, P), ALU.mult)
        for h in range(4):
            tr(p0[:, h * P:(h + 1) * P], Qa[:, h, :], IDN)
        if lat:
            for h in range(4):
                mm(p1[:, h * P:(h + 1) * P], kT[h][:, tk], qT[h][:, l * P:(l + 1) * P])
        yield
        cp("act", rr(QTa[:]), ps4(p0))
        stt(rr(TT[:]), QTa[:], -1.0, bc_mid(IDN, 4), ALU.mult, ALU.add)
        if lat:
            tt("dve", aT[:], ps4(p1), dT_[:], ALU.mult)
            tt("dve", rr(QTb_[:]), bc_mid(IDN, 4), bc_last(Gam[:, t, d * 4:d * 4 + 4], P), ALU.mult)
        yield
        for h in range(4):
            mm(p0[:, h * P:(h + 1) * P], rr(QTa[:, h, :]), rr(Qa[:, h, :]))
        for h in range(4):
            mm(p1[:, h * P:(h + 1) * P], rr(Qa[:, h, :]), rr(QTa[:, h, :]))
        if lat:
            mm(p2[:, :], rr(ONES_r), hview(rr(QTb_[:]), "p h d -> p (h d)"))
        yield
        cp("act", rr(Qb_[:]), ps4(p0))
        cp("act", rr(QTb_[:]), ps4(p1))
        if lat:
            tt("dve", qd[:], ps4(p2), V(qT[0], qT_raw[:, :, l * P:(l + 1) * P]), ALU.mult)
        yield
        Q, QT = Qb_, QTb_
        for k in range(1, 6):
            Qn, QTn = (Qa, QTa) if k % 2 == 1 else (Qb_, QTb_)
            if k < 5:
                for h in range(4):
                    mm(p0[:, h * P:(h + 1) * P], rr(QT[:, h, :]), rr(Q[:, h, :]))
            if k < 4:
                for h in range(4):
                    mm(p1[:, h * P:(h + 1) * P], rr(Q[:, h, :]), rr(QT[:, h, :]))
            for h in range(4):
                mm(p2[:, h * P:(h + 1) * P], rr(Q[:, h, :]), rr(TT[:, h, :]))
            yield
            if k < 5:
                cp("act", rr(Qn[:]), ps4(p0))
            if k < 4:
                cp("act", rr(QTn[:]), ps4(p1))
            tt("dve", rr(TT[:]), ps4(p2), TT[:], ALU.add)
            yield
            Q, QT = Qn, QTn
        tt("dve", rr(TT[:]), TT[:], bc_last(bcol, P), ALU.mult)
        yield

    def scan(t, d, slot):
        lat = t >= 2
        l = t - 2
        psX = SPS[d]
        psS = psX
        TT = TTs[d][slot]
        aT = attnTs[d][slot]
        qd = qdecs[d][slot]
        rhs2, vn, vdc = rhs2s[d], vns[d], vdcs[d]
        tk = slice(t * P, (t + 1) * P)
        for c in ((0, 1) if d == 0 else (1, 0)):
            r0 = c * 64
            rows = slice(r0, r0 + 64)
            for h in range(4):
                mm(psX[:, h * P:(h + 1) * P], kT[h][:, tk], Sb[d][:, h, :])
            yield
            for h in range(4):
                stt(rr(rhs2[rows, h, :]), psX[rows, h * P:(h + 1) * P], nGb[rows, t, d * 4 + h:d * 4 + h + 1],
                    vtok[rows, t, h, :], ALU.mult, ALU.add)
            yield
            for h in range(4):
                mm(psX[:, h * P:(h + 1) * P], rr(TT[:, h, :]), rr(rhs2[:, h, :]))
            yield
            psX4 = hview(psX[rows, :], "p (h d) -> p h d", d=P)
            cp("act", vn[rows], psX4)
            for h in range(4):
                act(vdc[c][rows, h, :], psX[rows, h * P:(h + 1) * P], AF.Identity,
                    scale=epsl[rows, t, d * 4 + h:d * 4 + h + 1])
            yield
            if lat:
                for h in range(4):
                    mm(psX[:, h * P:(h + 1) * P], qd[:, h, :], Sb[d][:, h, :], start=True, stop=False)
                    mm(psX[:, h * P:(h + 1) * P], aT[:, h, :], vn[:, h, :], start=False, stop=True)
                yield
                if (d == 0) != (l < NLT // 2):
                    tt("dve", OF[l][rows], psX4, OF[l][rows], ALU.add)
                else:
                    cp("act", OF[l][rows], psX4)
            for h in range(4):
                mm(psS[:, h * P:(h + 1) * P], ktok[:, t, h, :], vdc[c][:, h, :])
            yield
            for h in range(4):
                stt(S[d][:, h, :], S[d][:, h, :], egl[:, t, c, d * 4 + h:d * 4 + h + 1],
                    psS[:, h * P:(h + 1) * P], ALU.mult, ALU.add)
            cp("act", Sb[d][:], S[d][:])
            yield

    orders = [list(range(NT)), [1, 0] + list(range(NT - 1, 1, -1))]
    nsteps = NT
    if "GDN_TILES" in os.environ:
        nsteps = int(os.environ["GDN_TILES"])
        orders[1] = [1, 0] + [None] * NT

    def chain(*gens):
        for g_ in gens:
            yield from g_

    def run_weighted(threads):
        live = [[g_, w_] for g_, w_ in threads]
        while live:
            for ent in list(live):
                for _ in range(ent[1]):
                    try:
                        next(ent[0])
                    except StopIteration:
                        live.remove(ent)
                        break

    def tgen(i, d):
        if i < nsteps and orders[d][i] is not None:
            return t_phase(orders[d][i], d, i % 2)
        return iter(())

    run_weighted([(tgen(0, 0), 1), (tgen(0, 1), 1)])
    TW = int(os.environ.get("GDN_TW", "1"))
    SW = int(os.environ.get("GDN_SW", "1"))
    for i in range(nsteps):
        sc_ = [(scan(orders[d][i], d, i % 2), SW) for d in range(2) if orders[d][i] is not None]
        tg_ = [(tgen(i + 1, d), TW) for d in range(2)]
        GO = os.environ.get("GDN_ORDER", "sTsT")
        if GO == "sTsT":
            threads = [x for pair in zip(sc_ + [None] * (2 - len(sc_)), tg_) for x in pair if x is not None]
        elif GO == "TTss":
            threads = tg_ + sc_
        elif GO == "ssTT":
            threads = sc_ + tg_
        else:
            threads = [x for pair in zip(tg_, sc_ + [None] * (2 - len(sc_))) for x in pair if x is not None]
        run_weighted(threads)
        if i == 1:
            dump("S_ctx", S[0][:], [P, 4, P])
        if i == 2:
            dump("TT", TTs[0][0][:], [P, 4, P])
    dump("OF", V(OF[0], OF_raw[:, 0:4]), [P, 4, 4, P])
    if stop_after == 3:
        return finish()

    K.barrier()
    R6.reset(barrier=False)
    R5.reset(barrier=False)
    R1.cur = R1.start
    ydT_raw = qT_raw
    ydT = Tile(ydT_raw, "ydT")
    ssq4 = R6.tile([P, NLT, 4], F32, "ssq4")
    junk4 = R6.tile([P, P], BF16, "junk4")
    on_bf = [R6.tile([P, 4, P], BF16, f"on{i}") for i in range(2)]
    Vsb = R5.tile([P, NLT, 4, 256], BF16, "Vsb")
    yfT_raw = R5.alloc([P, 4, SEQ], BF16, "yfT")
    yfT = [Tile(yfT_raw[:, g, :], f"yfT{g}") for g in range(4)]
    cfsf = R6.tile([P, 256], F32, "cfsf")
    wf_sb = R6.tile([P, 4, P], F32, "wf_sb")
    CW = R6.tile([P, 4, 256], BF16, "CW")
    tabs = [R1.tile([P, 4, 2, 512], BF16, f"tab{i}") for i in range(3)]

    def yd_gen():
        for l in range(NLT):
            for h in range(4):
                act(junk4[:], OF[l][:, h, :], AF.Square, accum=ssq4[:, l, h:h + 1])
            if l % 4 == 3:
                yield
        ts("dve", ssq4[:], ssq4[:], 1.0 / P, ALU.mult, EPS, ALU.add)
        tt("pool", ssq4[:], ssq4[:], V(cst, cst.ap[:, 1:2].unsqueeze(2).broadcast_to([P, NLT, 4])), ALU.pow)
        yield
        for l in range(NLT):
            on = on_bf[l % 2]
            for h in range(4):
                ts("dve", on[:, h, :], OF[l][:, h, :], ssq4[:, l, h:h + 1], ALU.mult)
            pst = PS[l % 2]
            pst_bf = V(pst, pst.ap[:].bitcast(BF16))
            for h in range(4):
                tr(pst_bf[:, h * P:(h + 1) * P], on[:, h, :], idn_bf[:])
            yield
            stt(ydT[:, :, l * P:(l + 1) * P], hview(pst_bf[:, 0:4 * P], "p (h d) -> p h d", d=P), gg_fm[:, 0:1],
                szT[:, :, l * P:(l + 1) * P], ALU.mult, ALU.mult)
            yield

    def load_tab(idx):
        mb, l4 = divmod(idx, 4)
        tb = tabs[idx % 3]
        K.dma(tb.ap[:], tab_d[mb, l4 * 4:(l4 + 1) * 4].rearrange("t p c m -> p t c m"), W=[tb[:]])

    def fourier_gen():
        K.dma(cfsf.ap[:], cfsf_d, W=[cfsf[:]])
        K.dma(wf_sb.ap[:], wf_d.rearrange("g c d -> c g d"), W=[wf_sb[:]])
        load_tab(0)
        load_tab(1)
        for g in range(4):
            psw = PS[2 + g % 2]
            mm(psw[:, 0:P], cfsf[:, 0:P], wf_sb[:, g, :])
            mm(psw[:, P:2 * P], cfsf[:, P:2 * P], wf_sb[:, g, :])
            cp("dve", CW[:, g, :], psw[:, 0:256])
        yield
        for lt in range(NLT):
            for gp in range(2):
                psv = PS[2 + (lt * 2 + gp) % 2]
                for gi in range(2):
                    g = gp * 2 + gi
                    mm(psv[:, gi * 256:(gi + 1) * 256], uT[g][:, lt * P:(lt + 1) * P], CW[:, g, :])
                cp("act" if gp == 0 else "dve", Vsb[:, lt, gp * 2:gp * 2 + 2, :],
                   hview(psv[:, :], "p (g c) -> p g c", c=256))
            if lt % 2 == 1:
                yield
        for mb in range(4):
            pss = [PS[4 + g] for g in range(4)]
            for l4 in range(4):
                idx = mb * 4 + l4
                if idx + 2 < 16:
                    load_tab(idx + 2)
                tb = tabs[idx % 3]
                for li in range(4):
                    lt = l4 * 4 + li
                    for g in range(4):
                        mm(pss[g][:, :], Vsb[:, lt, g, 0:P], tb[:, li, 0, :], start=(lt == 0), stop=False)
                        mm(pss[g][:, :], Vsb[:, lt, g, P:2 * P], tb[:, li, 1, :], start=False,
                           stop=(lt == NLT - 1))
                    yield
            for g in range(4):
                cp("act" if g % 2 == 0 else "dve", yfT[g][:, mb * 512:(mb + 1) * 512], pss[g][:, :])
            yield

    if stop_after == 4:
        for _ in yd_gen():
            pass
        dump("ydT", ydT[:, :, 0:512], [P, 4, 512], BF16)
        return finish()
    run_weighted([(yd_gen(), 1), (fourier_gen(), 2)])
    dump("ydT", ydT[:, :, 0:512], [P, 4, 512], BF16)
    dump("yfT", V(yfT[0], yfT_raw[:, :, 0:512]), [P, 4, 512], BF16)
    R1.reset()
    R6.reset(barrier=False)
    if stop_after == 5:
        return finish()

    R3.reset(barrier=False)
    R4.reset(barrier=False)
    X1_raw = R1.alloc([P, NLT, D_MODEL], F32, "X1")
    X1 = [Tile(X1_raw[:, l, :], f"X1_{l}") for l in range(NLT)]
    wo_raw = R3.alloc([P, 8, D_MODEL], BF16, "wo_bf")
    wo_bf = [Tile(wo_raw[:, kc, :], f"wo{kc}") for kc in range(8)]
    GT1 = R4.tile([P, D_MODEL], F32, "GT1")
    xr = [R4.tile([P, D_MODEL], F32, f"xr{i}") for i in range(2)]
    tmp5 = R4.tile([P, D_MODEL], F32, "tmp5")
    junk5 = R6.tile([P, D_MODEL], BF16, "junk5")
    dgs5 = [R6.tile([P, P], F32, f"dgs5_{i}") for i in range(2)]
    ssq5 = R6.tile([P, NLT], F32, "ssq5")
    wout_v = wout_d.rearrange("(kc p) n -> p kc n", p=P)
    K.dma(wo_raw[:], wout_v, W=[wo_bf[kc][:] for kc in range(8)], queue="pool")
    make_GT(0, GT1, dgs5, PS[4], PS[5])
    mixT = [yfT[g] for g in range(4)] + [V(ydT, ydT_raw[:, h, :]) for h in range(4)]

    ssq5b = R6.tile([P, NLT], F32, "ssq5b")
    for l in range(NLT):
        K.dma(xr[l % 2].ap[:], x_d[l * P:(l + 1) * P, :], W=[xr[l % 2][:]])
        psA, psB = PS[(l % 4) * 2], PS[(l % 4) * 2 + 1]
        for half, psx in enumerate((psA, psB)):
            for kc in range(8):
                mm(psx[:, :], mixT[kc][:, l * P:(l + 1) * P], wo_bf[kc][:, half * 512:(half + 1) * 512],
                   start=(kc == 0), stop=(kc == 7))
        act(junk5[:, 0:512], psA[:, :], AF.Square, accum=ssq5[:, l:l + 1])
        act(junk5[:, 512:1024], psB[:, :], AF.Square, accum=ssq5b[:, l:l + 1])
        tt("dve", ssq5[:, l:l + 1], ssq5[:, l:l + 1], ssq5b[:, l:l + 1], ALU.add)
        ts("dve", ssq5[:, l:l + 1], ssq5[:, l:l + 1], 1.0 / D_MODEL, ALU.mult, EPS, ALU.add)
        tt("pool", ssq5[:, l:l + 1], ssq5[:, l:l + 1], c_mhalf, ALU.pow)
        stt(tmp5[:, 0:512], psA[:, :], ssq5[:, l:l + 1], GT1[:, 0:512], ALU.mult, ALU.mult)
        stt(tmp5[:, 512:1024], psB[:, :], ssq5[:, l:l + 1], GT1[:, 512:1024], ALU.mult, ALU.mult)
        tt("pool", X1[l][:], tmp5[:], xr[l % 2][:], ALU.add)
    dump("X1", V(X1[0], X1_raw[:, 0:4, :]), [P, 4, D_MODEL])
    R2.reset()
    R3.reset(barrier=False)
    R4.reset(barrier=False)
    R5.reset(barrier=False)
    R6.reset(barrier=False)
    if stop_after == 6:
        return finish()

    FF = Arena(nc, K, base + 80 * KB, avail - 80 * KB, "FF")
    actT_raw = FF.alloc([P, NFC, 1024], BF16, "actT")
    actT = [Tile(actT_raw[:, c, :], f"actT{c}") for c in range(NFC)]
    GT2 = FF.tile([P, D_MODEL], F32, "GT2")
    ssq6 = FF.tile([P, 32], F32, "ssq6")
    ssq6b = FF.tile([P, 32], F32, "ssq6b")
    dgs6 = [FF.tile([P, P], F32, f"dgs6_{i}") for i in range(2)]
    make_GT(1, GT2, dgs6, PS[4], PS[5])
    NWA = 11
    wdA_raw = FF.alloc([P, NWA, D_MODEL], BF16, "wdA")
    wdA = [Tile(wdA_raw[:, kc, :], f"wdA{kc}") for kc in range(NWA)]
    ff_mark = FF.cur
    NLOC = 9 * P
    for hf in range(2):
        FF.cur = ff_mark
        h2T = FF.tile([P, 8, NLOC], BF16, "h2T")
        wbf6 = [FF.tile([P, 8, 2, P], BF16, f"wbf6_{i}") for i in range(3)]
        PR, PC = 18, 66
        gpad = [FF.tile([P, PR, PC], BF16, f"gpad{i}") for i in range(2)]
        dg9 = [FF.tile([P, 9, P], BF16, f"dg9_{i}") for i in range(2)]
        sg = [FF.tile([P, 512], F32, f"sg{i}") for i in range(2)]
        xn6 = [FF.tile([P, D_MODEL], BF16, f"xn6_{i}") for i in range(2)]
        junk6 = FF.tile([P, D_MODEL], BF16, "junk6")
        for gp_ in gpad:
            memset("pool", gp_[:], 0.0)
        tile0 = 0 if hf == 0 else 7
        def f_stageA(i):
            l = tile0 + i
            col = hf * 16 + i
            norm_tile(X1[l][:], xn6[i % 2], ssq6[:, col:col + 1], ssq6b[:, col:col + 1], junk6)

        def f_stageB(i):
            transpose_mod(xn6[i % 2], PS[i % 2],
                          lambda kc: h2T[:, kc, i * P:(i + 1) * P],
                          lambda kc: sA2[:, kc:kc + 1],
                          lambda kc: modB[:, 8 + kc, 0:1], use_act=(i % 2 == 0))

        f_stageA(0)
        for i in range(9):
            if i + 1 < 9:
                f_stageA(i + 1)
            f_stageB(i)
        main0 = 0 if hf == 0 else 128
        halo0 = 1024 if hf == 0 else 64
        halo_row = 17 if hf == 0 else 0

        def load_wup(cc):
            w = wbf6[cc % 3]
            K.dma(w.ap[:], wup_d[cc], W=[w[:]], queue="pool")

        load_wup(0)
        load_wup(1)
        if hf == 0:
            K.dma(wdA_raw[:], wdown_d[0:NWA * P, :].rearrange("(kc p) n -> p kc n", p=P),
                  W=[wdA[kc][:] for kc in range(NWA)], queue="pool")
        for cc in range(NFC):
            if cc + 2 < NFC:
                load_wup(cc + 2)
            wb = wbf6[cc % 3]
            dg = dg9[cc % 2]
            gp_ = gpad[cc % 2]
            for tap in range(9):
                ts("dve", dg[:, tap, :], IDN, dw_fm[:, tap * NFC + cc:tap * NFC + cc + 1], ALU.mult)
            psgs = [PS[2], PS[3]]
            for piece in range(2):
                psg = psgs[piece]
                t0 = main0 + piece * 512
                for kc in range(8):
                    mm(psg[:, :], wb[:, kc, 1, :], h2T[:, kc, t0:t0 + 512], start=(kc == 0), stop=(kc == 7))
                cp("act", gp_[:, 1 + piece * 8:9 + piece * 8, 1:65], hview(psg[:, :], "p (r c) -> p r c", c=64))
            psh = PS[4]
            for kc in range(8):
                mm(psh[:, 0:64], wb[:, kc, 1, :], h2T[:, kc, halo0:halo0 + 64], start=(kc == 0), stop=(kc == 7))
            cp("act", gp_[:, halo_row, 1:65], psh[:, 0:64])
            psvs = [PS[0], PS[1]]
            for piece in range(2):
                t0 = main0 + piece * 512
                for kc in range(8):
                    mm(psvs[piece][:, :], wb[:, kc, 0, :], h2T[:, kc, t0:t0 + 512], start=(kc == 0), stop=(kc == 7))
            for piece in range(2):
                psc = PS[5 + piece]
                R0_ = 1 + piece * 8
                for tap in range(9):
                    kh, kw = divmod(tap, 3)
                    mm(hview(psc[:, :], "p (r c) -> p r c", c=64), dg[:, tap, :],
                       gp_[:, R0_ + kh - 1:R0_ + kh - 1 + 8, kw:kw + 64], start=(tap == 0), stop=(tap == 8))
                act(sg[piece][:], psc[:, :], AF.Silu)
                tt("dve", actT[cc][:, piece * 512:(piece + 1) * 512], psvs[piece][:, :], sg[piece][:], ALU.mult)
        if hf == 0:
            dump("actT", V(actT[0], actT_raw[:, 0:4, 0:512]), [P, 4, 512], BF16)
        K.barrier()
        FF.cur = ff_mark
        wdB_raw = FF.alloc([P, NFC - NWA, D_MODEL], BF16, "wdB")
        wdB = [Tile(wdB_raw[:, kc, :], f"wdB{kc}") for kc in range(NFC - NWA)]
        wd_bf = wdA + wdB
        ot = [FF.tile([P, D_MODEL], F32, f"ot{i}") for i in range(2)]
        tmp6 = FF.tile([P, D_MODEL], F32, "tmp6")
        junk7 = FF.tile([P, D_MODEL], BF16, "junk7")
        sq7 = FF.tile([P, 16], F32, "sq7")
        sq7b = FF.tile([P, 16], F32, "sq7b")
        for k0, k1 in ((0, 4), (4, NFC - NWA)):
            K.dma(wdB_raw[:, k0:k1, :],
                  wdown_d[(NWA + k0) * P:(NWA + k1) * P, :].rearrange("(kc p) n -> p kc n", p=P),
                  W=[wdB[kc][:] for kc in range(k0, k1)], queue="pool")
        for grp in range(2):
            for kc in range(NFC):
                for j in range(4):
                    i = grp * 4 + j
                    for half in range(2):
                        mm(PS[2 * j + half][:, :], actT[kc][:, i * P:(i + 1) * P],
                           wd_bf[kc][:, half * 512:(half + 1) * 512], start=(kc == 0), stop=(kc == NFC - 1))
            for j in range(4):
                i = grp * 4 + j
                l = hf * 8 + i
                psA, psB = PS[2 * j], PS[2 * j + 1]
                c1, c2 = sq7[:, i:i + 1], sq7b[:, i:i + 1]
                act(junk7[:, 0:512], psA[:, :], AF.Square, accum=c1)
                act(junk7[:, 512:1024], psB[:, :], AF.Square, accum=c2)
                tt("dve", c1, c1, c2, ALU.add)
                ts("dve", c1, c1, 1.0 / D_MODEL, ALU.mult, EPS, ALU.add)
                tt("pool", c1, c1, c_mhalf, ALU.pow)
                stt(tmp6[:, 0:512], psA[:, :], c1, GT2[:, 0:512], ALU.mult, ALU.mult)
                stt(tmp6[:, 512:1024], psB[:, :], c1, GT2[:, 512:1024], ALU.mult, ALU.mult)
                o_t = ot[i % 2]
                tt("pool", o_t[:], tmp6[:], X1[l][:], ALU.add)
                K.dma(out_d[l * P:(l + 1) * P, :], o_t.ap[:], R=[o_t[:]], is_output=True)
        K.barrier()
    return finish()


G_IDN, G_ONES, G_NEG1, G_MF, G_MB, G_CB, G_IND0, G_IND1, G_NM_SL, G_NM_SU, G_PM_LI, G_PM_UI = range(12)
NG = 12
BIG = 30000.0


def _gconst():
    t = np.arange(P)
    a = t[:, None]
    b = t[None, :]
    same = (a // 64) == (b // 64)
    g = np.zeros((NG, P, P), np.float32)
    g[G_IDN] = np.eye(P)
    g[G_ONES] = 1.0
    g[G_NEG1] = -1.0
    g[G_MF] = (same & (a <= b))
    g[G_MB] = (same & (a >= b))
    g[G_CB] = same
    g[G_IND0] = (a < 64) * np.ones((1, P))
    g[G_IND1] = (a >= 64) * np.ones((1, P))
    g[G_NM_SL] = np.where(same & (a > b), 0.0, -BIG)
    g[G_NM_SU] = np.where(same & (a < b), 0.0, -BIG)
    g[G_PM_LI] = np.where(same & (a >= b), 0.0, BIG)
    g[G_PM_UI] = np.where(same & (a <= b), 0.0, BIG)
    return np.ascontiguousarray(g.transpose(1, 0, 2))


def _dft_consts():
    L, Fd = SEQ, 128
    sc = 1.0 / np.sqrt(L * Fd)
    n = np.arange(Fd)
    ang = 2.0 * np.pi * ((n[:, None] * n[None, :]) % Fd) / Fd
    cfsf = np.concatenate([np.cos(ang) * sc, -np.sin(ang) * sc], axis=1).astype(np.float32)
    l = np.arange(L, dtype=np.int64)
    prod = (l[:, None] * l[None, :]) % L
    angL = 2.0 * np.pi * prod / L
    c = np.cos(angL).astype(np.float32)
    s = np.sin(angL).astype(np.float32)
    tab = np.stack([c, s], axis=0)
    tab = tab.reshape(2, 16, 128, 4, 512)
    tab = tab.transpose(3, 1, 2, 0, 4)
    return cfsf, np.ascontiguousarray(tab).astype(ml_dtypes.bfloat16)


_CACHE = {}


def _host_consts():
    if "c" not in _CACHE:
        cfsf, tab = _dft_consts()
        _CACHE["c"] = (_gconst(), cfsf, tab)
    return _CACHE["c"]


def _relayout_win(w):
    wp = np.zeros((D_MODEL, 21 * P), np.float32)
    wp[:, :IN_W] = w
    return np.ascontiguousarray(wp.reshape(8, P, 21, P).transpose(2, 1, 0, 3))


def _relayout_wup(w):
    return np.ascontiguousarray(w.reshape(8, P, 2, NFC, P).transpose(3, 1, 0, 2, 4))


def make_in_maps(inputs, n_cores=8):
    gconst, cfsf, tab = _host_consts()
    f = lambda a: np.ascontiguousarray(np.asarray(a, dtype=np.float32))
    x = f(inputs["x"]); c = f(inputs["c"]); ctx = f(inputs["ctx"]); c_ctx = f(inputs["c_ctx"])
    shared = {
        "w_ada": f(inputs["w_ada"][0]), "w_in": _relayout_win(f(inputs["w_in"][0])), "w_out": f(inputs["w_out"][0]),
        "w_up": _relayout_wup(f(inputs["w_up"][0])), "w_down": f(inputs["w_down"][0]),
        "convw": f(inputs["w_qkv_conv"][0]).reshape(36, P),
        "dwcw": f(inputs["w_dwc"][0]).reshape(198, P),
        "g_gdn": f(inputs["g_gdn"][0]).reshape(1, P),
        "a_log": f(inputs["a_log"][0]).reshape(8), "dt_bias": f(inputs["dt_bias"][0]).reshape(8),
        "w_fourier": f(inputs["w_fourier"][0]),
        "gconst": gconst, "cfsf": cfsf, "dft_tab": tab,
    }
    maps = []
    for b in range(n_cores):
        vecs = np.concatenate([
            f(inputs["b_ada"][0]).reshape(48, P), f(inputs["g_pre_mix"][0]).reshape(8, P),
            f(inputs["g_post_mix"][0]).reshape(8, P), f(inputs["g_pre_ffn"][0]).reshape(8, P),
            f(inputs["g_post_ffn"][0]).reshape(8, P), c[b].reshape(8, P), c_ctx.reshape(8, P)], axis=0)
        m = dict(shared)
        m["x"] = x[b]
        m["ctx"] = ctx[b]
        m["vecs"] = np.ascontiguousarray(vecs)
        maps.append(m)
    return maps


def kernel(**inputs):
    nc, _ = build_program()
    maps = make_in_maps(inputs, 8)
    res = run_bass_kernel_spmd(nc, maps, core_ids=list(range(8)))
    return np.stack([np.asarray(r["out"], dtype=np.float32) for r in res.results], axis=0)
```

```python
import os
import numpy as np
import ml_dtypes
import concourse.bass as bass
import concourse.mybir as mybir
from concourse.bass_utils import run_bass_kernel_spmd

F32 = mybir.dt.float32
BF16 = mybir.dt.bfloat16
AF = mybir.ActivationFunctionType
ALU = mybir.AluOpType

P = 128
D_MODEL = 1024
SEQ = 2048
CTX = 256
NTOK = SEQ + CTX
NT = NTOK // P
NLT = SEQ // P
IN_W = 2576
D_FF = 2816
NFC = D_FF // P
EPS = 1e-6
EPOCH = 12000
NDMA = 12


class Tile:
    def __init__(self, ap, name="", psum=False):
        self.ap = ap
        self.name = name
        self.w = None
        self.r = {}
        self.psum = psum

    def __getitem__(self, idx):
        return V(self, self.ap[idx])

    def v(self, ap):
        return V(self, ap)


class V:
    def __init__(self, tile, a):
        self.tile = tile
        self.a = a

    def __getitem__(self, idx):
        return V(self.tile, self.a[idx])

    def bc(self, shape):
        return V(self.tile, self.a.broadcast_to(list(shape)))

    def bitcast(self, dt):
        return V(self.tile, self.a.bitcast(dt))


class _Eng:
    def __init__(self, K, name, h):
        self.K = K
        self.name = name
        self.h = h
        self.sem = None
        self.cnt = 0
        self.seen = {}
        self.nsem = 0
        self.ninst = 0

    def newsem(self):
        self.sem = self.K.nc.alloc_semaphore(f"s_{self.name}_{self.nsem}")
        self.nsem += 1
        self.cnt = 0


class Kern:
    def __init__(self, nc):
        self.nc = nc
        self.E = {}
        for name, h in (("pe", nc.tensor), ("dve", nc.vector), ("act", nc.scalar),
                        ("pool", nc.gpsimd), ("sp", nc.sync)):
            e = _Eng(self, name, h)
            if name != "sp":
                e.newsem()
            self.E[name] = e
        self.dpool = [[nc.alloc_semaphore(f"s_dma_{i}"), 0] for i in range(NDMA)]
        self.drr = 0
        self.out_tokens = []
        self.swsems = []
        self.sw_done = 0

    def _deps(self, e, W, R):
        need = []
        for v in R:
            t = v.tile
            if t.w is not None:
                need.append((t.w, "raw"))
            if t.psum:
                for tok in t.r.values():
                    if tok[2] is not e:
                        need.append((tok, "rar"))
        for v in W:
            t = v.tile
            if t.w is not None:
                need.append((t.w, "waw"))
            for tok in t.r.values():
                need.append((tok, "war"))
        for (sem, val, src), kind in need:
            if src is e:
                if e.name == "pe":
                    continue
            if e.seen.get(id(sem), 0) >= val:
                continue
            e.h.wait_ge(sem, val)
            e.seen[id(sem)] = val

    def _post(self, tok, W, R):
        sem = tok[0]
        for v in R:
            v.tile.r[id(sem)] = tok
        for v in W:
            v.tile.w = tok
            v.tile.r = {}

    def emit(self, eng, fn, W=(), R=()):
        e = self.E[eng]
        self._deps(e, W, R)
        if e.cnt >= EPOCH:
            e.newsem()
        inst = fn(e.h)
        e.cnt += 1
        e.ninst += 1
        inst.then_inc(e.sem, 1)
        tok = (e.sem, e.cnt, e)
        self._post(tok, W, R)
        return tok

    def dma(self, out, in_, W=(), R=(), is_output=False, queue="sp"):
        e = self.E[queue]
        self._deps(e, W, R)
        if queue == "pool":
            sem = self.nc.alloc_semaphore(f"s_sw_{len(self.swsems)}")
            self.swsems.append(sem)
            e.h.dma_start(out=out, in_=in_).then_inc(sem, 16)
            tok = (sem, 16, None)
            self._post(tok, W, R)
            return tok
        slot = self.dpool[self.drr]
        self.drr = (self.drr + 1) % NDMA
        sem, n = slot
        if n > 0 and e.seen.get(id(sem), 0) < 16 * n:
            e.h.wait_ge(sem, 16 * n)
            e.seen[id(sem)] = 16 * n
        e.h.dma_start(out=out, in_=in_).then_inc(sem, 16)
        slot[1] = n + 1
        tok = (sem, 16 * (n + 1), None)
        self._post(tok, W, R)
        if is_output:
            self.out_tokens.append(tok)
        return tok

    def barrier(self):
        names = ["pe", "dve", "act", "pool"]
        for a in names:
            ea = self.E[a]
            for b in names + ["sp"]:
                if a == b:
                    continue
                eb = self.E[b]
                if b == "sp":
                    continue
                if eb.cnt == 0:
                    continue
                if ea.seen.get(id(eb.sem), 0) >= eb.cnt:
                    continue
                ea.h.wait_ge(eb.sem, eb.cnt)
                ea.seen[id(eb.sem)] = eb.cnt
            for sem, n in self.dpool:
                if n > 0 and ea.seen.get(id(sem), 0) < 16 * n:
                    ea.h.wait_ge(sem, 16 * n)
                    ea.seen[id(sem)] = 16 * n
            for sem in self.swsems[self.sw_done:]:
                if ea.seen.get(id(sem), 0) < 16:
                    ea.h.wait_ge(sem, 16)
                    ea.seen[id(sem)] = 16
        sp = self.E["sp"]
        for b in names:
            eb = self.E[b]
            if eb.cnt and sp.seen.get(id(eb.sem), 0) < eb.cnt:
                sp.h.wait_ge(eb.sem, eb.cnt)
                sp.seen[id(eb.sem)] = eb.cnt
        for sem, n in self.dpool:
            if n > 0 and sp.seen.get(id(sem), 0) < 16 * n:
                sp.h.wait_ge(sem, 16 * n)
                sp.seen[id(sem)] = 16 * n
        for sem in self.swsems[self.sw_done:]:
            if sp.seen.get(id(sem), 0) < 16:
                sp.h.wait_ge(sem, 16)
                sp.seen[id(sem)] = 16
        self.sw_done = len(self.swsems)

    def finish(self):
        sp = self.E["sp"]
        for sem, val, _ in self.out_tokens:
            if sp.seen.get(id(sem), 0) < val:
                sp.h.wait_ge(sem, val)
                sp.seen[id(sem)] = val


class Arena:
    def __init__(self, nc, K, start, size, name):
        self.nc = nc
        self.K = K
        self.start = start
        self.end = start + size
        self.cur = start
        self.name = name
        self.n = 0

    def alloc(self, shape, dt, name=None):
        esz = 4 if dt == F32 else 2
        nbytes = esz
        for s in shape[1:]:
            nbytes *= s
        off = (self.cur + 63) // 64 * 64
        if off + nbytes > self.end:
            raise RuntimeError(f"arena {self.name} overflow allocating {name} {shape}: "
                               f"{off - self.start}+{nbytes} > {self.end - self.start}")
        self.cur = off + nbytes
        self.n += 1
        return self.nc.alloc_sbuf_tensor_at(f"{self.name}_{name or 't'}_{self.n}", list(shape), dt, offset=off)

    def tile(self, shape, dt, name=None):
        return Tile(self.alloc(shape, dt, name), name or "t")

    def reset(self, barrier=True):
        if barrier:
            self.K.barrier()
        self.cur = self.start


KB = 1024


def build_program(debug=None, stop_after=None):
    debug = debug or []
    nc = bass.Bass("TRN2", target_bir_lowering=False)
    K = Kern(nc)
    base = (nc.sbuf_base + 63) // 64 * 64
    avail = nc.sbuf_top - base
    R0 = Arena(nc, K, base, 16 * KB, "R0")
    R1 = Arena(nc, K, base + 16 * KB, 64 * KB, "R1")
    R2 = Arena(nc, K, base + 80 * KB, 16 * KB, "R2")
    R3 = Arena(nc, K, base + 96 * KB, 16 * KB, "R3")
    R4 = Arena(nc, K, base + 112 * KB, 16 * KB, "R4")
    R5 = Arena(nc, K, base + 128 * KB, 54 * KB, "R5")
    R6 = Arena(nc, K, base + 182 * KB, avail - 182 * KB, "R6")
    dbg_out = {}

    def dram(name, shape, dt=F32, kind="ExternalInput"):
        return nc.dram_tensor(name, list(shape), dt, kind=kind).ap()

    x_d = dram("x", [SEQ, D_MODEL])
    ctx_d = dram("ctx", [CTX, D_MODEL])
    wada_d = dram("w_ada", [D_MODEL, 6 * D_MODEL])
    win_d = dram("w_in", [21, P, 8, P])
    wout_d = dram("w_out", [D_MODEL, D_MODEL])
    wup_d = dram("w_up", [NFC, P, 8, 2, P])
    wdown_d = dram("w_down", [D_FF, D_MODEL])
    vecs_d = dram("vecs", [96, P])
    convw_d = dram("convw", [36, P])
    dwcw_d = dram("dwcw", [198, P])
    ggdn_d = dram("g_gdn", [1, P])
    alog_d = dram("a_log", [8])
    dtb_d = dram("dt_bias", [8])
    wf_d = dram("w_fourier", [4, P, P])
    gconst_d = dram("gconst", [P, NG, P])
    cfsf_d = dram("cfsf", [P, 256])
    tab_d = dram("dft_tab", [4, 16, P, 2, 512], BF16)
    out_d = dram("out", [SEQ, D_MODEL], kind="ExternalOutput")

    PS = [Tile(nc.alloc_psum_tensor(f"ps{i}", [P, 512], F32), f"ps{i}", psum=True) for i in range(8)]

    def dump(name, view, shape, dt=F32):
        if name not in debug:
            return
        d = dram("dbg_" + name, shape, dt, kind="ExternalOutput")
        dbg_out[name] = (shape, dt)
        K.dma(d, view.a, R=[view], is_output=True)

    def finish():
        K.finish()
        return nc, dbg_out

    def mm(out, lhsT, rhs, start=True, stop=True):
        K.emit("pe", lambda e: e.matmul(out.a, lhsT=lhsT.a, rhs=rhs.a, start=start, stop=stop),
               W=[out], R=[lhsT, rhs])

    def tr(out, in_, ident):
        K.emit("pe", lambda e: e.transpose(out.a, in_.a, ident.a), W=[out], R=[in_, ident])

    def act(out, in_, func, scale=1.0, bias=None, accum=None):
        R = [in_]
        kw = {}
        if isinstance(scale, V):
            R.append(scale)
            kw["scale"] = scale.a
        else:
            kw["scale"] = float(scale)
        if bias is not None:
            R.append(bias)
            kw["bias"] = bias.a
        W = [out]
        if accum is not None:
            W.append(accum)
            kw["accum_out"] = accum.a
        K.emit("act", lambda e: e.activation(out.a, in_.a, func, **kw), W=W, R=R)

    def ts(eng, out, in0, s1, op0, s2=None, op1=None):
        R = [in0]
        a1 = s1.a if isinstance(s1, V) else float(s1)
        if isinstance(s1, V):
            R.append(s1)
        a2 = None
        if s2 is not None:
            a2 = s2.a if isinstance(s2, V) else float(s2)
            if isinstance(s2, V):
                R.append(s2)
        if op1 is None:
            K.emit(eng, lambda e: e.tensor_scalar(out.a, in0.a, a1, None, op0), W=[out], R=R)
        else:
            K.emit(eng, lambda e: e.tensor_scalar(out.a, in0.a, a1, a2, op0, op1), W=[out], R=R)

    def tt(eng, out, in0, in1, op):
        K.emit(eng, lambda e: e.tensor_tensor(out.a, in0.a, in1.a, op), W=[out], R=[in0, in1])

    def stt(out, in0, s, in1, op0, op1):
        R = [in0, in1]
        a = s.a if isinstance(s, V) else float(s)
        if isinstance(s, V):
            R.append(s)
        K.emit("dve", lambda e: e.scalar_tensor_tensor(out.a, in0.a, a, in1.a, op0, op1), W=[out], R=R)

    def cp(eng, out, in_):
        if eng == "act":
            act(out, in_, AF.Identity)
        else:
            K.emit(eng, lambda e: e.tensor_copy(out.a, in_.a), W=[out], R=[in_])

    def memset(eng, out, val):
        K.emit(eng, lambda e: e.memset(out.a, val), W=[out])

    def hview(view, pat, **kw):
        return V(view.tile, view.a.rearrange(pat, **kw))

    def bc_mid(view, n):
        sh = list(view.a.shape)
        return V(view.tile, view.a.unsqueeze(1).broadcast_to([sh[0], n, sh[1]]))

    def bc_last(view, n):
        sh = list(view.a.shape)
        return V(view.tile, view.a.unsqueeze(len(sh)).broadcast_to(sh + [n]))

    def ps4(ps):
        return hview(ps[:, :], "p (h d) -> p h d", d=P)

    GC = R0.tile([P, NG, P], F32, "gconst")
    K.dma(GC.ap[:], gconst_d, W=[GC[:]])

    def gcv(i):
        return GC[:, i, :]

    IDN = gcv(G_IDN)
    ONES = gcv(G_ONES)
    NEG1 = gcv(G_NEG1)
    idn_bf = R0.tile([P, P], BF16, "idn_bf")
    ones_bf = R0.tile([P, P], BF16, "ones_bf")
    cp("dve", idn_bf[:], IDN)
    cp("dve", ones_bf[:], ONES)
    cst = R0.tile([P, 8], F32, "cst")
    memset("dve", cst[:, 0:1], EPS)
    memset("dve", cst[:, 1:2], -0.5)
    memset("dve", cst[:, 2:3], 1.0)
    c_eps = cst[:, 0:1]
    c_mhalf = cst[:, 1:2]
    c_one = cst[:, 2:3]
    if stop_after == -3:
        dump("idn", idn_bf[:], [P, P], BF16)
        return finish()

    vec_fm = R0.tile([P, 96], F32, "vec_fm")
    cw_fm = R0.tile([P, 36], F32, "cw_fm")
    dw_fm = R0.tile([P, 198], F32, "dw_fm")
    gg_fm = R0.tile([P, 1], F32, "gg_fm")
    st_v = R6.tile([96, P], F32, "st_v")
    st_c = R6.tile([36, P], F32, "st_c")
    st_d1 = R6.tile([P, P], F32, "st_d1")
    st_d2 = R6.tile([70, P], F32, "st_d2")
    st_g = R6.tile([1, P], F32, "st_g")
    K.dma(st_v.ap[:], vecs_d, W=[st_v[:]])
    K.dma(st_c.ap[:], convw_d, W=[st_c[:]])
    K.dma(st_d1.ap[:], dwcw_d[0:128, :], W=[st_d1[:]])
    K.dma(st_d2.ap[:], dwcw_d[128:198, :], W=[st_d2[:]])
    K.dma(st_g.ap[:], ggdn_d, W=[st_g[:]])
    ps = PS[0]
    tr(ps[:, 0:96], st_v[:], GC[0:96, G_IDN, 0:96])
    cp("dve", vec_fm[:], ps[:, 0:96])
    ps = PS[1]
    tr(ps[:, 0:36], st_c[:], GC[0:36, G_IDN, 0:36])
    tr(ps[:, 36:37], st_g[:], GC[0:1, G_IDN, 0:1])
    cp("dve", cw_fm[:], ps[:, 0:36])
    cp("dve", gg_fm[:], ps[:, 36:37])
    ps = PS[2]
    tr(ps[:, 0:128], st_d1[:], IDN)
    tr(ps[:, 128:198], st_d2[:], GC[0:70, G_IDN, 0:70])
    cp("dve", dw_fm[:], ps[:, 0:198])
    if stop_after == -2:
        dump("dw_fm", dw_fm[:], [P, 198])
        return finish()

    ab_c = R0.tile([P, 16], F32, "ab_c")
    K.dma(ab_c.ap[:, 0:8], dtb_d.partition_broadcast(P), W=[ab_c[:]])
    K.dma(ab_c.ap[:, 8:16], alog_d.partition_broadcast(P), W=[ab_c[:]])
    act(ab_c[:, 8:16], ab_c[:, 8:16], AF.Exp)
    ts("dve", ab_c[:, 8:16], ab_c[:, 8:16], -1.0, ALU.mult)

    silu_c = R0.tile([P, 8, 2], F32, "silu_c")
    act(silu_c[:, :, 0], vec_fm[:, 80:88], AF.Silu)
    act(silu_c[:, :, 1], vec_fm[:, 88:96], AF.Silu)
    if stop_after == -1:
        dump("silu_c", silu_c[:], [P, 8, 2])
        dump("ab_c", ab_c[:], [P, 16])
        return finish()

    modA = R0.tile([P, 16, 2], F32, "modA")
    modB = R0.tile([P, 32, 2], F32, "modB")
    wada_v = wada_d.rearrange("(kc p) n -> p kc n", p=P)
    wa = [R5.tile([P, 8, 512], F32, f"wa{i}") for i in range(2)]
    sA1 = R0.tile([P, 8, 2], F32, "sA1")
    sA2 = R0.tile([P, 8], F32, "sA2")
    gtg = R0.tile([P, 16], F32, "gtg")

    def mod_groups(g0, g1, psm, jbase):
        for gidx in range(g0, g1):
            w = wa[gidx % 2]
            K.dma(w.ap[:], wada_v[:, :, gidx * 512:(gidx + 1) * 512], W=[w[:]])
            for cc in range(4):
                j = gidx * 4 + cc - jbase
                for kc in range(8):
                    mm(psm[:, 2 * j:2 * j + 2], w[:, kc, cc * 128:(cc + 1) * 128], silu_c[:, kc, :],
                       start=(kc == 0), stop=(kc == 7))
                if cc % 2 == 1:
                    yield

    for _ in mod_groups(0, 4, PS[3], 0):
        pass
    tt("dve", modA[:], hview(PS[3][:, 0:32], "p (j v) -> p j v", v=2), bc_last(vec_fm[:, 0:16], 2), ALU.add)
    stt(sA1[:], modA[:, 8:16, :], 1.0, bc_last(vec_fm[:, 48:56], 2), ALU.add, ALU.mult)

    def mod_rest():
        yield from mod_groups(4, 12, PS[2], 16)
        tt("dve", modB[:], hview(PS[2][:, 0:64], "p (j v) -> p j v", v=2), bc_last(vec_fm[:, 16:48], 2), ALU.add)
        stt(sA2[:], modB[:, 16:24, 0], 1.0, vec_fm[:, 64:72], ALU.add, ALU.mult)
        tt("dve", gtg[:, 0:8], modB[:, 0:8, 0], vec_fm[:, 56:64], ALU.mult)
        tt("dve", gtg[:, 8:16], modB[:, 24:32, 0], vec_fm[:, 72:80], ALU.mult)
        dump("mod", modB[:], [P, 32, 2])

    mod_rest_gen = mod_rest()
    if stop_after in (-0.5, 0):
        for _ in mod_rest_gen:
            pass
        return finish()

    def make_GT(j, GTt, dgs, psa, psb_):
        for half in range(2):
            psb = (psa, psb_)[half]
            for c4 in range(4):
                c = half * 4 + c4
                dgt = dgs[c % 2]
                ts("dve", dgt[:], IDN, gtg[:, j * 8 + c:j * 8 + c + 1], ALU.mult)
                mm(psb[:, c4 * 128:(c4 + 1) * 128], ONES, dgt[:])
            cp("act", GTt[:, half * 512:(half + 1) * 512], psb[:])

    if stop_after == 0:
        return finish()

    TOKBLK = [(0, 256)] + [(256 + 512 * b, 512) for b in range(4)]
    hT_raw = R1.alloc([P, 8, NTOK], BF16, "hT")
    hTb = [Tile(hT_raw[:, :, o:o + n], f"hT{b}") for b, (o, n) in enumerate(TOKBLK)]

    def hT_view(tok0, n):
        for b, (o, nn) in enumerate(TOKBLK):
            if o <= tok0 and tok0 + n <= o + nn:
                return hTb[b], tok0 - o
        raise AssertionError

    ssq = R0.tile([P, NT], F32, "ssq")
    rstd = R0.tile([P, NT], F32, "rstd")
    xts = [R1.tile([P, D_MODEL], F32, f"xt{i}") for i in range(3)]
    xns = [R1.tile([P, D_MODEL], BF16, f"xn{i}") for i in range(2)]
    junk = R1.tile([P, D_MODEL], BF16, "junk")

    def norm_tile(xt_v, xn, ssq_col, rstd_col, junk_t):
        act(junk_t[:], xt_v, AF.Square, accum=ssq_col)
        ts("dve", rstd_col, ssq_col, 1.0 / D_MODEL, ALU.mult, EPS, ALU.add)
        tt("pool", rstd_col, rstd_col, c_mhalf, ALU.pow)
        ts("dve", xn[:], xt_v, rstd_col, ALU.mult)

    def transpose_mod(xn, pst, dst_fn, sc_fn, bi_fn, use_act=True):
        pst_bf = V(pst, pst.ap[:].bitcast(BF16))
        for kc in range(8):
            tr(pst_bf[:, kc * 128:(kc + 1) * 128], xn[:, kc * 128:(kc + 1) * 128], idn_bf[:])
        for kc in range(8):
            if use_act:
                act(dst_fn(kc), pst_bf[:, kc * 128:(kc + 1) * 128], AF.Identity, scale=sc_fn(kc), bias=bi_fn(kc))
            else:
                ts("dve", dst_fn(kc), pst_bf[:, kc * 128:(kc + 1) * 128], sc_fn(kc), ALU.mult, bi_fn(kc), ALU.add)

    def p1_stageA(ti):
        xt = xts[ti % 3]
        src = ctx_d[ti * P:(ti + 1) * P, :] if ti < 2 else x_d[(ti - 2) * P:(ti - 1) * P, :]
        K.dma(xt.ap[:], src, W=[xt[:]])
        norm_tile(xt[:], xns[ti % 2], ssq[:, ti:ti + 1], rstd[:, ti:ti + 1], junk)

    def p1_stageB(ti):
        vsel = 1 if ti < 2 else 0
        hb, off = hT_view(ti * P, P)
        transpose_mod(xns[ti % 2], PS[ti % 2],
                      lambda kc: hb[:, kc, off:off + P],
                      lambda kc: sA1[:, kc, vsel:vsel + 1],
                      lambda kc: modA[:, kc, vsel:vsel + 1], use_act=(ti % 2 == 0))

    p1_stageA(0)
    for ti in range(NT):
        if ti + 1 < NT:
            p1_stageA(ti + 1)
        p1_stageB(ti)
        next(mod_rest_gen, None)
    for _ in mod_rest_gen:
        pass
    dump("hT", hTb[1][:], [P, 8, 512], BF16)
    if stop_after == 1:
        return finish()

    R5.reset(barrier=False)
    uT_raw = R3.alloc([P, 4, SEQ], BF16, "uT")
    szT_raw = R4.alloc([P, 4, SEQ], BF16, "szT")
    qT_raw = R2.alloc([P, 4, SEQ], BF16, "qT")
    kT_raw = R5.alloc([P, 4, NTOK], BF16, "kT")
    ktok_raw = R5.alloc([P, NT, 4, P], BF16, "ktok")
    vtok_raw = R5.alloc([P, NT, 4, P], BF16, "vtok")
    uT = [Tile(uT_raw[:, g, :], f"uT{g}") for g in range(4)]
    szT = Tile(szT_raw, "szT")
    qT = [Tile(qT_raw[:, g, :], f"qT{g}") for g in range(4)]
    kT = [Tile(kT_raw[:, g, :], f"kT{g}") for g in range(4)]
    ktok = Tile(ktok_raw, "ktok")
    vtok = Tile(vtok_raw, "vtok")
    abtok = R0.tile([P, NT, 16], F32, "abtok")

    K.barrier()
    R1.cur = R1.start + 8 * NTOK * 2
    wbf = [R6.tile([P, 8, P], BF16, f"wbf{i}") for i in range(3)]
    PADW = NTOK + 4
    pads = [R6.tile([P, PADW], BF16, f"pad{i}") for i in range(2)]
    dgc = [R1.tile([P, 3, P], BF16, f"dgc{i}") for i in range(2)]
    sil = R1.tile([P, NTOK], F32, "sil")
    sqs = [R1.tile([P, 512], BF16, f"sq{i}") for i in range(2)]
    lnt = R1.tile([P, NTOK], F32, "lnt")
    vT_tmp = V(sil, sil.ap[:].bitcast(BF16)[:, 0:NTOK])
    for pd in pads:
        memset("pool", pd[:], 0.0)

    def padcol(tok):
        return 1 + tok if tok < CTX else 3 + tok

    NCH = 21

    def load_w(j):
        w = wbf[j % 3]
        K.dma(w.ap[:], win_d[j], W=[w[:]], queue="pool")

    load_w(0)
    load_w(1)
    state = {"mmrr": 0, "evrr": 0}

    def kind_of(j):
        return "uqkvz"[j // 4] if j < 20 else "ab"

    def main_part(j):
        if j + 2 < NCH:
            load_w(j + 2)
        wb = wbf[j % 3]
        kind = kind_of(j)
        g = j % 4
        if kind == "ab":
            psab = PS[7]
            for t in range(NT):
                hb, off = hT_view(t * P, P)
                for kc in range(8):
                    mm(psab[:, t * 16:(t + 1) * 16], hb[:, kc, off:off + P], wb[:, kc, 0:16],
                       start=(kc == 0), stop=(kc == 7))
            cp("dve", abtok[:], hview(psab[:, 0:NT * 16], "p (t c) -> p t c", c=16))
            return
        blocks = list(range(5)) if kind in "kv" else list(range(1, 5))
        pad = pads[j % 2]
        if kind in "qkv":
            dg = dgc[j % 2]
            cj = j - 4
            for tap in range(3):
                ts("dve", dg[:, tap, :], IDN, cw_fm[:, tap * 12 + cj:tap * 12 + cj + 1], ALU.mult)
        yield
        for b_ in blocks:
            o, n = TOKBLK[b_]
            psa = PS[state["mmrr"] % 4]
            state["mmrr"] += 1
            for kc in range(8):
                mm(psa[:, 0:n], wb[:, kc, :], hTb[b_][:, kc, :], start=(kc == 0), stop=(kc == 7))
            if kind == "u":
                cp("dve", uT[g][:, o - CTX:o - CTX + n], psa[:, 0:n])
            elif kind == "z":
                act(szT[:, g, o - CTX:o - CTX + n], psa[:, 0:n], AF.Silu)
            else:
                pc = padcol(o)
                cp("act" if state["evrr"] % 2 == 0 else "dve", pad[:, pc:pc + n], psa[:, 0:n])
                state["evrr"] += 1
            yield

    def post_a(j):
        kind = kind_of(j)
        if kind not in "qkv":
            return
        blocks = list(range(5)) if kind in "kv" else list(range(1, 5))
        pad = pads[j % 2]
        dg = dgc[j % 2]
        for b_ in blocks:
            o, n = TOKBLK[b_]
            psc = PS[5 + (b_ % 2)]
            pc = padcol(o)
            for tap in range(3):
                mm(psc[:, 0:n], dg[:, tap, :], pad[:, pc + tap - 1:pc + tap - 1 + n],
                   start=(tap == 0), stop=(tap == 2))
            if kind == "v":
                act(vT_tmp[:, o:o + n], psc[:, 0:n], AF.Silu)
            else:
                act(sil[:, o:o + n], psc[:, 0:n], AF.Silu)
            yield
        if kind == "v":
            return
        for b_ in blocks:
            o, n = TOKBLK[b_]
            sq = sqs[b_ % 2]
            tt("dve", sq[:, 0:n], sil[:, o:o + n], sil[:, o:o + n], ALU.mult)
            psn = PS[4] if b_ % 2 == 0 else PS[7]
            mm(psn[:, 0:n], ones_bf[:], sq[:, 0:n])
            act(lnt[:, o:o + n], psn[:, 0:n], AF.Ln, bias=c_eps)
            yield

    def post_b(j):
        kind = kind_of(j)
        if kind not in "qkv":
            return
        g = j % 4
        blocks = list(range(5)) if kind in "kv" else list(range(1, 5))
        if kind == "v":
            src = vT_tmp
            dst = vtok
        else:
            scale = float(P) ** -0.5 if kind == "q" else 1.0
            for b_ in blocks:
                o, n = TOKBLK[b_]
                act(lnt[:, o:o + n], lnt[:, o:o + n], AF.Exp, scale=-0.5)
                if kind == "q":
                    stt(qT[g][:, o - CTX:o - CTX + n], sil[:, o:o + n], scale, lnt[:, o:o + n],
                        ALU.mult, ALU.mult)
                else:
                    stt(kT[g][:, o:o + n], sil[:, o:o + n], scale, lnt[:, o:o + n],
                        ALU.mult, ALU.mult)
                yield
            if kind == "q":
                return
            src = kT[g]
            dst = ktok
        for t0 in (0, 8, 16):
            nt = min(8, NT - t0)
            pst = PS[5 + ((t0 // 8) % 2)]
            pst_bf = V(pst, pst.ap[:].bitcast(BF16))
            for i in range(nt):
                t = t0 + i
                tr(pst_bf[:, i * P:(i + 1) * P], src[:, t * P:(t + 1) * P], idn_bf[:])
            cp("dve", dst[:, t0:t0 + nt, g, :], hview(pst_bf[:, 0:nt * P], "p (t d) -> p t d", d=P))
            yield

    def gen_or_empty(fn, *a):
        r_ = fn(*a)
        return r_ if r_ is not None else iter(())

    def chain2(*gens):
        for g_ in gens:
            yield from g_

    def rr_weighted(threads):
        live = [[g_, w_] for g_, w_ in threads]
        while live:
            for ent in list(live):
                for _ in range(ent[1]):
                    try:
                        next(ent[0])
                    except StopIteration:
                        live.remove(ent)
                        break

    for _ in gen_or_empty(main_part, 0):
        pass
    P2_ORDER = os.environ.get("P2_ORDER", "mab")
    for j in range(NCH):
        parts = {"a": gen_or_empty(post_a, j), "b": gen_or_empty(post_b, j),
                 "m": gen_or_empty(main_part, j + 1) if j + 1 < NCH else iter(())}
        for key in P2_ORDER:
            if key.isdigit():
                for _ in range(int(key)):
                    next(parts["m"], None)
                continue
            for _ in parts[key]:
                pass
    dump("uT", V(uT[0], uT_raw[:, :, 0:512]), [P, 4, 512], BF16)
    dump("szT", szT[:, :, 0:512], [P, 4, 512], BF16)
    dump("qT", V(qT[0], qT_raw[:, :, 0:512]), [P, 4, 512], BF16)
    dump("kT", V(kT[0], kT_raw[:, :, 0:768]), [P, 4, 768], BF16)
    dump("ktok", ktok[:, 0:4, :, :], [P, 4, 4, P], BF16)
    dump("vtok", vtok[:, 0:4, :, :], [P, 4, 4, P], BF16)
    dump("abtok", abtok[:], [P, NT, 16])
    R1.reset()
    R6.reset(barrier=False)
    if stop_after == 2:
        return finish()

    def sc8(name):
        return R6.tile([P, NT, 8], F32, name)

    beta = sc8("beta")
    gg = sc8("g")
    Gs = sc8("Gs")
    Gam = sc8("Gam")
    epsl = sc8("epsl")
    nGb = sc8("nGb")
    xs_t = sc8("xs")
    ax_t = sc8("ax")
    egl = R6.tile([P, NT, 2, 8], F32, "egl")
    act(beta[:], abtok[:, :, 0:8], AF.Sigmoid)
    tt("dve", xs_t[:], abtok[:, :, 8:16], bc_mid(ab_c[:, 0:8], NT), ALU.add)
    stt(ax_t[:], xs_t[:], -1.0, xs_t[:], ALU.mult, ALU.max)
    act(ax_t[:], ax_t[:], AF.Exp, scale=-1.0)
    act(ax_t[:], ax_t[:], AF.Ln, bias=c_one)
    stt(xs_t[:], xs_t[:], 0.0, ax_t[:], ALU.max, ALU.add)
    tt("dve", gg[:], xs_t[:], bc_mid(ab_c[:, 8:16], NT), ALU.mult)
    psG, psGL, psE = PS[0], PS[1], PS[2]
    for t in range(NT):
        for d in range(2):
            mm(psG[:, t * 8 + d * 4:t * 8 + d * 4 + 4], gcv(G_MF if d == 0 else G_MB), gg[:, t, d * 4:d * 4 + 4])
        mm(psGL[:, t * 8:(t + 1) * 8], gcv(G_CB), gg[:, t, :])
        for c in range(2):
            mm(psE[:, (t * 2 + c) * 8:(t * 2 + c + 1) * 8], gcv(G_IND0 if c == 0 else G_IND1), gg[:, t, :])
    cp("act", Gs[:], hview(psG[:, 0:NT * 8], "p (t c) -> p t c", c=8))
    act(Gam[:], Gs[:], AF.Exp)
    tt("dve", epsl[:], hview(psGL[:, 0:NT * 8], "p (t c) -> p t c", c=8), Gs[:], ALU.subtract)
    act(epsl[:], epsl[:], AF.Exp)
    ts("dve", nGb[:], Gam[:], -1.0, ALU.mult)
    act(egl[:], hview(psE[:, 0:NT * 16], "p (t c e) -> p t c e", c=2, e=8), AF.Exp)
    dump("g", gg[:], [P, NT, 8])
    dump("beta", beta[:], [P, NT, 8])
    dump("Gs", Gs[:], [P, NT, 8])
    dump("epsl", epsl[:], [P, NT, 8])
    dump("egl", egl[:], [P, NT, 2, 8])
    if stop_after == 2.5:
        return finish()

    F32R = mybir.dt.float32r
    cr = R6.tile([P, 2, P], F32, "ones_neg1_r")
    cp("dve", V(cr, cr.ap[:, 0, :].bitcast(F32R)), ONES)
    cp("dve", V(cr, cr.ap[:, 1, :].bitcast(F32R)), NEG1)
    ONES_r = cr[:, 0, :]
    NEG1_r = cr[:, 1, :]

    def rr(view):
        return V(view.tile, view.a.bitcast(F32R))

    def f4(name, reg=R1):
        return reg.tile([P, 4, P], F32, name)

    Ttmp = [dict(tA=f4(f"tA{d}"), Qa=f4(f"Qa{d}"), QTa=f4(f"QTa{d}"), Qb=f4(f"Qb{d}"), QTb=f4(f"QTb{d}"))
            for d in range(2)]
    TTs = [[f4(f"TT{d}{i}") for i in range(2)] for d in range(2)]
    attnTs = [[R6.tile([P, 4, P], BF16, f"attnT{d}{i}") for i in range(2)] for d in range(2)]
    qdecs = [[R6.tile([P, 4, P], BF16, f"qdec{d}{i}") for i in range(2)] for d in range(2)]
    rhs2s = [f4(f"rhs2_{d}") for d in range(2)]
    OF_raw = R1.alloc([P, NLT, 4, P], F32, "OF")
    OF = [Tile(OF_raw[:, l], f"OF{l}") for l in range(NLT)]
    vns = [R0.tile([P, 4, P], BF16, f"vn{d}") for d in range(2)]
    vdcs = [[(R0 if (d, c) != (1, 1) else R6).tile([P, 4, P], BF16, f"vd{d}{c}") for c in range(2)] for d in range(2)]
    S = [f4(f"S{d}", R6) for d in range(2)]
    Sb = [R6.tile([P, 4, P], BF16, f"Sb{d}") for d in range(2)]
    for d in range(2):
        memset("pool", S[d][:], 0.0)
        memset("pool", Sb[d][:], 0.0)
        memset("pool", vns[d][:], 0.0)
        memset("pool", Ttmp[d]["tA"][:], 0.0)
        cp("dve", rr(rhs2s[d][:]), Ttmp[d]["tA"][:])
        for c in range(2):
            memset("pool", vdcs[d][c][:], 0.0)
    TPSd = [[PS[0], PS[1], PS[2]], [PS[3], PS[4], PS[5]]]
    SPS = [PS[6], PS[7]]

    def t_phase(t, d, slot):
        lat = t >= 2
        l = t - 2
        Mdir = gcv(G_MF if d == 0 else G_MB)
        NMA = gcv(G_NM_SL if d == 0 else G_NM_SU)
        PMT = gcv(G_PM_UI if d == 0 else G_PM_LI)
        gcol = gg[:, t, d * 4:d * 4 + 4]
        bcol = beta[:, t, d * 4:d * 4 + 4]
        tk = slice(t * P, (t + 1) * P)
        TT = TTs[d][slot]
        aT = attnTs[d][slot]
        qd = qdecs[d][slot]
        tm = Ttmp[d]
        tA, Qa, QTa, Qb_, QTb_ = tm["tA"], tm["Qa"], tm["QTa"], tm["Qb"], tm["QTb"]
        dA, dT_ = QTb_, Qb_
        p0, p1, p2 = TPSd[d]
        tt("dve", rr(Qa[:]), bc_mid(Mdir, 4), bc_last(gcol, P), ALU.mult)
        mm(p0[:, :], rr(NEG1_r), hview(rr(Qa[:]), "p h d -> p (h d)"), start=True, stop=False)
        for h in range(4):
            mm(p0[:, h * P:(h + 1) * P], rr(Qa[:, h, :]), rr(ONES_r), start=False, stop=(h == 3))
        for h in range(4):
            mm(p1[:, h * P:(h + 1) * P], kT[h][:, tk], kT[h][:, tk])
        yield
        stt(rr(dA[:]), ps4(p0), 0.0, bc_mid(NMA, 4), ALU.min, ALU.add)
        act(rr(dA[:]), dA[:], AF.Exp)
        if lat:
            stt(rr(dT_[:]), ps4(p0), 0.0, bc_mid(PMT, 4), ALU.max, ALU.add)
            act(rr(dT_[:]), dT_[:], AF.Exp, scale=-1.0)
        yield
        tt("dve", tA[:], ps4(p1), dA[:], ALU.mult)
        tt("dve", rr(Qa[:]), tA[:], bc_last(bcol, P), ALU.mult)
        for h in range(4):
            tr(p0[:, h * P:(h + 1) * P], Qa[:, h, :], IDN)
        if lat:
            for h in range(4):
                mm(p1[:, h * P:(h + 1) * P], kT[h][:, tk], qT[h][:, l * P:(l + 1) * P])
        yield
        cp("act", rr(QTa[:]), ps4(p0))
        stt(rr(TT[:]), QTa[:], -1.0, bc_mid(IDN, 4), ALU.mult, ALU.add)
        if lat:
            tt("dve", aT[:], ps4(p1), dT_[:], ALU.mult)
            tt("dve", rr(QTb_[:]), bc_mid(IDN, 4), bc_last(Gam[:, t, d * 4:d * 4 + 4], P), ALU.mult)
        yield
        for h in range(4):
            mm(p0[:, h * P:(h + 1) * P], rr(QTa[:, h, :]), rr(Qa[:, h, :]))
        for h in range(4):
            mm(p1[:, h * P:(h + 1) * P], rr(Qa[:, h, :]), rr(QTa[:, h, :]))
        if lat:
            mm(p2[:, :], rr(ONES_r), hview(rr(QTb_[:]), "p h d -> p (h d)"))
        yield
        cp("act", rr(Qb_[:]), ps4(p0))
        cp("act", rr(QTb_[:]), ps4(p1))
        if lat:
            tt("dve", qd[:], ps4(p2), V(qT[0], qT_raw[:, :, l * P:(l + 1) * P]), ALU.mult)
        yield
        Q, QT = Qb_, QTb_
        for k in range(1, 6):
            Qn, QTn = (Qa, QTa) if k % 2 == 1 else (Qb_, QTb_)
            if k < 5:
                for h in range(4):
                    mm(p0[:, h * P:(h + 1) * P], rr(QT[:, h, :]), rr(Q[:, h, :]))
            if k < 4:
                for h in range(4):
                    mm(p1[:, h * P:(h + 1) * P], rr(Q[:, h, :]), rr(QT[:, h, :]))
            for h in range(4):
                mm(p2[:, h * P:(h + 1) * P], rr(Q[:, h, :]), rr(TT[:, h, :]))
            yield
            if k < 5:
                cp("act", rr(Qn[:]), ps4(p0))
            if k < 4:
                cp("act", rr(QTn[:]), ps4(p1))
            tt("dve", rr(TT[:]), ps4(p2), TT[:], ALU.add)
            yield
            Q, QT = Qn, QTn
        tt("dve", rr(TT[:]), TT[:], bc_last(bcol, P), ALU.mult)
        yield

    def scan(t, d, slot):
        lat = t >= 2
        l = t - 2
        psX = SPS[d]
        psS = psX
        TT = TTs[d][slot]
        aT = attnTs[d][slot]
        qd = qdecs[d][slot]
        rhs2, vn, vdc = rhs2s[d], vns[d], vdcs[d]
        tk = slice(t * P, (t + 1) * P)
        for c in ((0, 1) if d == 0 else (1, 0)):
            r0 = c * 64
            rows = slice(r0, r0 + 64)
            for h in range(4):
                mm(psX[:, h * P:(h + 1) * P], kT[h][:, tk], Sb[d][:, h, :])
            yield
            for h in range(4):
                stt(rr(rhs2[rows, h, :]), psX[rows, h * P:(h + 1) * P], nGb[rows, t, d * 4 + h:d * 4 + h + 1],
                    vtok[rows, t, h, :], ALU.mult, ALU.add)
            yield
            for h in range(4):
                mm(psX[:, h * P:(h + 1) * P], rr(TT[:, h, :]), rr(rhs2[:, h, :]))
            yield
            psX4 = hview(psX[rows, :], "p (h d) -> p h d", d=P)
            cp("act", vn[rows], psX4)
            for h in range(4):
                act(vdc[c][rows, h, :], psX[rows, h * P:(h + 1) * P], AF.Identity,
                    scale=epsl[rows, t, d * 4 + h:d * 4 + h + 1])
            yield
            if lat:
                for h in range(4):
                    mm(psX[:, h * P:(h + 1) * P], qd[:, h, :], Sb[d][:, h, :], start=True, stop=False)
                    mm(psX[:, h * P:(h + 1) * P], aT[:, h, :], vn[:, h, :], start=False, stop=True)
                yield
                if (d == 0) != (l < NLT // 2):
                    tt("dve", OF[l][rows], psX4, OF[l][rows], ALU.add)
                else:
                    cp("act", OF[l][rows], psX4)
            for h in range(4):
                mm(psS[:, h * P:(h + 1) * P], ktok[:, t, h, :], vdc[c][:, h, :])
            yield
            for h in range(4):
                stt(S[d][:, h, :], S[d][:, h, :], egl[:, t, c, d * 4 + h:d * 4 + h + 1],
                    psS[:, h * P:(h + 1) * P], ALU.mult, ALU.add)
            cp("act", Sb[d][:], S[d][:])
            yield

    orders = [list(range(NT)), [1, 0] + list(range(NT - 1, 1, -1))]
    nsteps = NT
    if "GDN_TILES" in os.environ:
        nsteps = int(os.environ["GDN_TILES"])
        orders[1] = [1, 0] + [None] * NT

    def chain(*gens):
        for g_ in gens:
            yield from g_

    def run_weighted(threads):
        live = [[g_, w_] for g_, w_ in threads]
        while live:
            for ent in list(live):
                for _ in range(ent[1]):
                    try:
                        next(ent[0])
                    except StopIteration:
                        live.remove(ent)
                        break

    def tgen(i, d):
        if i < nsteps and orders[d][i] is not None:
            return t_phase(orders[d][i], d, i % 2)
        return iter(())

    run_weighted([(tgen(0, 0), 1), (tgen(0, 1), 1)])
    TW = int(os.environ.get("GDN_TW", "1"))
    SW = int(os.environ.get("GDN_SW", "1"))
    for i in range(nsteps):
        sc_ = [(scan(orders[d][i], d, i % 2), SW) for d in range(2) if orders[d][i] is not None]
        tg_ = [(tgen(i + 1, d), TW) for d in range(2)]
        GO = os.environ.get("GDN_ORDER", "sTsT")
        if GO == "sTsT":
            threads = [x for pair in zip(sc_ + [None] * (2 - len(sc_)), tg_) for x in pair if x is not None]
        elif GO == "TTss":
            threads = tg_ + sc_
        elif GO == "ssTT":
            threads = sc_ + tg_
        else:
            threads = [x for pair in zip(tg_, sc_ + [None] * (2 - len(sc_))) for x in pair if x is not None]
        run_weighted(threads)
        if i == 1:
            dump("S_ctx", S[0][:], [P, 4, P])
        if i == 2:
            dump("TT", TTs[0][0][:], [P, 4, P])
    dump("OF", V(OF[0], OF_raw[:, 0:4]), [P, 4, 4, P])
    if stop_after == 3:
        return finish()

    K.barrier()
    R6.reset(barrier=False)
    R5.reset(barrier=False)
    R1.cur = R1.start
    ydT_raw = qT_raw
    ydT = Tile(ydT_raw, "ydT")
    ssq4 = R6.tile([P, NLT, 4], F32, "ssq4")
    junk4 = R6.tile([P, P], BF16, "junk4")
    on_bf = [R6.tile([P, 4, P], BF16, f"on{i}") for i in range(2)]
    Vsb = R5.tile([P, NLT, 4, 256], BF16, "Vsb")
    yfT_raw = R5.alloc([P, 4, SEQ], BF16, "yfT")
    yfT = [Tile(yfT_raw[:, g, :], f"yfT{g}") for g in range(4)]
    cfsf = R6.tile([P, 256], F32, "cfsf")
    wf_sb = R6.tile([P, 4, P], F32, "wf_sb")
    CW = R6.tile([P, 4, 256], BF16, "CW")
    tabs = [R1.tile([P, 4, 2, 512], BF16, f"tab{i}") for i in range(3)]

    def yd_gen():
        for l in range(NLT):
            for h in range(4):
                act(junk4[:], OF[l][:, h, :], AF.Square, accum=ssq4[:, l, h:h + 1])
            if l % 4 == 3:
                yield
        ts("dve", ssq4[:], ssq4[:], 1.0 / P, ALU.mult, EPS, ALU.add)
        tt("pool", ssq4[:], ssq4[:], V(cst, cst.ap[:, 1:2].unsqueeze(2).broadcast_to([P, NLT, 4])), ALU.pow)
        yield
        for l in range(NLT):
            on = on_bf[l % 2]
            for h in range(4):
                ts("dve", on[:, h, :], OF[l][:, h, :], ssq4[:, l, h:h + 1], ALU.mult)
            pst = PS[l % 2]
            pst_bf = V(pst, pst.ap[:].bitcast(BF16))
            for h in range(4):
                tr(pst_bf[:, h * P:(h + 1) * P], on[:, h, :], idn_bf[:])
            yield
            stt(ydT[:, :, l * P:(l + 1) * P], hview(pst_bf[:, 0:4 * P], "p (h d) -> p h d", d=P), gg_fm[:, 0:1],
                szT[:, :, l * P:(l + 1) * P], ALU.mult, ALU.mult)
            yield

    def load_tab(idx):
        mb, l4 = divmod(idx, 4)
        tb = tabs[idx % 3]
        K.dma(tb.ap[:], tab_d[mb, l4 * 4:(l4 + 1) * 4].rearrange("t p c m -> p t c m"), W=[tb[:]])

    def fourier_gen():
        K.dma(cfsf.ap[:], cfsf_d, W=[cfsf[:]])
        K.dma(wf_sb.ap[:], wf_d.rearrange("g c d -> c g d"), W=[wf_sb[:]])
        load_tab(0)
        load_tab(1)
        for g in range(4):
            psw = PS[2 + g % 2]
            mm(psw[:, 0:P], cfsf[:, 0:P], wf_sb[:, g, :])
            mm(psw[:, P:2 * P], cfsf[:, P:2 * P], wf_sb[:, g, :])
            cp("dve", CW[:, g, :], psw[:, 0:256])
        yield
        for lt in range(NLT):
            for gp in range(2):
                psv = PS[2 + (lt * 2 + gp) % 2]
                for gi in range(2):
                    g = gp * 2 + gi
                    mm(psv[:, gi * 256:(gi + 1) * 256], uT[g][:, lt * P:(lt + 1) * P], CW[:, g, :])
                cp("act" if gp == 0 else "dve", Vsb[:, lt, gp * 2:gp * 2 + 2, :],
                   hview(psv[:, :], "p (g c) -> p g c", c=256))
            if lt % 2 == 1:
                yield
        for mb in range(4):
            pss = [PS[4 + g] for g in range(4)]
            for l4 in range(4):
                idx = mb * 4 + l4
                if idx + 2 < 16:
                    load_tab(idx + 2)
                tb = tabs[idx % 3]
                for li in range(4):
                    lt = l4 * 4 + li
                    for g in range(4):
                        mm(pss[g][:, :], Vsb[:, lt, g, 0:P], tb[:, li, 0, :], start=(lt == 0), stop=False)
                        mm(pss[g][:, :], Vsb[:, lt, g, P:2 * P], tb[:, li, 1, :], start=False,
                           stop=(lt == NLT - 1))
                    yield
            for g in range(4):
                cp("act" if g % 2 == 0 else "dve", yfT[g][:, mb * 512:(mb + 1) * 512], pss[g][:, :])
            yield

    if stop_after == 4:
        for _ in yd_gen():
            pass
        dump("ydT", ydT[:, :, 0:512], [P, 4, 512], BF16)
        return finish()
    run_weighted([(yd_gen(), 1), (fourier_gen(), 2)])
    dump("ydT", ydT[:, :, 0:512], [P, 4, 512], BF16)
    dump("yfT", V(yfT[0], yfT_raw[:, :, 0:512]), [P, 4, 512], BF16)
    R1.reset()
    R6.reset(barrier=False)
    if stop_after == 5:
        return finish()

    R3.reset(barrier=False)
    R4.reset(barrier=False)
    X1_raw = R1.alloc([P, NLT, D_MODEL], F32, "X1")
    X1 = [Tile(X1_raw[:, l, :], f"X1_{l}") for l in range(NLT)]
    wo_raw = R3.alloc([P, 8, D_MODEL], BF16, "wo_bf")
    wo_bf = [Tile(wo_raw[:, kc, :], f"wo{kc}") for kc in range(8)]
    GT1 = R4.tile([P, D_MODEL], F32, "GT1")
    xr = [R4.tile([P, D_MODEL], F32, f"xr{i}") for i in range(2)]
    tmp5 = R4.tile([P, D_MODEL], F32, "tmp5")
    junk5 = R6.tile([P, D_MODEL], BF16, "junk5")
    dgs5 = [R6.tile([P, P], F32, f"dgs5_{i}") for i in range(2)]
    ssq5 = R6.tile([P, NLT], F32, "ssq5")
    wout_v = wout_d.rearrange("(kc p) n -> p kc n", p=P)
    K.dma(wo_raw[:], wout_v, W=[wo_bf[kc][:] for kc in range(8)], queue="pool")
    make_GT(0, GT1, dgs5, PS[4], PS[5])
    mixT = [yfT[g] for g in range(4)] + [V(ydT, ydT_raw[:, h, :]) for h in range(4)]

    ssq5b = R6.tile([P, NLT], F32, "ssq5b")
    for l in range(NLT):
        K.dma(xr[l % 2].ap[:], x_d[l * P:(l + 1) * P, :], W=[xr[l % 2][:]])
        psA, psB = PS[(l % 4) * 2], PS[(l % 4) * 2 + 1]
        for half, psx in enumerate((psA, psB)):
            for kc in range(8):
                mm(psx[:, :], mixT[kc][:, l * P:(l + 1) * P], wo_bf[kc][:, half * 512:(half + 1) * 512],
                   start=(kc == 0), stop=(kc == 7))
        act(junk5[:, 0:512], psA[:, :], AF.Square, accum=ssq5[:, l:l + 1])
        act(junk5[:, 512:1024], psB[:, :], AF.Square, accum=ssq5b[:, l:l + 1])
        tt("dve", ssq5[:, l:l + 1], ssq5[:, l:l + 1], ssq5b[:, l:l + 1], ALU.add)
        ts("dve", ssq5[:, l:l + 1], ssq5[:, l:l + 1], 1.0 / D_MODEL, ALU.mult, EPS, ALU.add)
        tt("pool", ssq5[:, l:l + 1], ssq5[:, l:l + 1], c_mhalf, ALU.pow)
        stt(tmp5[:, 0:512], psA[:, :], ssq5[:, l:l + 1], GT1[:, 0:512], ALU.mult, ALU.mult)
        stt(tmp5[:, 512:1024], psB[:, :], ssq5[:, l:l + 1], GT1[:, 512:1024], ALU.mult, ALU.mult)
        tt("pool", X1[l][:], tmp5[:], xr[l % 2][:], ALU.add)
    dump("X1", V(X1[0], X1_raw[:, 0:4, :]), [P, 4, D_MODEL])
    R2.reset()
    R3.reset(barrier=False)
    R4.reset(barrier=False)
    R5.reset(barrier=False)
    R6.reset(barrier=False)
    if stop_after == 6:
        return finish()

    FF = Arena(nc, K, base + 80 * KB, avail - 80 * KB, "FF")
    actT_raw = FF.alloc([P, NFC, 1024], BF16, "actT")
    actT = [Tile(actT_raw[:, c, :], f"actT{c}") for c in range(NFC)]
    GT2 = FF.tile([P, D_MODEL], F32, "GT2")
    ssq6 = FF.tile([P, 32], F32, "ssq6")
    ssq6b = FF.tile([P, 32], F32, "ssq6b")
    dgs6 = [FF.tile([P, P], F32, f"dgs6_{i}") for i in range(2)]
    make_GT(1, GT2, dgs6, PS[4], PS[5])
    NWA = 11
    wdA_raw = FF.alloc([P, NWA, D_MODEL], BF16, "wdA")
    wdA = [Tile(wdA_raw[:, kc, :], f"wdA{kc}") for kc in range(NWA)]
    ff_mark = FF.cur
    NLOC = 9 * P
    for hf in range(2):
        FF.cur = ff_mark
        h2T = FF.tile([P, 8, NLOC], BF16, "h2T")
        wbf6 = [FF.tile([P, 8, 2, P], BF16, f"wbf6_{i}") for i in range(3)]
        PR, PC = 18, 66
        gpad = [FF.tile([P, PR, PC], BF16, f"gpad{i}") for i in range(2)]
        dg9 = [FF.tile([P, 9, P], BF16, f"dg9_{i}") for i in range(2)]
        sg = [FF.tile([P, 512], F32, f"sg{i}") for i in range(2)]
        xn6 = [FF.tile([P, D_MODEL], BF16, f"xn6_{i}") for i in range(2)]
        junk6 = FF.tile([P, D_MODEL], BF16, "junk6")
        for gp_ in gpad:
            memset("pool", gp_[:], 0.0)
        tile0 = 0 if hf == 0 else 7
        def f_stageA(i):
            l = tile0 + i
            col = hf * 16 + i
            norm_tile(X1[l][:], xn6[i % 2], ssq6[:, col:col + 1], ssq6b[:, col:col + 1], junk6)

        def f_stageB(i):
            transpose_mod(xn6[i % 2], PS[i % 2],
                          lambda kc: h2T[:, kc, i * P:(i + 1) * P],
                          lambda kc: sA2[:, kc:kc + 1],
                          lambda kc: modB[:, 8 + kc, 0:1], use_act=(i % 2 == 0))

        f_stageA(0)
        for i in range(9):
            if i + 1 < 9:
                f_stageA(i + 1)
            f_stageB(i)
        main0 = 0 if hf == 0 else 128
        halo0 = 1024 if hf == 0 else 64
        halo_row = 17 if hf == 0 else 0

        def load_wup(cc):
            w = wbf6[cc % 3]
            K.dma(w.ap[:], wup_d[cc], W=[w[:]], queue="pool")

        load_wup(0)
        load_wup(1)
        if hf == 0:
            K.dma(wdA_raw[:], wdown_d[0:NWA * P, :].rearrange("(kc p) n -> p kc n", p=P),
                  W=[wdA[kc][:] for kc in range(NWA)], queue="pool")
        for cc in range(NFC):
            if cc + 2 < NFC:
                load_wup(cc + 2)
            wb = wbf6[cc % 3]
            dg = dg9[cc % 2]
            gp_ = gpad[cc % 2]
            for tap in range(9):
                ts("dve", dg[:, tap, :], IDN, dw_fm[:, tap * NFC + cc:tap * NFC + cc + 1], ALU.mult)
            psgs = [PS[2], PS[3]]
            for piece in range(2):
                psg = psgs[piece]
                t0 = main0 + piece * 512
                for kc in range(8):
                    mm(psg[:, :], wb[:, kc, 1, :], h2T[:, kc, t0:t0 + 512], start=(kc == 0), stop=(kc == 7))
                cp("act", gp_[:, 1 + piece * 8:9 + piece * 8, 1:65], hview(psg[:, :], "p (r c) -> p r c", c=64))
            psh = PS[4]
            for kc in range(8):
                mm(psh[:, 0:64], wb[:, kc, 1, :], h2T[:, kc, halo0:halo0 + 64], start=(kc == 0), stop=(kc == 7))
            cp("act", gp_[:, halo_row, 1:65], psh[:, 0:64])
            psvs = [PS[0], PS[1]]
            for piece in range(2):
                t0 = main0 + piece * 512
                for kc in range(8):
                    mm(psvs[piece][:, :], wb[:, kc, 0, :], h2T[:, kc, t0:t0 + 512], start=(kc == 0), stop=(kc == 7))
            for piece in range(2):
                psc = PS[5 + piece]
                R0_ = 1 + piece * 8
                for tap in range(9):
                    kh, kw = divmod(tap, 3)
                    mm(hview(psc[:, :], "p (r c) -> p r c", c=64), dg[:, tap, :],
                       gp_[:, R0_ + kh - 1:R0_ + kh - 1 + 8, kw:kw + 64], start=(tap == 0), stop=(tap == 8))
                act(sg[piece][:], psc[:, :], AF.Silu)
                tt("dve", actT[cc][:, piece * 512:(piece + 1) * 512], psvs[piece][:, :], sg[piece][:], ALU.mult)
        if hf == 0:
            dump("actT", V(actT[0], actT_raw[:, 0:4, 0:512]), [P, 4, 512], BF16)
        K.barrier()
        FF.cur = ff_mark
        wdB_raw = FF.alloc([P, NFC - NWA, D_MODEL], BF16, "wdB")
        wdB = [Tile(wdB_raw[:, kc, :], f"wdB{kc}") for kc in range(NFC - NWA)]
        wd_bf = wdA + wdB
        ot = [FF.tile([P, D_MODEL], F32, f"ot{i}") for i in range(2)]
        tmp6 = FF.tile([P, D_MODEL], F32, "tmp6")
        junk7 = FF.tile([P, D_MODEL], BF16, "junk7")
        sq7 = FF.tile([P, 16], F32, "sq7")
        sq7b = FF.tile([P, 16], F32, "sq7b")
        for k0, k1 in ((0, 4), (4, NFC - NWA)):
            K.dma(wdB_raw[:, k0:k1, :],
                  wdown_d[(NWA + k0) * P:(NWA + k1) * P, :].rearrange("(kc p) n -> p kc n", p=P),
                  W=[wdB[kc][:] for kc in range(k0, k1)], queue="pool")
        for grp in range(2):
            for kc in range(NFC):
                for j in range(4):
                    i = grp * 4 + j
                    for half in range(2):
                        mm(PS[2 * j + half][:, :], actT[kc][:, i * P:(i + 1) * P],
                           wd_bf[kc][:, half * 512:(half + 1) * 512], start=(kc == 0), stop=(kc == NFC - 1))
            for j in range(4):
                i = grp * 4 + j
                l = hf * 8 + i
                psA, psB = PS[2 * j], PS[2 * j + 1]
                c1, c2 = sq7[:, i:i + 1], sq7b[:, i:i + 1]
                act(junk7[:, 0:512], psA[:, :], AF.Square, accum=c1)
                act(junk7[:, 512:1024], psB[:, :], AF.Square, accum=c2)
                tt("dve", c1, c1, c2, ALU.add)
                ts("dve", c1, c1, 1.0 / D_MODEL, ALU.mult, EPS, ALU.add)
                tt("pool", c1, c1, c_mhalf, ALU.pow)
                stt(tmp6[:, 0:512], psA[:, :], c1, GT2[:, 0:512], ALU.mult, ALU.mult)
                stt(tmp6[:, 512:1024], psB[:, :], c1, GT2[:, 512:1024], ALU.mult, ALU.mult)
                o_t = ot[i % 2]
                tt("pool", o_t[:], tmp6[:], X1[l][:], ALU.add)
                K.dma(out_d[l * P:(l + 1) * P, :], o_t.ap[:], R=[o_t[:]], is_output=True)
        if hf == 0:
            K.barrier()
    return finish()


G_IDN, G_ONES, G_NEG1, G_MF, G_MB, G_CB, G_IND0, G_IND1, G_NM_SL, G_NM_SU, G_PM_LI, G_PM_UI = range(12)
NG = 12
BIG = 30000.0


def _gconst():
    t = np.arange(P)
    a = t[:, None]
    b = t[None, :]
    same = (a // 64) == (b // 64)
    g = np.zeros((NG, P, P), np.float32)
    g[G_IDN] = np.eye(P)
    g[G_ONES] = 1.0
    g[G_NEG1] = -1.0
    g[G_MF] = (same & (a <= b))
    g[G_MB] = (same & (a >= b))
    g[G_CB] = same
    g[G_IND0] = (a < 64) * np.ones((1, P))
    g[G_IND1] = (a >= 64) * np.ones((1, P))
    g[G_NM_SL] = np.where(same & (a > b), 0.0, -BIG)
    g[G_NM_SU] = np.where(same & (a < b), 0.0, -BIG)
    g[G_PM_LI] = np.where(same & (a >= b), 0.0, BIG)
    g[G_PM_UI] = np.where(same & (a <= b), 0.0, BIG)
    return np.ascontiguousarray(g.transpose(1, 0, 2))


def _dft_consts():
    L, Fd = SEQ, 128
    sc = 1.0 / np.sqrt(L * Fd)
    n = np.arange(Fd)
    ang = 2.0 * np.pi * ((n[:, None] * n[None, :]) % Fd) / Fd
    cfsf = np.concatenate([np.cos(ang) * sc, -np.sin(ang) * sc], axis=1).astype(np.float32)
    l = np.arange(L, dtype=np.int64)
    prod = (l[:, None] * l[None, :]) % L
    angL = 2.0 * np.pi * prod / L
    c = np.cos(angL).astype(np.float32)
    s = np.sin(angL).astype(np.float32)
    tab = np.stack([c, s], axis=0)
    tab = tab.reshape(2, 16, 128, 4, 512)
    tab = tab.transpose(3, 1, 2, 0, 4)
    return cfsf, np.ascontiguousarray(tab).astype(ml_dtypes.bfloat16)


_CACHE = {}


def _host_consts():
    if "c" not in _CACHE:
        cfsf, tab = _dft_consts()
        _CACHE["c"] = (_gconst(), cfsf, tab)
    return _CACHE["c"]


def _relayout_win(w):
    wp = np.zeros((D_MODEL, 21 * P), np.float32)
    wp[:, :IN_W] = w
    return np.ascontiguousarray(wp.reshape(8, P, 21, P).transpose(2, 1, 0, 3))


def _relayout_wup(w):
    return np.ascontiguousarray(w.reshape(8, P, 2, NFC, P).transpose(3, 1, 0, 2, 4))


def make_in_maps(inputs, n_cores=8):
    gconst, cfsf, tab = _host_consts()
    f = lambda a: np.ascontiguousarray(np.asarray(a, dtype=np.float32))
    x = f(inputs["x"]); c = f(inputs["c"]); ctx = f(inputs["ctx"]); c_ctx = f(inputs["c_ctx"])
    shared = {
        "w_ada": f(inputs["w_ada"][0]), "w_in": _relayout_win(f(inputs["w_in"][0])), "w_out": f(inputs["w_out"][0]),
        "w_up": _relayout_wup(f(inputs["w_up"][0])), "w_down": f(inputs["w_down"][0]),
        "convw": f(inputs["w_qkv_conv"][0]).reshape(36, P),
        "dwcw": f(inputs["w_dwc"][0]).reshape(198, P),
        "g_gdn": f(inputs["g_gdn"][0]).reshape(1, P),
        "a_log": f(inputs["a_log"][0]).reshape(8), "dt_bias": f(inputs["dt_bias"][0]).reshape(8),
        "w_fourier": f(inputs["w_fourier"][0]),
        "gconst": gconst, "cfsf": cfsf, "dft_tab": tab,
    }
    maps = []
    for b in range(n_cores):
        vecs = np.concatenate([
            f(inputs["b_ada"][0]).reshape(48, P), f(inputs["g_pre_mix"][0]).reshape(8, P),
            f(inputs["g_post_mix"][0]).reshape(8, P), f(inputs["g_pre_ffn"][0]).reshape(8, P),
            f(inputs["g_post_ffn"][0]).reshape(8, P), c[b].reshape(8, P), c_ctx.reshape(8, P)], axis=0)
        m = dict(shared)
        m["x"] = x[b]
        m["ctx"] = ctx[b]
        m["vecs"] = np.ascontiguousarray(vecs)
        maps.append(m)
    return maps


def kernel(**inputs):
    nc, _ = build_program()
    maps = make_in_maps(inputs, 8)
    res = run_bass_kernel_spmd(nc, maps, core_ids=list(range(8)))
    return np.stack([np.asarray(r["out"], dtype=np.float32) for r in res.results], axis=0)
```
